# Optimizing a Trainium2 kernel written in Bass

```python
import math
import jax, jax.numpy as jnp
from jax import lax
import numpy as np

D_MODEL = 1024
BATCH = 8
SEQ = 4096
DEPTH = 2

CTX_LEN = 256
GRID_W = 64
D_MIX = D_MODEL
D_A = 512
D_B = D_MIX - D_A
A_HEADS = 4
A_HEAD_DIM = D_A // A_HEADS
CHUNK = 64
SHORT_K = 3
HY_EMB = 33
HY_BANDS = (HY_EMB - 1) // 2
HY_WIDTH = 64
HY_MIN_DECAY = math.log(1e-2) / 1.5
HY_MAX_DECAY = math.log(1e-2) / 0.3
N_IN = 5 * D_A + 4 * D_B
EPS = 1e-6

kernel_name = "hymba_hgrn2_hyena_prefix_dit"

F32 = jnp.float32


def rmsnorm(x, w):
    xf = x.astype(F32)
    y = xf * lax.rsqrt(jnp.mean(xf * xf, axis=-1, keepdims=True) + EPS)
    return y.astype(x.dtype) * w


def _chunk_scan(q, k, v, g, S0):
    Bn, H, L, DK = q.shape
    DV = v.shape[-1]
    N = L // CHUNK

    def to_chunks(a):
        return jnp.moveaxis(a.reshape(Bn, H, N, CHUNK, a.shape[-1]), 2, 0)

    mask = jnp.tril(jnp.ones((CHUNK, CHUNK), dtype=bool))[:, :, None]

    def step(S, inp):
        qc, kc, vc, gc = inp
        bcum = jnp.cumsum(gc, axis=2)
        o_inter = jnp.einsum('bhck,bhkv->bhcv', qc * jnp.exp(bcum), S)
        diff = bcum[:, :, :, None, :] - bcum[:, :, None, :, :]
        decay = jnp.exp(jnp.where(mask, diff, -jnp.inf))
        scores = jnp.einsum('bhik,bhijk,bhjk->bhij', qc, decay, kc)
        o = o_inter + jnp.einsum('bhij,bhjv->bhiv', scores, vc)
        b_last = bcum[:, :, -1:, :]
        S_new = jnp.exp(b_last[:, :, 0, :])[..., None] * S + jnp.einsum(
            'bhck,bhcv->bhkv', kc * jnp.exp(b_last - bcum), vc)
        return S_new, o

    S_fin, o = lax.scan(step, S0, (to_chunks(q), to_chunks(k), to_chunks(v), to_chunks(g)))
    o = jnp.moveaxis(o, 0, 2).reshape(Bn, H, L, DV)
    return o, S_fin


def hgrn2_mixer(q_raw, ff_raw, fb_raw, i_raw, lb, S0_f, S0_b):
    Bn, L, _ = q_raw.shape

    def heads(a):
        return a.astype(F32).reshape(Bn, L, A_HEADS, A_HEAD_DIM).transpose(0, 2, 1, 3)

    q = jax.nn.silu(heads(q_raw))
    v = heads(i_raw)

    def gates(f_raw, lb_dir):
        f = heads(f_raw)
        lbh = lb_dir.reshape(A_HEADS, 1, A_HEAD_DIM)
        g = jnp.logaddexp(jnp.log(lbh), jnp.log1p(-lbh) + jax.nn.log_sigmoid(f))
        k = (1.0 - lbh) * jax.nn.sigmoid(-f)
        return k, g

    k_f, g_f = gates(ff_raw, lb[0])
    k_b, g_b = gates(fb_raw, lb[1])
    o_f, S_f = _chunk_scan(q, k_f, v, g_f, S0_f)

    def flip(a):
        return jnp.flip(a, axis=2)

    o_b, S_b = _chunk_scan(flip(q), flip(k_b), flip(v), flip(g_b), S0_b)
    return o_f + flip(o_b), S_f, S_b


def head_rmsnorm(o, w):
    Bn, H, L, DV = o.shape
    o = o * lax.rsqrt(jnp.mean(o * o, axis=-1, keepdims=True) + EPS)
    return o.transpose(0, 2, 1, 3).reshape(Bn, L, H * DV) * w.astype(F32)


def pos_features(L):
    t = jnp.linspace(0.0, 1.0, L, dtype=F32)[:, None]
    w = 2.0 * math.pi * jnp.arange(L, dtype=F32)[:, None] / L
    f = jnp.linspace(1e-4, HY_BANDS - 1, HY_BANDS, dtype=F32)[None, :]
    return jnp.concatenate([t, jnp.cos(f * w), -jnp.sin(f * w)], axis=-1)


def hyena_filter(L, w1, b1, freq, w2, b2, w3, b3, w4):
    z = pos_features(L)
    fr = freq.astype(F32)
    h = jnp.sin(fr * (z @ w1.astype(F32) + b1.astype(F32)))
    h = jnp.sin(fr * (h @ w2.astype(F32) + b2.astype(F32)))
    h = jnp.sin(fr * (h @ w3.astype(F32) + b3.astype(F32)))
    h = h @ w4.astype(F32)
    deltas = jnp.abs(jnp.linspace(HY_MIN_DECAY, HY_MAX_DECAY, D_B, dtype=F32))
    decay = jnp.exp(-z[:, :1] * deltas)
    h_f = h[:, :D_B] * decay
    h_b = h[:, D_B:] * decay
    k = jnp.concatenate([h_f, jnp.zeros((1, D_B), F32), jnp.flip(h_b[1:], axis=0)], axis=0)
    return k / jnp.sum(jnp.abs(k), axis=0, keepdims=True)


def long_conv(v, k):
    L = v.shape[1]
    vf = jnp.fft.rfft(v.astype(F32), n=2 * L, axis=1)
    kf = jnp.fft.rfft(k, n=2 * L, axis=0)
    return jnp.fft.irfft(vf * kf[None], n=2 * L, axis=1)[:, :L].astype(v.dtype)


def short_conv(u, w, b):
    L = u.shape[1]
    pad = SHORT_K // 2
    up = jnp.pad(u, ((0, 0), (pad, SHORT_K - 1 - pad), (0, 0)))
    out = b
    for j in range(SHORT_K):
        out = out + up[:, j:j + L] * w[j]
    return out


def hyena_mixer(u, conv_w, conv_b, filt, hy_bias):
    uc = short_conv(u, conv_w, conv_b)
    x0, x1, v = jnp.split(uc, 3, axis=-1)
    v = v * x1
    y = long_conv(v, filt) + v * hy_bias
    return y * x0


def mixer_outputs(p, lb, g_norm_w, conv_w, conv_b, filt, hy_bias, S0_f, S0_b):
    q, ff, fb, iv, gA = [p[..., j * D_A:(j + 1) * D_A] for j in range(5)]
    uB = p[..., 5 * D_A:5 * D_A + 3 * D_B]
    gB = p[..., 5 * D_A + 3 * D_B:]
    oA, S_f, S_b = hgrn2_mixer(q, ff, fb, iv, lb, S0_f, S0_b)
    oA = head_rmsnorm(oA, g_norm_w).astype(p.dtype) * jax.nn.silu(gA)
    oB = hyena_mixer(uB, conv_w, conv_b, filt, hy_bias) * jax.nn.silu(gB)
    return jnp.concatenate([oA, oB], axis=-1), S_f, S_b


def setup_inputs(seed: int = 0) -> dict:
    key = jax.random.key(seed)
    ks = jax.random.split(key, 24)
    nrm = jax.random.normal
    D = D_MODEL
    return {
        "x": nrm(ks[0], (BATCH, SEQ, D), F32),
        "c": nrm(ks[1], (BATCH, D), F32),
        "ctx": nrm(ks[2], (BATCH, CTX_LEN, D), F32),
        "c_ctx": nrm(ks[3], (D,), F32),
        "norm_w": 1.0 + 0.05 * nrm(ks[4], (DEPTH, D), F32),
        "w_ada": 0.5 * D ** -0.5 * nrm(ks[5], (DEPTH, D, 3 * D), F32),
        "b_ada": 0.02 * nrm(ks[6], (DEPTH, 3 * D), F32),
        "w_in": D ** -0.5 * nrm(ks[7], (DEPTH, D, N_IN), F32),
        "w_out": D_MIX ** -0.5 * nrm(ks[8], (DEPTH, D_MIX, D), F32),
        "lb_logits": 0.5 * nrm(ks[9], (DEPTH, 2, D_A), F32),
        "g_norm_w": 1.0 + 0.05 * nrm(ks[10], (DEPTH, D_A), F32),
        "conv_w": 0.5 * nrm(ks[11], (DEPTH, SHORT_K, 3 * D_B), F32),
        "conv_b": 0.02 * nrm(ks[12], (DEPTH, 3 * D_B), F32),
        "hy_w1": HY_EMB ** -0.5 * nrm(ks[13], (DEPTH, HY_EMB, HY_WIDTH), F32),
        "hy_b1": 0.1 * nrm(ks[14], (DEPTH, HY_WIDTH), F32),
        "hy_freq": 1.0 + 0.05 * nrm(ks[15], (DEPTH, HY_WIDTH), F32),
        "hy_w2": HY_WIDTH ** -0.5 * nrm(ks[16], (DEPTH, HY_WIDTH, HY_WIDTH), F32),
        "hy_b2": 0.1 * nrm(ks[17], (DEPTH, HY_WIDTH), F32),
        "hy_w3": HY_WIDTH ** -0.5 * nrm(ks[18], (DEPTH, HY_WIDTH, HY_WIDTH), F32),
        "hy_b3": 0.1 * nrm(ks[19], (DEPTH, HY_WIDTH), F32),
        "hy_w4": HY_WIDTH ** -0.5 * nrm(ks[20], (DEPTH, HY_WIDTH, 2 * D_B), F32),
        "hy_bias": nrm(ks[21], (DEPTH, D_B), F32),
        "final_norm_w": 1.0 + 0.05 * nrm(ks[22], (D,), F32),
    }


def reference(x, c, ctx, c_ctx, norm_w, w_ada, b_ada, w_in, w_out, lb_logits, g_norm_w,
              conv_w, conv_b, hy_w1, hy_b1, hy_freq, hy_w2, hy_b2, hy_w3, hy_b3, hy_w4,
              hy_bias, final_norm_w):
    Bn, L_lat, _ = x.shape
    L_ctx = ctx.shape[1]
    p_lb = jax.nn.softmax(lb_logits.astype(F32), axis=0)
    lbs = jnp.cumsum(p_lb, axis=0)
    lbs = lbs - lbs[0:1]
    zero_state = jnp.zeros((Bn, A_HEADS, A_HEAD_DIM, A_HEAD_DIM), F32)

    for l in range(DEPTH):
        last = l == DEPTH - 1
        filt_args = (hy_w1[l], hy_b1[l], hy_freq[l], hy_w2[l], hy_b2[l], hy_w3[l], hy_b3[l], hy_w4[l])
        mod_x = jax.nn.silu(c) @ w_ada[l] + b_ada[l]
        sh_x, sc_x, gt_x = jnp.split(mod_x[:, None, :], 3, axis=-1)
        mod_c = jax.nn.silu(c_ctx) @ w_ada[l] + b_ada[l]
        sh_c, sc_c, gt_c = jnp.split(mod_c, 3, axis=-1)

        hc = rmsnorm(ctx, norm_w[l]) * (1.0 + sc_c) + sh_c
        if last:
            pA = hc @ w_in[l, :, :4 * D_A]
            q, ff, fb, iv = [pA[..., j * D_A:(j + 1) * D_A] for j in range(4)]
            _, S_f, S_b = hgrn2_mixer(q, ff, fb, iv, lbs[l], zero_state, zero_state)
        else:
            pc = hc @ w_in[l]
            filt_c = hyena_filter(L_ctx, *filt_args)
            oc, S_f, S_b = mixer_outputs(pc, lbs[l], g_norm_w[l], conv_w[l], conv_b[l],
                                         filt_c, hy_bias[l], zero_state, zero_state)
            ctx = ctx + gt_c * (oc @ w_out[l])

        hx = rmsnorm(x, norm_w[l]) * (1.0 + sc_x) + sh_x
        px = hx @ w_in[l]
        filt_x = hyena_filter(L_lat, *filt_args)
        ox, _, _ = mixer_outputs(px, lbs[l], g_norm_w[l], conv_w[l], conv_b[l],
                                 filt_x, hy_bias[l], S_f, S_b)
        x = x + gt_x * (ox @ w_out[l])

    return rmsnorm(x, final_norm_w)
```

```python
import numpy as np
import ml_dtypes
import concourse.bass as bass
import concourse.mybir as mybir
from concourse.bass_utils import run_bass_kernel_spmd

F32 = mybir.dt.float32
BF16 = mybir.dt.bfloat16
AF = mybir.ActivationFunctionType
ALU = mybir.AluOpType

NDS = 24
HG_W = 5


class Tok:
    __slots__ = ("eng", "idx")

    def __init__(self, eng, idx):
        self.eng = eng
        self.idx = idx


class Reg:
    __slots__ = ("name", "w", "r", "pw", "pr", "full")

    def __init__(self, name=""):
        self.name = name
        self.w = {}
        self.r = {}
        self.pw = {}
        self.pr = {}
        self.full = {}


def _merge(d, tok):
    o = d.get(tok.eng)
    if o is None or o.idx < tok.idx:
        d[tok.eng] = tok


class _Rec:
    __slots__ = ("fn", "waits", "flag", "dsem", "dval")

    def __init__(self, fn, waits):
        self.fn = fn
        self.waits = waits
        self.flag = False
        self.dsem = None
        self.dval = 0


class _Capture:
    def __init__(self):
        self.call = None

    def __getattr__(self, name):
        def f(*args, **kw):
            self.call = (name, args, kw)
        return f


class Prog:
    ENGS = ("pe", "dve", "act", "pool", "sp")

    def __init__(self, nc):
        self.nc = nc
        self.q = {e: [] for e in self.ENGS}
        self.waited = {e: {} for e in self.ENGS}
        self.dma_n = 0
        self.dma_last = [None] * NDS
        self.dma_val = [0] * NDS
        self.n_sb = 0

    def sb(self, name, shape, dtype):
        return self.nc.alloc_sbuf_tensor(name, list(shape), dtype)

    def ps(self, name, shape, dtype=F32):
        return self.nc.alloc_psum_tensor(name, list(shape), dtype)

    def dram(self, name, shape, dtype, kind="Internal"):
        return self.nc.dram_tensor(name, list(shape), dtype, kind=kind).ap()

    def _deps(self, eng, rd, wr, wrp, extra, xb=()):
        deps = []
        for r in xb:
            deps.extend(t for e2, t in r.w.items() if e2 != eng)
        for r in rd:
            deps.extend(r.w.values())
        for r in wr:
            deps.extend(r.w.values())
            deps.extend(r.r.values())
        for r in wrp:
            if r.r:
                r.pw, r.pr = r.w, r.r
                r.w, r.r = {}, {}
                r.full = {}
            deps.extend(r.pw.values())
            deps.extend(r.pr.values())
            deps.extend(r.full.values())
        deps.extend(extra)
        return deps

    def _post(self, tok, rd, wr, wrp, xb=()):
        for r in xb:
            _merge(r.w, tok)
        for r in rd:
            _merge(r.r, tok)
        for r in wr:
            r.pw, r.pr = {}, {}
            r.w, r.r = {tok.eng: tok}, {}
            r.full = {tok.eng: tok}
        for r in wrp:
            _merge(r.w, tok)

    def _waits(self, eng, deps):
        waits = []
        wd = self.waited[eng]
        for d in deps:
            if d is None:
                continue
            if d.eng == eng and eng == "pe":
                continue
            if wd.get(d.eng, -1) >= d.idx:
                continue
            wd[d.eng] = d.idx
            if not d.eng.startswith("dma"):
                self.q[d.eng][d.idx].flag = True
            waits.append(d)
        return waits

    def op(self, eng, fn, rd=(), wr=(), wrp=(), deps=(), xb=()):
        cap = _Capture()
        fn(cap)
        name, args, kw = cap.call
        fn = lambda h, name=name, args=args, kw=kw: getattr(h, name)(*args, **kw)
        dl = self._deps(eng, rd, wr, wrp, deps, xb)
        waits = self._waits(eng, dl)
        q = self.q[eng]
        q.append(_Rec(fn, waits))
        tok = Tok(eng, len(q) - 1)
        self._post(tok, rd, wr, wrp, xb)
        return tok

    def t(self, fn, **kw):
        return self.op("pe", fn, **kw)

    def v(self, fn, **kw):
        return self.op("dve", fn, **kw)

    def a(self, fn, **kw):
        return self.op("act", fn, **kw)

    def g(self, fn, **kw):
        return self.op("pool", fn, **kw)

    def dma(self, out, in_, rd=(), wr=(), wrp=(), deps=(), q="sp"):
        k = self.dma_n % NDS
        self.dma_n += 1
        dl = self._deps(q, rd, wr, wrp, deps)
        if self.dma_last[k] is not None:
            dl.append(self.dma_last[k])
        waits = self._waits(q, dl)
        rec = _Rec(lambda e: e.dma_start(out=out, in_=in_), waits)
        self.dma_val[k] += 16
        rec.dsem = k
        rec.dval = self.dma_val[k]
        self.q[q].append(rec)
        tok = Tok("dma%d" % k, self.dma_val[k])
        self.dma_last[k] = tok
        self._post(tok, rd, wr, wrp)
        return tok

    def last_real(self, e):
        q = self.q[e]
        for i in range(len(q) - 1, -1, -1):
            if q[i].fn is not None and q[i].dsem is None:
                return Tok(e, i)
        return None

    def finish(self):
        nc = self.nc
        fin = [t for t in self.dma_last if t is not None]
        for e in ("pe", "dve", "act", "pool"):
            t = self.last_real(e)
            if t is not None:
                fin.append(t)
        waits = self._waits("sp", fin)
        self.q["sp"].append(_Rec(None, waits))
        cnt = self.cnt = {}
        for e in self.ENGS:
            c = 0
            arr = []
            for rec in self.q[e]:
                if rec.flag:
                    c += 1
                arr.append(c)
            cnt[e] = arr
        import contextlib
        with contextlib.ExitStack() as st:
            sems = {e: st.enter_context(nc.semaphore("s_" + e)) for e in self.ENGS}
            dsems = [st.enter_context(nc.semaphore("d%d" % i)) for i in range(NDS)]
            block = st.enter_context(nc.Block())

            def run(e, h):
                for rec in self.q[e]:
                    for w in rec.waits:
                        if w.eng.startswith("dma"):
                            h.wait_ge(dsems[int(w.eng[3:])], w.idx)
                        else:
                            h.wait_ge(sems[w.eng], cnt[w.eng][w.idx])
                    if rec.fn is None:
                        continue
                    ins = rec.fn(h)
                    if rec.dsem is not None:
                        ins.then_inc(dsems[rec.dsem], 16)
                    elif rec.flag:
                        ins.then_inc(sems[e], 1)

            @block.tensor
            def _(h):
                run("pe", h)

            @block.vector
            def _(h):
                run("dve", h)

            @block.scalar
            def _(h):
                run("act", h)

            @block.gpsimd
            def _(h):
                run("pool", h)

            @block.sync
            def _(h):
                run("sp", h)


D = 1024
L = 4096
LC = 256
T = LC + L
NT = T // 128
BLK = 256
NB = T // BLK
EPS = 1e-6
NFFT = 8192
HY_MIN = float(np.log(1e-2) / 1.5)
HY_MAX = float(np.log(1e-2) / 0.3)
PI = float(np.pi)
bf16 = ml_dtypes.bfloat16

_SV = {}
_o = 0
for _n, _w in [("cvT", 16), ("normw", 16), ("fnw", 8), ("bada", 48), ("lbl", 16), ("gnw", 8),
               ("convw", 72), ("convb", 24), ("hyb", 8), ("hb1", 2), ("hb2", 2), ("hb3", 2), ("hfr", 2),
               ("delt", 4)]:
    _SV[_n] = (_o, _w)
    _o += _w
NSV = _o


def _pos_feat(Ln):
    t = np.linspace(0.0, 1.0, Ln, dtype=np.float32)[:, None]
    w = (2.0 * np.pi * np.arange(Ln, dtype=np.float32)[:, None] / Ln).astype(np.float32)
    f = np.linspace(1e-4, 15.0, 16, dtype=np.float32)[None, :]
    return np.concatenate([t, np.cos(f * w), -np.sin(f * w)], axis=-1).astype(np.float32)


def _host_consts():
    c = {}
    c["identF"] = np.eye(128, dtype=np.float32)
    c["identB"] = np.eye(128).astype(bf16)
    j = np.arange(64)[:, None]
    i = np.arange(64)[None, :]
    mk = np.stack([(j <= i), (j >= i)], axis=0).astype(np.float32)
    c["masks"] = np.concatenate([mk, mk], axis=1).transpose(1, 0, 2).copy()
    for Ln, nm in ((L, "x"), (LC, "c")):
        z = _pos_feat(Ln)
        zr = np.zeros_like(z)
        zr[1:] = z[Ln - np.arange(1, Ln)]
        c["ZZ" + nm] = np.concatenate([z.T, zr.T], axis=0).copy()
        tt = np.zeros((2, Ln), np.float32)
        tt[0] = z[:, 0]
        tt[1, 1:] = z[Ln - np.arange(1, Ln), 0]
        c["TT" + nm] = tt
    n1 = np.arange(64, dtype=np.float64)[:, None]
    FA = np.zeros((64, 64))
    k1r = np.arange(33, dtype=np.float64)[None, :]
    FA[:, 0:33] = np.cos(2 * np.pi * n1 * k1r / 64)
    k1i = np.arange(1, 32, dtype=np.float64)[None, :]
    FA[:, 33:64] = -np.sin(2 * np.pi * n1 * k1i / 64)
    c["FA2"] = np.concatenate([FA, FA], axis=0).astype(bf16)
    nn1 = np.arange(32, dtype=np.float64)[None, :]
    RA = np.zeros((64, 32))
    RA[0, :] = 1.0
    RA[32, :] = (-1.0) ** np.arange(32)
    kk = np.arange(1, 32, dtype=np.float64)[:, None]
    RA[1:32, :] = 2 * np.cos(2 * np.pi * nn1 * kk / 64)
    RA[33:64, :] = -2 * np.sin(2 * np.pi * nn1 * kk / 64)
    RA /= NFFT
    c["RA2"] = np.concatenate([RA, RA], axis=0).astype(bf16)
    n2 = np.arange(128, dtype=np.float64)[:, None]
    k2 = np.arange(128, dtype=np.float64)[None, :]
    TM = np.zeros((128, 33, 6, 128), np.float64)
    for k1 in range(33):
        th = 2 * np.pi * n2 * (k1 + 64 * k2) / NFFT
        Tc, Ts = np.cos(th), -np.sin(th)
        TM[:, k1, 0], TM[:, k1, 1], TM[:, k1, 2] = Tc, Ts, -Ts
        TM[:, k1, 3], TM[:, k1, 4], TM[:, k1, 5] = Tc.T, Ts.T, -Ts.T
    c["TM"] = TM.astype(bf16)
    return c


def _pack_small(inp, b):
    sv = np.zeros((128, NSV), np.float32)

    def put(name, arr):
        o, w = _SV[name]
        a = np.asarray(arr, np.float32).reshape(128, -1)
        assert a.shape[1] == w, (name, a.shape, w)
        sv[:, o:o + w] = a

    cv = np.stack([inp["c"][b], inp["c_ctx"]], axis=0)
    put("cvT", cv.reshape(2, 8, 128).transpose(2, 1, 0))
    put("normw", inp["norm_w"].reshape(2, 8, 128).transpose(2, 0, 1))
    put("fnw", inp["final_norm_w"].reshape(8, 128).T)
    put("bada", inp["b_ada"].reshape(2, 24, 128).transpose(2, 0, 1))
    put("lbl", inp["lb_logits"].reshape(2, 2, 4, 128).transpose(3, 0, 1, 2))
    put("gnw", inp["g_norm_w"].reshape(2, 4, 128).transpose(2, 0, 1))
    put("convw", inp["conv_w"].reshape(2, 3, 12, 128).transpose(3, 0, 1, 2))
    put("convb", inp["conv_b"].reshape(2, 12, 128).transpose(2, 0, 1))
    put("hyb", inp["hy_bias"].reshape(2, 4, 128).transpose(2, 0, 1))
    for nm, key in (("hb1", "hy_b1"), ("hb2", "hy_b2"), ("hb3", "hy_b3"), ("hfr", "hy_freq")):
        a = inp[key]
        put(nm, np.concatenate([a.T, a.T], axis=0))
    delt = -np.abs(np.linspace(HY_MIN, HY_MAX, 512, dtype=np.float32))
    put("delt", delt.reshape(4, 128).T)
    return sv


class Arena:
    def __init__(self, P, words):
        self.t = P.sb("arena", [128, words], F32)
        self.words = words
        self.off = 0

    def alloc(self, shape, dtype=F32):
        n = 1
        for s in shape:
            n *= s
        w = n if dtype == F32 else (n + 1) // 2
        w = (w + 7) // 8 * 8
        assert self.off + w <= self.words, ("arena overflow", self.off, w, self.words)
        ap = self.t[:, self.off:self.off + w]
        self.off += w
        if dtype != F32:
            ap = ap.bitcast(dtype)
        ap = ap[:, 0:n]
        if len(shape) == 2:
            ap = ap.rearrange("p (a b) -> p a b", a=shape[0])
        elif len(shape) == 3:
            ap = ap.rearrange("p (a b c) -> p a b c", a=shape[0], b=shape[1])
        return ap


class Builder:
    def __init__(self, nlayers=2, dbg=()):
        self.nlayers = nlayers
        self.dbg = set(dbg)
        nc = self.nc = bass.Bass("TRN2", target_bir_lowering=False)
        P = self.P = Prog(nc)
        self.dbg_outs = {}
        inp = lambda name, shape, dt=F32: nc.dram_tensor(name, list(shape), dt, kind="ExternalInput").ap()
        self.x_in = inp("x", [L, D])
        self.ctx_in = inp("ctx", [LC, D])
        self.smallv_in = inp("smallv", [128, NSV])
        self.w_ada = inp("w_ada", [2, D, 3 * D])
        self.w_in = inp("w_in", [2, D, 4608])
        self.w_out = inp("w_out", [2, D, D])
        self.hy_w1 = inp("hy_w1", [2, 33, 64])
        self.hy_w2 = inp("hy_w2", [2, 64, 64])
        self.hy_w3 = inp("hy_w3", [2, 64, 64])
        self.hy_w4 = inp("hy_w4", [2, 64, 1024])
        self.c_identF = inp("identF", [128, 128])
        self.c_identB = inp("identB", [128, 128], BF16)
        self.c_masks = inp("masks", [128, 2, 64])
        self.c_ZZ = {"x": inp("ZZx", [66, L]), "c": inp("ZZc", [66, LC])}
        self.c_TT = {"x": inp("TTx", [2, L]), "c": inp("TTc", [2, LC])}
        self.c_FA2 = inp("FA2", [128, 64], BF16)
        self.c_RA2 = inp("RA2", [128, 32], BF16)
        self.c_TM = inp("TM", [128, 33, 6, 128], BF16)
        self.out = nc.dram_tensor("out", [L, D], F32, kind="ExternalOutput").ap()
        self.XRES = P.dram("xres", [T, D], F32)
        self.MIX = P.dram("mix", [8, 128, T], BF16)
        self.VD = P.dram("vd", [512, L], BF16)
        self.VDC = P.dram("vdc", [512, LC], BF16)
        self.KD = P.dram("kd", [512, NFFT], BF16)
        self.KF = {}
        for l in range(nlayers):
            segs = ("c", "x") if l < nlayers - 1 or nlayers == 1 else ("x",)
            for s in segs:
                self.KF[(l, s)] = P.dram("kf%d%s" % (l, s), [33, 2, 128, 512], F32)
        self.R_XRES = [Reg() for _ in range(NT)]
        self.R_MIX = [[Reg() for _ in range(NB)] for _ in range(8)]
        self.R_VD = [Reg() for _ in range(4)]
        self.R_KD = Reg()
        self.R_KD4 = [Reg() for _ in range(4)]
        self.R_KF = {k: [Reg() for _ in range(4)] for k in self.KF}
        self.pb = [P.ps("pb%d" % i, [128, 512], F32) for i in range(8)]
        self.R_pb = [Reg() for _ in range(8)]
        self._tt_cnt = 0
        self.ar = Arena(P, 212000 // 4)
        A = self.ar
        self.sv = A.alloc([NSV]); self.R_sv = Reg()
        self.identF = A.alloc([128]); self.identB = A.alloc([128], BF16)
        self.masks = A.alloc([2, 64]); self.onesF = A.alloc([128]); self.onesB = A.alloc([128], BF16)
        self.FA2 = A.alloc([64], BF16); self.RA2 = A.alloc([32], BF16)
        self.R_const = Reg()
        self.mod = A.alloc([2, 24, 2]); self.R_mod = Reg()
        self.weff = A.alloc([2, 8, 2]); self.lbv = A.alloc([2, 8]); self.omlb = A.alloc([2, 8])
        self.kwin = A.alloc([4, 2 * LC - 1]); self.R_kwin = [Reg() for _ in range(4)]
        self.hx_off = A.off
        self.hxT = A.alloc([8, T], BF16)
        self.R_hx = [Reg() for _ in range(NT)]
        self.persist_mark = A.off
        P.dma(self.sv, self.smallv_in, wr=[self.R_sv])
        P.dma(self.identF, self.c_identF, wrp=[self.R_const])
        P.dma(self.identB, self.c_identB, wrp=[self.R_const])
        P.dma(self.masks, self.c_masks, wrp=[self.R_const])
        P.dma(self.FA2, self.c_FA2, wrp=[self.R_const])
        P.dma(self.RA2, self.c_RA2, wrp=[self.R_const])
        P.g(lambda e: e.memset(self.onesF, 1.0), wrp=[self.R_const])
        P.g(lambda e: e.memset(self.onesB, 1.0), wrp=[self.R_const])

    def svv(self, name, *idx_shape):
        o, w = _SV[name]
        return self.sv[:, o:o + w]

    def barrier(self, name=None):
        P = self.P
        if not hasattr(self, "marks"):
            self.marks = []
        self.marks.append((name or "b%d" % len(self.marks), {e: len(P.q[e]) - 1 for e in ("pe", "dve", "act", "pool")}))
        toks = [t for t in (P.last_real(e) for e in ("pe", "dve", "act", "pool")) if t is not None]
        toks += [t for t in P.dma_last if t is not None]
        for e in ("pe", "dve", "act", "pool", "sp"):
            w = P._waits(e, [t for t in toks if t.eng != e])
            if w:
                P.q[e].append(_Rec(None, w))

    def dump(self, name, ap, rd=(), shape=None, dt=F32):
        if name not in self.dbg:
            return
        if len(ap.shape) == 3:
            ap = ap.rearrange("p a b -> p (a b)")
        elif len(ap.shape) == 4:
            ap = ap.rearrange("p a b c -> p (a b c)")
        shp = list(ap.shape)
        o = self.nc.dram_tensor("dbg_" + name, shp, dt, kind="ExternalOutput").ap()
        self.dbg_outs[name] = "dbg_" + name
        self.P.dma(o, ap, rd=list(rd))

    def adaln(self, stage=9):
        P, A = self.P, self.ar
        mark = A.off
        scv = A.alloc([8, 2]); R_scv = Reg()
        o, _ = _SV["cvT"]
        cv = self.sv[:, o:o + 16].rearrange("p (k r) -> p k r", k=8)
        P.a(lambda e: e.activation(scv, cv, AF.Silu), rd=[self.R_sv], wr=[R_scv])
        wst = [A.alloc([8, 512]) for _ in range(2)]
        R_w = [Reg(), Reg()]
        ob, _ = _SV["bada"]
        bada = self.sv[:, ob:ob + 48].rearrange("p (l j) -> p l j", l=2)
        R_ps = Reg()
        n = 0
        for l in range(self.nlayers):
            wv = self.w_ada[l].rearrange("(k p) n -> p k n", p=128)
            for pc in range(6):
                wb, rw = wst[n % 2], R_w[n % 2]
                n += 1
                P.dma(wb, wv[:, :, pc * 512:(pc + 1) * 512], wr=[rw])
                for jj in range(4):
                    j = pc * 4 + jj
                    for k in range(8):
                        P.t(lambda e, wb=wb, jj=jj, k=k, j=j: e.matmul(self.pb[0][:, 2 * j:2 * j + 2], wb[:, k, jj * 128:(jj + 1) * 128],
                                                                      scv[:, k, :], start=(k == 0), stop=(k == 7)),
                            rd=[rw, R_scv], xb=[self.R_pb[0]])
            if stage < 3:
                continue
            P.v(lambda e, l=l: e.tensor_tensor(self.mod[:, l], self.pb[0][:, 0:48].rearrange("p (j r) -> p j r", j=24),
                                               bada[:, l, :].unsqueeze(2).to_broadcast([128, 24, 2]), ALU.add),
                rd=[self.R_sv], wrp=[self.R_mod], xb=[self.R_pb[0]])
            on, _ = _SV["normw"]
            nw = self.sv[:, on:on + 16].rearrange("p (l k) -> p l k", l=2)
            P.v(lambda e, l=l: e.scalar_tensor_tensor(self.weff[:, l], self.mod[:, l, 8:16, :], 1.0,
                                                      nw[:, l, :].unsqueeze(2).to_broadcast([128, 8, 2]), ALU.add, ALU.mult),
                rd=[self.R_mod, self.R_sv], wrp=[self.R_mod])
        if stage < 4:
            self.barrier(); A.off = mark; return
        ol, _ = _SV["lbl"]
        lbl = self.sv[:, ol:ol + 16].rearrange("p (l x) -> p l x", l=2)
        P.g(lambda e: e.memset(self.lbv[:, 0, :], 0.0), wrp=[self.R_mod])
        P.g(lambda e: e.memset(self.omlb[:, 0, :], 1.0), wrp=[self.R_mod])
        if self.nlayers > 1:
            dl = A.alloc([8]); R_dl = Reg()
            P.v(lambda e: e.tensor_tensor(dl, lbl[:, 1, :], lbl[:, 0, :], ALU.subtract), rd=[self.R_sv], wr=[R_dl])
            P.a(lambda e: e.activation(self.lbv[:, 1, :], dl, AF.Sigmoid), rd=[R_dl], wrp=[self.R_mod])
            P.a(lambda e: e.activation(self.omlb[:, 1, :], dl, AF.Sigmoid, scale=-1.0), rd=[R_dl], wrp=[self.R_mod])
        self.dump("mod", self.mod, rd=[self.R_mod])
        self.barrier()
        A.off = mark

    def tile_to_hx(self, l, tt, xt, R_xt, bufs):
        P = self.P
        seg = 1 if tt < 2 else 0
        junk, ssq, rstd, xn, R_t = bufs
        P.a(lambda e: e.activation(junk, xt, AF.Square, accum_out=ssq), rd=[R_xt], wr=[R_t])
        P.a(lambda e: e.activation(ssq, ssq, AF.Sqrt, scale=1.0 / D, bias=EPS), wr=[R_t])
        P.v(lambda e: e.reciprocal(rstd, ssq), wr=[R_t])
        xnb = junk
        P.v(lambda e: e.tensor_scalar(xnb, xt, rstd, None, ALU.mult), rd=[R_xt], wr=[R_t])
        for half in range(2):
            bi_ = 1 + half
            bank, R_b = self.pb[bi_], self.R_pb[bi_]
            for kk in range(4):
                k = half * 4 + kk
                P.t(lambda e, k=k, kk=kk, bank=bank: e.transpose(bank[:, 0:256].bitcast(BF16)[:, kk * 128:(kk + 1) * 128], xnb[:, k * 128:(k + 1) * 128], self.identB),
                    rd=[R_t, self.R_const], xb=[R_b])
            for kk in range(4):
                k = half * 4 + kk
                dst = self.hxT[:, k, tt * 128:(tt + 1) * 128]
                src = bank[:, 0:256].bitcast(BF16)[:, kk * 128:(kk + 1) * 128]
                sc = self.weff[:, l, k, seg:seg + 1]
                bi = self.mod[:, l, k, seg:seg + 1]
                if half == 0:
                    P.a(lambda e, dst=dst, src=src, sc=sc, bi=bi: e.activation(dst, src, AF.Identity, bias=bi, scale=sc),
                        rd=[self.R_mod], wrp=[self.R_hx[tt]], xb=[R_b])
                else:
                    P.v(lambda e, dst=dst, src=src, sc=sc, bi=bi: e.tensor_scalar(dst, src, sc, bi, ALU.mult, ALU.add),
                        rd=[self.R_mod], wrp=[self.R_hx[tt]], xb=[R_b])

    def phase_b_from_dram(self, l, ntiles=NT, stage=9):
        P, A = self.P, self.ar
        mark = A.off
        xts = [A.alloc([D]) for _ in range(2)]
        R_x = [Reg(), Reg()]
        bufs = []
        for i in range(2):
            bufs.append((A.alloc([D], BF16), A.alloc([1]), A.alloc([1]), A.alloc([D]), Reg()))
        for tt in range(ntiles):
            src = self.ctx_in[tt * 128:(tt + 1) * 128, :] if tt < 2 else self.x_in[(tt - 2) * 128:(tt - 1) * 128, :]
            P.dma(xts[tt % 2], src, wr=[R_x[tt % 2]])
            self.tile_to_hx(l, tt, xts[tt % 2], R_x[tt % 2], bufs[tt % 2])
        self.barrier()
        A.off = mark

    def load_w_slice(self, l, col0, dst_bf, R_dst, stg, R_stg, eng="dve"):
        P = self.P
        wv = self.w_in[l].rearrange("(k p) n -> p k n", p=128)
        P.dma(stg, wv[:, :, col0:col0 + 128], wr=[R_stg])
        fn = lambda e: e.tensor_copy(dst_bf, stg)
        P.op(eng, fn, rd=[R_stg], wr=[R_dst])

    def proj(self, bank_ap, R_bank, w_bf, R_w, c0, n):
        P = self.P
        t0, t1 = c0 // 128, (c0 + n - 1) // 128
        rds = [R_w] + [self.R_hx[t] for t in range(t0, t1 + 1)]
        for k in range(8):
            P.t(lambda e, k=k: e.matmul(bank_ap, w_bf[:, k, :], self.hxT[:, k, c0:c0 + n], start=(k == 0), stop=(k == 7)),
                rd=rds, xb=[R_bank])

    def hgrn2(self, l, need_ctx_out, heads=(0, 1, 2, 3)):
        P, A = self.P, self.ar
        mark = A.off
        pb, Rpb = self.pb, self.R_pb
        stg = [A.alloc([8, 128]) for _ in range(2)]; R_stg = [Reg(), Reg()]
        wbf = [A.alloc([8, 128], BF16) for _ in range(5)]; R_w = [Reg() for _ in range(5)]
        vtok = A.alloc([NT, 128], BF16); R_v = Reg()
        o_f = A.alloc([T]); R_of = [Reg() for _ in range(NB)]
        qsb = A.alloc([T], BF16); R_qsb = [Reg() for _ in range(NB)]
        S = [A.alloc([128]) for _ in range(3)]; R_S = [Reg(), Reg(), Reg()]
        U = [A.alloc([128], BF16) for _ in range(4)]; R_U = [Reg() for _ in range(4)]
        sct = [A.alloc([64], BF16) for _ in range(4)]; R_sct = [Reg() for _ in range(4)]
        scm = [A.alloc([64], BF16) for _ in range(4)]; R_scm = [Reg() for _ in range(4)]
        sets = []
        for _ in range(4):
            d_ = {}
            for nm in ("kk", "g", "p", "pe", "qs", "E", "ex", "exn", "exh", "sgA"):
                d_[nm] = A.alloc([BLK])
            for nm in ("Qt", "Kh", "res", "KtP0", "KtP1", "KtZ0", "KtZ1"):
                d_[nm] = A.alloc([BLK], BF16)
            d_["Khz"] = [A.alloc([2, 128], BF16) for _ in range(2)]
            for nm in ("L1", "sgq"):
                d_[nm] = A.alloc([BLK])
            d_["ad"] = A.alloc([4, 2]); d_["t4"] = A.alloc([4])
            d_["R"] = {nm: Reg() for nm in ("kk", "g", "p", "pe", "qs", "e", "qt", "kt", "ktz", "kh", "khtok", "ad", "sga", "o")}
            sets.append(d_)
        og, _ = _SV["gnw"]
        ones64 = self.onesF[:, 0:64]
        nstg = 0
        rmask = A.alloc([BLK]); R_rmask = Reg()
        P.g(lambda e: e.memset(rmask, 1.0), wr=[R_rmask])
        for c_ in range(4):
            P.g(lambda e: e.memset(rmask[:, c_ * 64:c_ * 64 + 1], 0.0), wr=[R_rmask])
        for st_ in sets:
            for nm in ("KtP0", "KtP1"):
                P.g(lambda e: e.memset(st_[nm], 0.0), wr=[st_["R"]["kt"]])
            for kz in st_["Khz"]:
                P.g(lambda e: e.memset(kz, 0.0), wr=[st_["R"]["khtok"]])
        for ci in range(4):
            P.g(lambda e: e.memset(scm[ci], 0.0), wr=[R_scm[ci]])
        for h in heads:
            for si in range(5):
                self.load_w_slice(l, si * 512 + h * 128, wbf[si], R_w[si], stg[nstg % 2], R_stg[nstg % 2])
                nstg += 1
            def vtok_gen():
                for g4 in range((NT + 3) // 4):
                    bk = 6 + g4 % 2
                    tiles = list(range(g4 * 4, min(NT, g4 * 4 + 4)))
                    for j, tt in enumerate(tiles):
                        for k in range(8):
                            P.t(lambda e: e.matmul(pb[bk][:, j * 128:(j + 1) * 128], self.hxT[:, k, tt * 128:(tt + 1) * 128],
                                                   wbf[3][:, k, :], start=(k == 0), stop=(k == 7)),
                                rd=[R_w[3], self.R_hx[tt]], xb=[Rpb[bk]])
                        yield
                    n = len(tiles) * 128
                    dst = vtok[:, tiles[0]:tiles[0] + len(tiles), :]
                    src = pb[bk][:, 0:n].rearrange("p (a b) -> p a b", b=128)
                    if g4 % 2 == 0:
                        P.a(lambda e: e.activation(dst, src, AF.Copy), wrp=[R_v], xb=[Rpb[bk]])
                    else:
                        P.v(lambda e: e.tensor_copy(dst, src), wrp=[R_v], xb=[Rpb[bk]])
                    yield
            vtg = vtok_gen()
            for d in (0, 1):
                oml = self.omlb[:, l, d * 4 + h:d * 4 + h + 1]
                P.g(lambda e: e.memset(S[0], 0.0), wr=[R_S[0]])
                for st_ in sets:
                    for nm in ("KtZ0", "KtZ1"):
                        P.g(lambda e: e.memset(st_[nm], 0.0), wr=[st_["R"]["ktz"]])
                order = list(range(NB)) if d == 0 else [0] + list(range(NB - 1, 0, -1))
                gcs = [0]

                def prep0(bi, blk):
                    bp = bi % 4
                    need_out = blk > 0 or need_ctx_out
                    c0 = blk * BLK
                    self.proj(pb[bp][:, 0:BLK], Rpb[bp], wbf[1 + d], R_w[1 + d], c0, BLK)
                    if need_out and d == 0:
                        self.proj(pb[bp][:, BLK:2 * BLK], Rpb[bp], wbf[0], R_w[0], c0, BLK)

                def prep1(bi, blk):
                    st = sets[bi % 4]; R = st["R"]
                    bp = bi % 4
                    need_out = blk > 0 or need_ctx_out
                    c0 = blk * BLK
                    kk, g, p, pe, qs, E, ex, exn, exh = (st[n_] for n_ in ("kk", "g", "p", "pe", "qs", "E", "ex", "exn", "exh"))
                    Qt, Kh, Khz, ad, t4, L1, sgq = st["Qt"], st["Kh"], st["Khz"], st["ad"], st["t4"], st["L1"], st["sgq"]
                    p3 = p.rearrange("p (c t) -> p c t", c=4); g3 = g.rearrange("p (c t) -> p c t", c=4)
                    pe3 = pe.rearrange("p (c t) -> p c t", c=4)
                    P.a(lambda e: e.activation(kk, pb[bp][:, 0:BLK], AF.Exp), wr=[R["kk"]], xb=[Rpb[bp]])
                    yield
                    P.a(lambda e: e.activation(L1, kk, AF.Ln, bias=1.0), rd=[R["kk"]], wr=[R["g"]])
                    yield
                    P.a(lambda e: e.activation(kk, L1, AF.Exp, scale=-1.0), rd=[R["g"]], wr=[R["kk"]])
                    yield
                    if need_out and d == 0:
                        P.a(lambda e: e.activation(sgq, pb[bp][:, BLK:2 * BLK], AF.Exp, scale=-1.0), wr=[R["qs"]], xb=[Rpb[bp]])
                        yield
                        P.a(lambda e: e.activation(sgq, sgq, AF.Ln, bias=1.0), wr=[R["qs"]])
                        yield
                        P.a(lambda e: e.activation(sgq, sgq, AF.Exp, scale=-1.0), wr=[R["qs"]])
                        yield
                        P.v(lambda e: e.tensor_tensor(qsb[:, c0:c0 + BLK], pb[bp][:, BLK:2 * BLK], sgq, ALU.mult), rd=[R["qs"]], wr=[R_qsb[blk]], xb=[Rpb[bp]])
                        yield
                    P.v(lambda e: e.tensor_scalar(kk, kk, oml, None, ALU.mult), rd=[self.R_mod], wr=[R["kk"]])
                    yield
                    P.a(lambda e: e.activation(g, kk, AF.Ln, scale=-1.0, bias=1.0), rd=[R["kk"]], wr=[R["g"]])
                    yield
                    P.v(lambda e: e.tensor_tensor_scan(p, rmask, g, 0.0, ALU.mult, ALU.add), rd=[R_rmask, R["g"]], wr=[R["p"]])
                    yield
                    if d == 0:
                        base3 = p3
                        mid = 31
                        P.v(lambda e: e.tensor_tensor(exh.rearrange("p (c t) -> p c t", c=4), p3[:, :, 63:64].to_broadcast([128, 4, 64]), p3, ALU.subtract),
                            rd=[R["p"]], wr=[R["kh"]])
                        yield
                        P.a(lambda e: e.activation(exh, exh, AF.Exp), wr=[R["kh"]])
                        yield
                        P.a(lambda e: e.activation(ad, p3[:, :, 31::32], AF.Exp), rd=[R["p"]], wr=[R["ad"]])
                        yield
                    else:
                        base3 = pe3
                        mid = 32
                        P.g(lambda e: e.tensor_tensor(pe, p, g, ALU.subtract), rd=[R["g"], R["p"]], wr=[R["pe"]])
                        yield
                        P.a(lambda e: e.activation(exh, pe, AF.Exp), rd=[R["pe"]], wr=[R["kh"]])
                        yield
                        P.v(lambda e: e.tensor_tensor(t4, p3[:, :, 63], pe3[:, :, 32], ALU.subtract), rd=[R["p"], R["pe"]], wr=[R["ad"]])
                        yield
                        P.a(lambda e: e.activation(ad[:, :, 0], t4, AF.Exp), wr=[R["ad"]])
                        yield
                        P.a(lambda e: e.activation(ad[:, :, 1], p3[:, :, 63], AF.Exp), rd=[R["p"]], wr=[R["ad"]])
                        yield
                    P.g(lambda e: e.tensor_tensor(Kh, kk, exh, ALU.mult), rd=[R["kk"]], wr=[R["kh"]])
                    yield
                    if need_out:
                        base = p if d == 0 else pe
                        rbase = R["p"] if d == 0 else R["pe"]
                        P.v(lambda e: e.tensor_tensor(E.rearrange("p (c t) -> p c t", c=4), base3, base3[:, :, mid:mid + 1].to_broadcast([128, 4, 64]), ALU.subtract),
                            rd=[rbase], wr=[R["e"]])
                        yield
                        P.a(lambda e: e.activation(ex, E, AF.Exp), wr=[R["e"]])
                        yield
                        P.a(lambda e: e.activation(exn, E, AF.Exp, scale=-1.0), wr=[R["e"]])
                        yield
                        eq, ek = (ex, exn) if d == 0 else (exn, ex)
                        P.g(lambda e: e.tensor_tensor(Qt, qsb[:, c0:c0 + BLK], eq, ALU.mult), rd=[R_qsb[blk], R["e"]], wr=[R["qt"]])
                        yield
                        hs = slice(0, 32) if d == 0 else slice(32, 64)
                        v5 = lambda a_: a_.rearrange("p (a b t) -> p a b t", a=2, b=2)
                        for hh_ in range(2):
                            P.g(lambda e: e.tensor_tensor(v5(st["KtP%d" % hh_])[:, :, hh_, :], v5(kk)[:, :, hh_, :], v5(ek)[:, :, hh_, :], ALU.mult),
                                rd=[R["kk"], R["e"]], wrp=[R["kt"]])
                            yield
                            P.g(lambda e: e.tensor_tensor(v5(st["KtZ%d" % hh_])[:, :, hh_, hs], v5(kk)[:, :, hh_, hs], v5(ek)[:, :, hh_, hs], ALU.mult),
                                rd=[R["kk"], R["e"]], wrp=[R["ktz"]])
                            yield

                def prep2(bi, blk):
                    st = sets[bi % 4]; R = st["R"]
                    bp = bi % 4
                    need_out = blk > 0 or need_ctx_out
                    c0 = blk * BLK
                    Kh, Khz, sgq = st["Kh"], st["Khz"], st["sgq"]
                    khps = pb[bp][:, 0:128].bitcast(BF16).rearrange("p (a b) -> p a b", a=2)
                    for ti in range(2):
                        P.t(lambda e, ti=ti: e.transpose(khps[:, ti, :], Kh[:, ti * 128:(ti + 1) * 128], self.identB),
                            rd=[R["kh"], self.R_const], xb=[Rpb[bp]])
                        yield
                    P.v(lambda e: e.tensor_copy(Khz[0][0:64], khps[0:64]), wrp=[R["khtok"]], xb=[Rpb[bp]])
                    yield
                    P.v(lambda e: e.tensor_copy(Khz[1][64:128], khps[64:128]), wrp=[R["khtok"]], xb=[Rpb[bp]])
                    yield
                    if d == 1 and need_out:
                        self.proj(pb[bp][:, BLK:2 * BLK], Rpb[bp], wbf[4], R_w[4], c0, BLK)
                        yield
                        P.a(lambda e: e.activation(sgq, pb[bp][:, BLK:2 * BLK], AF.Exp, scale=-1.0), rd=[R["qs"]], wr=[R["sga"]], xb=[Rpb[bp]])
                        yield
                        P.a(lambda e: e.activation(sgq, sgq, AF.Ln, bias=1.0), wr=[R["sga"]])
                        yield
                        P.a(lambda e: e.activation(sgq, sgq, AF.Exp, scale=-1.0), wr=[R["sga"]])
                        yield
                        P.v(lambda e: e.tensor_tensor(st["sgA"], pb[bp][:, BLK:2 * BLK], sgq, ALU.mult), wr=[R["sga"]], xb=[Rpb[bp]])
                        yield

                def chain(bi, blk):
                    st = sets[bi % 4]; R = st["R"]
                    bp = bi % 2
                    need_out = blk > 0 or need_ctx_out
                    c0 = blk * BLK
                    Qt, Khz, ad = st["Qt"], st["Khz"], st["ad"]
                    ob = 6 + bp
                    cis = (0, 1, 2, 3) if d == 0 else (3, 2, 1, 0)
                    if need_out:
                        for ci in cis:
                            hh = ci % 2
                            r0, r1 = 64 * hh, 64 * hh + 64
                            cs = slice(ci * 64, ci * 64 + 64)
                            sb_ = 5
                            cA = slice(ci * 64, ci * 64 + 32)
                            cB = slice(ci * 64 + 32, ci * 64 + 64)
                            c1, o1, c2, o2 = (cB, slice(32, 64), cA, slice(0, 32)) if d == 0 else (cA, slice(0, 32), cB, slice(32, 64))
                            tcs = slice((ci // 2) * 128, (ci // 2) * 128 + 128)
                            P.t(lambda e: e.matmul(pb[sb_][:, o1], st["KtP%d" % hh][:, tcs], Qt[:, c1], start=True, stop=True),
                                rd=[R["kt"], R["qt"]], xb=[Rpb[sb_]])
                            yield
                            P.t(lambda e: e.matmul(pb[sb_][:, o2], st["KtZ%d" % hh][:, tcs], Qt[:, c2], start=True, stop=True),
                                rd=[R["ktz"], R["qt"]], xb=[Rpb[sb_]])
                            yield
                            P.v(lambda e: e.tensor_copy(sct[ci][r0:r1, :], pb[sb_][r0:r1, 0:64]), wr=[R_sct[ci]], xb=[Rpb[sb_]])
                            yield
                            P.g(lambda e: e.tensor_tensor(scm[ci][r0:r1, :], sct[ci][r0:r1, :], self.masks[r0:r1, d, :], ALU.mult),
                                rd=[R_sct[ci], self.R_const], wrp=[R_scm[ci]])
                            yield
                    for ci in cis:
                        gc = gcs[0]
                        gcs[0] += 1
                        cp = gc % 2
                        Sc, Sn = S[gc % 3], S[(gc + 1) % 3]
                        RSc, RSn = R_S[gc % 3], R_S[(gc + 1) % 3]
                        ti, hh = ci // 2, ci % 2
                        tile = blk * 2 + ti
                        r0, r1 = 64 * hh, 64 * hh + 64
                        dsb, Rdsb = (pb[4][:, 0:128], Rpb[4]) if cp == 0 else (pb[4][:, 128:256], Rpb[4])
                        P.t(lambda e: e.matmul(dsb, Khz[hh][:, ti, :], vtok[:, tile, :], start=True, stop=True),
                            rd=[R["khtok"], R_v], xb=[Rdsb])
                        yield
                        if need_out:
                            P.a(lambda e: e.activation(U[ci], Sc, AF.Copy, scale=ad[:, ci, 0:1]), rd=[RSc, R["ad"]], wr=[R_U[ci]])
                            yield
                        P.v(lambda e: e.scalar_tensor_tensor(Sn, Sc, ad[:, ci, 1:2], dsb, ALU.mult, ALU.add),
                            rd=[RSc, R["ad"]], wr=[RSn], xb=[Rdsb])
                        yield
                    if need_out:
                        for ci in cis:
                            ti, hh = ci // 2, ci % 2
                            tile = blk * 2 + ti
                            r0, r1 = 64 * hh, 64 * hh + 64
                            cs = slice(ci * 64, ci * 64 + 64)
                            P.t(lambda e: e.matmul(pb[ob][:, cs], vtok[:, tile, :], scm[ci][:, :], start=True, stop=False),
                                rd=[R_v, R_scm[ci]], xb=[Rpb[ob]])
                            yield
                            P.t(lambda e: e.matmul(pb[ob][:, cs], U[ci], Qt[:, cs], start=False, stop=True),
                                rd=[R_U[ci], R["qt"]], xb=[Rpb[ob]])
                            yield
                    if need_out:
                        cols = slice(c0, c0 + BLK)
                        if d == 0:
                            P.a(lambda e, cols=cols, ob=ob: e.activation(o_f[:, cols], pb[ob][:, 0:BLK], AF.Copy), wr=[R_of[blk]], xb=[Rpb[ob]])
                            yield
                        else:
                            osum, sq, rs, res = st["E"], st["ex"], st["exn"], st["res"]
                            P.v(lambda e, cols=cols, ob=ob: e.tensor_tensor(osum, o_f[:, cols], pb[ob][:, 0:BLK], ALU.add), rd=[R_of[blk]], wr=[R["e"]], xb=[Rpb[ob]])
                            yield
                            P.g(lambda e: e.tensor_tensor(res, osum, osum, ALU.mult), rd=[R["e"]], wr=[R["o"]])
                            yield
                            P.t(lambda e, ob=ob: e.matmul(pb[ob][:, BLK:2 * BLK], self.onesB, res, start=True, stop=True), rd=[R["o"], self.R_const], xb=[Rpb[ob]])
                            yield
                            P.a(lambda e, ob=ob: e.activation(rs, pb[ob][:, BLK:2 * BLK], AF.Ln, scale=1.0 / 128, bias=EPS), wr=[R["e"]], xb=[Rpb[ob]])
                            yield
                            P.a(lambda e: e.activation(rs, rs, AF.Exp, scale=-0.5), wr=[R["e"]])
                            yield
                            P.v(lambda e: e.tensor_tensor(osum, osum, rs, ALU.mult), wr=[R["e"]])
                            yield
                            gn = self.sv[:, og + l * 4 + h:og + l * 4 + h + 1]
                            P.v(lambda e, gn=gn: e.scalar_tensor_tensor(res, osum, gn, st["sgA"], ALU.mult, ALU.mult), rd=[R["e"], R["sga"], self.R_sv], wr=[R["o"]])
                            yield
                            P.dma(self.MIX[h, :, cols], res, rd=[R["o"]], wr=[self.R_MIX[h][blk]])
                            yield

                order_ = order

                def zipgen(gs):
                    gs = list(gs)
                    while gs:
                        for g_ in list(gs):
                            try:
                                next(g_)
                                yield
                            except StopIteration:
                                gs.remove(g_)

                npairs = (len(order_) + 1) // 2

                def pair_idx(j):
                    return [b_ for b_ in (2 * j, 2 * j + 1) if b_ < len(order_)]

                def stageA(j):
                    for b_ in pair_idx(j):
                        prep0(b_, order_[b_])
                    yield
                    yield from zipgen([prep1(b_, order_[b_]) for b_ in pair_idx(j)])

                def stageBC(j):
                    for b_ in pair_idx(j):
                        yield from prep2(b_, order_[b_])
                    for b_ in pair_idx(j):
                        yield from chain(b_, order_[b_])

                ga = stageA(0)
                if d == 0:
                    gl = [ga, vtg]
                    while gl:
                        for g_ in list(gl):
                            try:
                                next(g_)
                            except StopIteration:
                                gl.remove(g_)
                else:
                    for _ in ga:
                        pass
                def drain_weighted(gw):
                    gw = [[g_, w_] for g_, w_ in gw]
                    while gw:
                        for it in list(gw):
                            for _ in range(it[1]):
                                try:
                                    next(it[0])
                                except StopIteration:
                                    gw.remove(it)
                                    break

                for j in range(npairs):
                    gw = [(stageBC(j), HG_W)]
                    if j + 1 < npairs:
                        gw.insert(0, (stageA(j + 1), 1))
                    drain_weighted(gw)
        self.barrier()
        A.off = mark

    def load_tm(self, k0, nb, m0, buf, R_buf, q="sp"):
        self.P.dma(buf[:, 0:nb], self.c_TM[:, k0:k0 + nb, m0:m0 + 3, :], wr=[R_buf], q=q)

    def fft_stepA(self, Xs, R_xs, K, A_sb, R_a):
        P, pb, Rpb = self.P, self.pb, self.R_pb
        for c8 in range(16):
            bk = c8 % 2
            for j in range(8):
                cl = c8 * 8 + j
                q, cc = cl // 64, cl % 64
                P.t(lambda e: e.matmul(pb[bk][:, j * 64:(j + 1) * 64], Xs[64 * q:64 * q + K, cc, :], self.FA2[64 * q:64 * q + K, :], start=True, stop=True),
                    rd=[R_xs, self.R_const], xb=[Rpb[bk]])
            dst = A_sb[:, :, c8 * 8:(c8 + 1) * 8].rearrange("p k c -> p c k")
            src = pb[bk][:, 0:512].rearrange("p (c k) -> p c k", c=8)
            if bk == 0:
                P.a(lambda e: e.activation(dst, src, AF.Copy), wrp=[R_a], xb=[Rpb[bk]])
            else:
                P.v(lambda e: e.tensor_copy(dst, src), wrp=[R_a], xb=[Rpb[bk]])

    def fft_stepC(self, src_sb, R_src, inverse, tmb, R_tm, consume, tmq="sp"):
        P, pb, Rpb = self.P, self.pb, self.R_pb
        im_off = 33 if inverse else 32
        for bi, k0 in enumerate(range(0, 33, 4)):
            nb = min(4, 33 - k0)
            tm, rtm = tmb[bi % 2], R_tm[bi % 2]
            self.load_tm(k0, nb, 3 if inverse else 0, tm, rtm, q=tmq)
            br, bim = 2 + 2 * (bi % 2), 3 + 2 * (bi % 2)
            for j in range(nb):
                k1 = k0 + j
                real_only = k1 in (0, 32)
                re = src_sb[:, k1, :]
                im = None if (real_only and not inverse) else src_sb[:, im_off + k1, :]
                cs = slice(j * 128, (j + 1) * 128)
                Tc, Ts, mTs = tm[:, j, 0, :], tm[:, j, 1, :], tm[:, j, 2, :]
                if not inverse:
                    P.t(lambda e: e.matmul(pb[br][:, cs], Tc, re, start=True, stop=real_only), rd=[R_src, rtm], xb=[Rpb[br]])
                    if not real_only:
                        P.t(lambda e: e.matmul(pb[br][:, cs], mTs, im, start=False, stop=True), rd=[R_src, rtm], xb=[Rpb[br]])
                    P.t(lambda e: e.matmul(pb[bim][:, cs], Ts, re, start=True, stop=real_only), rd=[R_src, rtm], xb=[Rpb[bim]])
                    if not real_only:
                        P.t(lambda e: e.matmul(pb[bim][:, cs], Tc, im, start=False, stop=True), rd=[R_src, rtm], xb=[Rpb[bim]])
                else:
                    P.t(lambda e: e.matmul(pb[br][:, cs], Tc, re, start=True, stop=False), rd=[R_src, rtm], xb=[Rpb[br]])
                    P.t(lambda e: e.matmul(pb[br][:, cs], Ts, im, start=False, stop=True), rd=[R_src, rtm], xb=[Rpb[br]])
                    if not real_only:
                        P.t(lambda e: e.matmul(pb[bim][:, cs], mTs, re, start=True, stop=False), rd=[R_src, rtm], xb=[Rpb[bim]])
                        P.t(lambda e: e.matmul(pb[bim][:, cs], Tc, im, start=False, stop=True), rd=[R_src, rtm], xb=[Rpb[bim]])
            consume(bi, k0, nb, pb[br], pb[bim], Rpb[br], Rpb[bim])

    def load_xs(self, Xs, R_xs, src_rows, n1cnt, zero_first=False, rd=()):
        P = self.P
        if zero_first:
            P.g(lambda e: e.memset(Xs, 0.0), wr=[R_xs])
        for q in range(2):
            src = src_rows[64 * q:64 * q + 64, :].rearrange("c (n1 n2) -> n1 c n2", n2=128)
            if zero_first:
                P.dma(Xs[64 * q:64 * q + n1cnt, :, :], src, rd=list(rd), wrp=[R_xs])
            else:
                P.dma(Xs[64 * q:64 * q + n1cnt, :, :], src, rd=list(rd), wrp=[R_xs])

    def hy_filter(self, l, seg, alias_hx=False):
        P, A, pb, Rpb = self.P, self.ar, self.pb, self.R_pb
        mark = A.off
        if alias_hx:
            A.off = self.hx_off
        nbuf = 2 if alias_hx else 1
        Ln = L if seg == "x" else LC
        CH = min(512, Ln)
        nch = Ln // CH
        Ha = A.alloc([Ln]); R_Ha = Reg()
        mark2 = A.off
        zz = A.alloc([Ln]); R_zz = Reg()
        Hb = A.alloc([Ln]); R_Hb = Reg()
        W1 = A.alloc([128]); W2 = A.alloc([128]); W3 = A.alloc([128]); W4f = A.alloc([512]); W4b = A.alloc([512])
        R_W = Reg()
        frb = A.alloc([4]); R_frb = Reg()
        argb = [A.alloc([CH]) for _ in range(4)]; mb = [A.alloc([CH]) for _ in range(4)]; R_arg = [Reg() for _ in range(4)]
        for w_ in (W1, W2, W3, W4f, W4b):
            P.g(lambda e: e.memset(w_, 0.0), wr=[R_W])
        P.dma(W1[0:33, 0:64], self.hy_w1[l], wr=[R_W]); P.dma(W1[33:66, 64:128], self.hy_w1[l], wr=[R_W])
        P.dma(W2[0:64, 0:64], self.hy_w2[l], wr=[R_W]); P.dma(W2[64:128, 64:128], self.hy_w2[l], wr=[R_W])
        P.dma(W3[0:64, 0:64], self.hy_w3[l], wr=[R_W]); P.dma(W3[64:128, 64:128], self.hy_w3[l], wr=[R_W])
        P.dma(W4f[0:64, :], self.hy_w4[l][:, 0:512], wr=[R_W]); P.dma(W4b[64:128, :], self.hy_w4[l][:, 512:1024], wr=[R_W])
        P.dma(zz[0:66, :], self.c_ZZ[seg], wr=[R_zz])
        ofr = _SV["hfr"][0] + l
        fr = self.sv[:, ofr:ofr + 1]
        for i, nm in enumerate(("hb1", "hb2", "hb3")):
            ob = _SV[nm][0] + l
            P.v(lambda e: e.tensor_tensor(frb[:, i:i + 1], self.sv[:, ob:ob + 1], fr, ALU.mult), rd=[self.R_sv], wrp=[R_frb])
        layers = [(W1, zz, R_zz, 66, Ha, R_Ha), (W2, Ha, R_Ha, 128, Hb, R_Hb), (W3, Hb, R_Hb, 128, Ha, R_Ha)]
        n = 0
        for li, (W, src, R_src, Kc, dst, R_dst) in enumerate(layers):
            for ch in range(nch):
                cs = slice(ch * CH, (ch + 1) * CH)
                bk = n % 4
                arg, m_, ra = argb[n % 4], mb[n % 4], R_arg[n % 4]
                n += 1
                P.t(lambda e: e.matmul(pb[bk][:, 0:CH], W[0:Kc, :], src[0:Kc, cs], start=True, stop=True), rd=[R_W, R_src], xb=[Rpb[bk]])
                P.v(lambda e: e.tensor_scalar(arg, pb[bk][:, 0:CH], fr, frb[:, li:li + 1], ALU.mult, ALU.add), rd=[self.R_sv, R_frb], wr=[ra], xb=[Rpb[bk]])
                P.v(lambda e: e.tensor_scalar(m_, arg, PI, None, ALU.is_gt), wr=[ra])
                P.v(lambda e: e.scalar_tensor_tensor(arg, m_, -2 * PI, arg, ALU.mult, ALU.add), wr=[ra])
                P.v(lambda e: e.tensor_scalar(m_, arg, -PI, None, ALU.is_lt), wr=[ra])
                P.v(lambda e: e.scalar_tensor_tensor(arg, m_, 2 * PI, arg, ALU.mult, ALU.add), wr=[ra])
                P.a(lambda e: e.activation(dst[:, cs], arg, AF.Sin), rd=[ra], wrp=[R_dst])
        self.barrier()
        A.off = mark2
        H3, R_H3 = Ha, R_Ha
        W4f2 = A.alloc([512]); W4b2 = A.alloc([512]); R_W2 = Reg()
        P.g(lambda e: e.memset(W4f2, 0.0), wr=[R_W2]); P.g(lambda e: e.memset(W4b2, 0.0), wr=[R_W2])
        P.dma(W4f2[0:64, :], self.hy_w4[l][:, 0:512], wr=[R_W2]); P.dma(W4b2[64:128, :], self.hy_w4[l][:, 512:1024], wr=[R_W2])
        gb = []
        for _ in range(nbuf):
            gb.append(dict(HF=A.alloc([Ln]), HB=A.alloc([Ln]), R_HF=Reg(), R_HB=Reg(), kfb=A.alloc([NFFT], BF16), R_kfb=Reg(),
                           Xs=A.alloc([64, 128], BF16), R_xs=Reg(), nrm=A.alloc([4]), R_nrm=Reg()))
        ttb = [A.alloc([2, CH]) for _ in range(2)]; R_tt = [Reg(), Reg()]
        dcb = [A.alloc([2, CH]) for _ in range(2)]; R_dc = [Reg(), Reg()]
        tmb = [A.alloc([4, 3, 128], BF16) for _ in range(2)]; R_tm = [Reg(), Reg()]
        evb = [A.alloc([2, 512]) for _ in range(2)]; R_ev = [Reg(), Reg()]
        od = _SV["delt"][0]
        cnt = [0]

        def gen(cg):
            b_ = gb[cg % nbuf]
            HF, HB, R_HF, R_HB, kfb, R_kfb, nrm, R_nrm = (b_[k_] for k_ in ("HF", "HB", "R_HF", "R_HB", "kfb", "R_kfb", "nrm", "R_nrm"))
            dl = self.sv[:, od + cg:od + cg + 1]
            for ch in range(nch):
                cs = slice(ch * CH, (ch + 1) * CH)
                n = cnt[0]; cnt[0] += 1
                tt, rtt, dc, rdc = ttb[n % 2], R_tt[n % 2], dcb[n % 2], R_dc[n % 2]
                for r_ in range(2):
                    P.dma(tt[:, r_, :], self.c_TT[seg][r_:r_ + 1, cs].partition_broadcast(128).squeeze(1), wrp=[rtt])
                P.a(lambda e: e.activation(dc, tt, AF.Exp, scale=dl), rd=[rtt, self.R_sv], wr=[rdc])
                P.t(lambda e: e.matmul(pb[2][:, 0:CH], W4f2[:, cg * 128:(cg + 1) * 128], H3[:, cs], start=True, stop=True), rd=[R_W2, R_H3], xb=[Rpb[2]])
                P.t(lambda e: e.matmul(pb[3][:, 0:CH], W4b2[:, cg * 128:(cg + 1) * 128], H3[:, cs], start=True, stop=True), rd=[R_W2, R_H3], xb=[Rpb[3]])
                P.v(lambda e: e.tensor_tensor(HF[:, cs], pb[2][:, 0:CH], dc[:, 0, :], ALU.mult), rd=[rdc], wrp=[R_HF], xb=[Rpb[2]])
                P.v(lambda e: e.tensor_tensor(HB[:, cs], pb[3][:, 0:CH], dc[:, 1, :], ALU.mult), rd=[rdc], wrp=[R_HB], xb=[Rpb[3]])
            P.g(lambda e: e.memset(HB[:, 0:1], 0.0), rd=[R_HB], wrp=[R_HB])
            P.v(lambda e: e.tensor_reduce(nrm[:, 0:1], HF, mybir.AxisListType.X, ALU.add, apply_absolute_value=True), rd=[R_HF], wr=[R_nrm])
            P.v(lambda e: e.tensor_reduce(nrm[:, 1:2], HB, mybir.AxisListType.X, ALU.add, apply_absolute_value=True), rd=[R_HB], wr=[R_nrm])
            P.v(lambda e: e.tensor_tensor(nrm[:, 2:3], nrm[:, 0:1], nrm[:, 1:2], ALU.add), wr=[R_nrm])
            P.v(lambda e: e.reciprocal(nrm[:, 3:4], nrm[:, 2:3]), wr=[R_nrm])
            rn = nrm[:, 3:4]
            if seg == "c":
                P.a(lambda e: e.activation(self.kwin[:, cg, 0:LC - 1], HB[:, 1:LC], AF.Copy, scale=rn), rd=[R_HB, R_nrm], wrp=[self.R_kwin[cg]])
                P.a(lambda e: e.activation(self.kwin[:, cg, LC - 1:2 * LC - 1], HF, AF.Copy, scale=rn), rd=[R_HF, R_nrm], wrp=[self.R_kwin[cg]])
                return
            P.a(lambda e: e.activation(kfb[:, 0:Ln], HF, AF.Copy, scale=rn), rd=[R_HF, R_nrm], wr=[R_kfb])
            P.a(lambda e: e.activation(kfb[:, NFFT - Ln:NFFT], HB, AF.Copy, scale=rn), rd=[R_HB, R_nrm], wr=[R_kfb])
            self.dump("kfilt%d%s%d" % (l, seg, cg), kfb, rd=[R_kfb], dt=BF16)
            P.dma(self.KD[cg * 128:(cg + 1) * 128, :], kfb, rd=[R_kfb], wr=[self.R_KD4[cg]])
            self.load_xs(b_["Xs"], b_["R_xs"], self.KD[cg * 128:(cg + 1) * 128, :], 64, rd=[self.R_KD4[cg]])

        def fft(cg):
            b_ = gb[cg % nbuf]
            kfb, R_kfb = b_["kfb"], b_["R_kfb"]
            A_sb = kfb.rearrange("p (k c) -> p k c", c=128); R_a = R_kfb
            self.fft_stepA(b_["Xs"], b_["R_xs"], 64, A_sb, R_a)
            KFd = self.KF[(l, seg)]

            def consume(bi, k0, nb, XR, XI, RXR, RXI):
                ev, rev = evb[bi % 2], R_ev[bi % 2]
                P.a(lambda e: e.activation(ev[:, 0, 0:nb * 128], XR[:, 0:nb * 128], AF.Copy), wr=[rev], xb=[RXR])
                P.v(lambda e: e.tensor_copy(ev[:, 1, 0:nb * 128], XI[:, 0:nb * 128]), wr=[rev], xb=[RXI])
                for ri in range(2):
                    P.dma(KFd[k0:k0 + nb, ri, :, cg * 128:(cg + 1) * 128].rearrange("k p c -> p k c"),
                          ev[:, ri, 0:nb * 128].rearrange("p (k c) -> p k c", c=128), rd=[rev], wrp=[self.R_KF[(l, seg)][cg]])
            self.fft_stepC(A_sb, R_a, False, tmb, R_tm, consume, tmq="act")

        if nbuf == 1:
            for cg in range(4):
                gen(cg)
                if seg != "c":
                    fft(cg)
        else:
            gen(0)
            for cg in range(4):
                if cg + 1 < 4:
                    gen(cg + 1)
                if seg != "c":
                    fft(cg)
        self.barrier()
        A.off = mark

    def hyena(self, l, segs, groups=(0, 1, 2, 3)):
        P, A, pb, Rpb = self.P, self.ar, self.pb, self.R_pb
        mark = A.off
        vv = A.alloc([L]); R_vv = Reg()
        gate = A.alloc([L], BF16); R_gate = Reg()
        vvb = A.alloc([L], BF16); R_vvb = Reg()
        XZ = A.alloc([64, 128], BF16); R_xz = Reg()
        AZ = A.alloc([64 * 128], BF16); R_az = Reg()
        tmb = [A.alloc([4, 3, 128], BF16) for _ in range(2)]; R_tm = [Reg(), Reg()]
        mark2 = A.off
        ocw, ocb, ohb = _SV["convw"][0], _SV["convb"][0], _SV["hyb"][0]
        vvc = A.alloc([LC]); gatec = A.alloc([LC], BF16); vvbc = A.alloc([LC], BF16)
        VVC = (vvc, Reg(), gatec, Reg(), vvbc, Reg())
        VVX = (vv, R_vv, gate, R_gate, vvb, R_vvb)
        mark2 = A.off

        def seg_info(seg):
            Ln = L if seg == "x" else LC
            col0 = LC if seg == "x" else 0
            blocks = list(range(1, NB)) if seg == "x" else [0]
            return Ln, col0, blocks, Ln // 128

        def stage1_gen(seg, cg, VV):
            vv, R_vv, gate, R_gate, vvb, R_vvb = VV
            Ln, col0, blocks, n1cnt = seg_info(seg)
            stg = A.alloc([8, 128]); R_stg = Reg()
            wbf = [A.alloc([8, 128], BF16) for _ in range(4)]; R_w = [Reg() for _ in range(4)]
            Us = [[A.alloc([BLK + 2]) for _ in range(3)] for _ in range(2)]; R_U = [[Reg() for _ in range(3)] for _ in range(2)]
            cvs = [[A.alloc([BLK]) for _ in range(4)] for _ in range(2)]; R_cv = [[Reg() for _ in range(4)] for _ in range(2)]
            for si in range(4):
                self.load_w_slice(l, 2560 + si * 512 + cg * 128, wbf[si], R_w[si], stg, R_stg)
                yield
            for bi, blk in enumerate(blocks):
                sset = bi % 2
                c0 = blk * BLK
                first, last = bi == 0, bi == len(blocks) - 1
                lo = c0 if first else c0 - 1
                hi = c0 + BLK if last else c0 + BLK + 1
                n = hi - lo
                uo = 1 if first else 0
                lc = c0 - col0
                for si in range(4):
                    self.proj(pb[si][:, 0:n], Rpb[si], wbf[si], R_w[si], lo, n)
                    yield
                for si in range(3):
                    U, RU = Us[sset][si], R_U[sset][si]
                    if first:
                        P.g(lambda e: e.memset(U[:, 0:1], 0.0), wr=[RU])
                        yield
                    if last:
                        P.g(lambda e: e.memset(U[:, BLK + 1:BLK + 2], 0.0), wr=[RU])
                        yield
                    P.a(lambda e: e.activation(U[:, uo:uo + n], pb[si][:, 0:n], AF.Copy), wr=[RU], xb=[Rpb[si]])
                    yield
                sg, Rsg = cvs[sset][3], R_cv[sset][3]
                go = 0 if first else 1
                P.a(lambda e: e.activation(sg, pb[3][:, go:go + BLK], AF.Silu), wr=[Rsg], xb=[Rpb[3]])
                yield
                for si in range(3):
                    U, RU = Us[sset][si], R_U[sset][si]
                    cv, Rcv = cvs[sset][si], R_cv[sset][si]
                    wj = lambda j: self.sv[:, ocw + l * 36 + j * 12 + si * 4 + cg:ocw + l * 36 + j * 12 + si * 4 + cg + 1]
                    bb = self.sv[:, ocb + l * 12 + si * 4 + cg:ocb + l * 12 + si * 4 + cg + 1]
                    P.g(lambda e: e.tensor_scalar(cv, U[:, 1:BLK + 1], wj(1), bb, ALU.mult, ALU.add), rd=[RU, self.R_sv], wr=[Rcv])
                    yield
                    P.v(lambda e: e.scalar_tensor_tensor(cv, U[:, 0:BLK], wj(0), cv, ALU.mult, ALU.add), rd=[RU, self.R_sv], wr=[Rcv])
                    yield
                    P.v(lambda e: e.scalar_tensor_tensor(cv, U[:, 2:BLK + 2], wj(2), cv, ALU.mult, ALU.add), rd=[RU, self.R_sv], wr=[Rcv])
                    yield
                x0c, x1c, vc = cvs[sset][0], cvs[sset][1], cvs[sset][2]
                lcs = slice(lc, lc + BLK)
                P.v(lambda e: e.tensor_tensor(vv[:, lcs], vc, x1c, ALU.mult), rd=[R_cv[sset][1], R_cv[sset][2]], wrp=[R_vv])
                yield
                P.g(lambda e: e.tensor_copy(vvb[:, lcs], vv[:, lcs]), rd=[R_vv], wrp=[R_vvb])
                yield
                P.v(lambda e: e.tensor_tensor(gate[:, lcs], x0c, sg, ALU.mult), rd=[R_cv[sset][0], Rsg], wrp=[R_gate])
                yield
            VDd = self.VD if seg == "x" else self.VDC
            P.dma(VDd[cg * 128:(cg + 1) * 128, :], vvb[:, 0:Ln], rd=[R_vvb, R_vv], wr=[self.R_VD[cg]])
            yield
            self.dump("vv%d%s%d" % (l, seg, cg), vv[:, 0:Ln], rd=[R_vv])

        def convc_gen(cg, VV):
            vv, R_vv, gate, R_gate, vvb, R_vvb = VV
            seg = "c"
            Ln = LC
            ybuf = A.alloc([LC]); R_yb = Reg()
            accD = [A.alloc([LC]) for _ in range(8)]; R_aD = [Reg() for _ in range(8)]
            accP = [A.alloc([LC]) for _ in range(4)]; R_aP = [Reg() for _ in range(4)]
            tmpA = [A.alloc([LC]) for _ in range(8)]; R_tA = [Reg() for _ in range(8)]
            for a_, r_ in zip(accD + accP, R_aD + R_aP):
                P.g(lambda e: e.memset(a_, 0.0), wr=[r_])
                yield
            nd = na = 0
            for s_ in range(LC):
                win = self.kwin[:, cg, LC - 1 - s_:2 * LC - 1 - s_]
                vs = vv[:, s_:s_ + 1]
                if s_ % 8 < 7:
                    k_ = nd % 8; nd += 1
                    P.v(lambda e: e.scalar_tensor_tensor(accD[k_], win, vs, accD[k_], ALU.mult, ALU.add), rd=[self.R_kwin[cg], R_vv], wr=[R_aD[k_]])
                    yield
                else:
                    k_ = na % 8; j_ = na % 4; na += 1
                    P.a(lambda e: e.activation(tmpA[k_], win, AF.Copy, scale=vs), rd=[self.R_kwin[cg], R_vv], wr=[R_tA[k_]])
                    yield
                    P.g(lambda e: e.tensor_tensor(accP[j_], accP[j_], tmpA[k_], ALU.add), rd=[R_tA[k_]], wr=[R_aP[j_]])
                    yield
            for a_, b_ in ((0, 1), (2, 3), (4, 5), (6, 7), (0, 2), (4, 6), (0, 4)):
                P.v(lambda e: e.tensor_tensor(accD[a_], accD[a_], accD[b_], ALU.add), rd=[R_aD[b_]], wr=[R_aD[a_]])
                yield
            for a_, b_ in ((0, 1), (2, 3), (0, 2)):
                P.g(lambda e: e.tensor_tensor(accP[a_], accP[a_], accP[b_], ALU.add), rd=[R_aP[b_]], wr=[R_aP[a_]])
                yield
            P.v(lambda e: e.tensor_tensor(ybuf[:, 0:LC], accD[0], accP[0], ALU.add), rd=[R_aD[0], R_aP[0]], wr=[R_yb])
            yield
            self.dump("yconv%d%s%d" % (l, seg, cg), ybuf[:, 0:Ln], rd=[R_yb])
            hb = self.sv[:, ohb + l * 4 + cg:ohb + l * 4 + cg + 1]
            oBc = AZ[:, 0:Ln]
            P.v(lambda e: e.scalar_tensor_tensor(ybuf[:, 0:LC], vv[:, 0:LC], hb, ybuf[:, 0:LC], ALU.mult, ALU.add), rd=[R_vv, self.R_sv], wr=[R_yb])
            yield
            P.v(lambda e: e.tensor_tensor(oBc, ybuf[:, 0:LC], gate[:, 0:LC], ALU.mult), rd=[R_gate, R_yb], wrp=[R_az])
            yield
            P.dma(self.MIX[4 + cg, :, 0:BLK], oBc, rd=[R_az], wr=[self.R_MIX[4 + cg][0]])
            yield

        def stage2_x(cg):
            seg = "x"
            Ln, col0, blocks, n1cnt = seg_info(seg)
            VDd = self.VD
            A.off = mark2
            Y = A.alloc([66 * 128], BF16); R_y = Reg()
            Y_sb = Y.rearrange("p (k c) -> p k c", c=128)
            tq = [A.alloc([512]) for _ in range(4)]; R_tq = Reg()
            kfb = [A.alloc([2, 512]) for _ in range(2)]; R_kf = [Reg(), Reg()]
            ybuf = A.alloc([L]); R_yb = Reg()
            Xs = XZ
            self.load_xs(Xs, R_xz, VDd[cg * 128:(cg + 1) * 128, :], n1cnt, zero_first=(n1cnt < 32), rd=[self.R_VD[cg]])
            A_sb = AZ.rearrange("p (k c) -> p k c", c=128)
            self.fft_stepA(Xs, R_xz, 32, A_sb, R_az)
            KFd = self.KF[(l, seg)]

            def consume_f(bi, k0, nb, XR, XI, RXR, RXI):
                kf, rkf = kfb[bi % 2], R_kf[bi % 2]
                w = nb * 128
                for ri in range(2):
                    P.dma(kf[:, ri, 0:w].rearrange("p (k c) -> p k c", c=128),
                          KFd[k0:k0 + nb, ri, :, cg * 128:(cg + 1) * 128].rearrange("k p c -> p k c"),
                          rd=[self.R_KF[(l, seg)][cg]], wrp=[rkf])
                Kr, Ki = kf[:, 0, 0:w], kf[:, 1, 0:w]
                P.v(lambda e: e.tensor_tensor(tq[0][:, 0:w], XR[:, 0:w], Kr, ALU.mult), rd=[rkf], wr=[R_tq], xb=[RXR])
                P.v(lambda e: e.tensor_tensor(tq[1][:, 0:w], XI[:, 0:w], Ki, ALU.mult), rd=[rkf], wrp=[R_tq], xb=[RXI])
                P.v(lambda e: e.tensor_tensor(tq[2][:, 0:w], XR[:, 0:w], Ki, ALU.mult), rd=[rkf], wrp=[R_tq], xb=[RXR])
                P.v(lambda e: e.tensor_tensor(tq[3][:, 0:w], XI[:, 0:w], Kr, ALU.mult), rd=[rkf], wrp=[R_tq], xb=[RXI])
                yre = Y_sb[:, k0:k0 + nb, :].rearrange("p k c -> p (k c)")
                yim = Y_sb[:, 33 + k0:33 + k0 + nb, :].rearrange("p k c -> p (k c)")
                P.g(lambda e: e.tensor_tensor(yre, tq[0][:, 0:w], tq[1][:, 0:w], ALU.subtract), rd=[R_tq], wrp=[R_y])
                P.g(lambda e: e.tensor_tensor(yim, tq[2][:, 0:w], tq[3][:, 0:w], ALU.add), rd=[R_tq], wrp=[R_y])
            self.fft_stepC(A_sb, R_az, False, tmb, R_tm, consume_f)
            Zd = AZ.rearrange("p (m j k) -> p k j m", j=2, k=64)

            def consume_i(bi, k0, nb, ZR, ZI, RZR, RZI):
                dst = Zd[:, k0:k0 + nb, :, :]
                src = ZR[:, 0:nb * 128].rearrange("p (k j m) -> p k j m", j=2, m=64)
                P.a(lambda e: e.activation(dst, src, AF.Copy), wrp=[R_az], xb=[RZR])
                ks = [k for k in range(k0, k0 + nb) if k not in (0, 32)]
                if ks:
                    j0 = ks[0] - k0
                    dsti = Zd[:, 32 + ks[0]:32 + ks[-1] + 1, :, :]
                    srci = ZI[:, j0 * 128:(j0 + len(ks)) * 128].rearrange("p (k j m) -> p k j m", j=2, m=64)
                    P.v(lambda e: e.tensor_copy(dsti, srci), wrp=[R_az], xb=[RZI])
            self.fft_stepC(Y_sb, R_y, True, tmb, R_tm, consume_i)
            ZT = XZ
            Zv = AZ.rearrange("p (m x) -> p m x", x=128)
            for m8 in range(8):
                bk = m8 % 2
                pv = pb[bk][:, 0:512].bitcast(BF16).rearrange("p (a b) -> p a b", a=8)
                for j in range(8):
                    m = m8 * 8 + j
                    P.t(lambda e: e.transpose(pv[:, j, :], Zv[:, m, :], self.identB), rd=[R_az, self.R_const], xb=[Rpb[bk]])
                if bk == 0:
                    P.a(lambda e: e.activation(ZT[:, m8 * 8:(m8 + 1) * 8, :], pv, AF.Copy), wrp=[R_xz], xb=[Rpb[bk]])
                else:
                    P.v(lambda e: e.tensor_copy(ZT[:, m8 * 8:(m8 + 1) * 8, :], pv), wrp=[R_xz], xb=[Rpb[bk]])
            NN = n1cnt
            yv = ybuf[:, 0:Ln].rearrange("p (n1 n2) -> p n2 n1", n2=128)
            for g16 in range(8):
                bk = 2 + g16 % 2
                for j in range(16):
                    n2 = g16 * 16 + j
                    for jj in range(2):
                        P.t(lambda e: e.matmul(pb[bk][64 * jj:64 * jj + 64, j * NN:(j + 1) * NN], ZT[64 * jj:64 * jj + 64, :, n2],
                                               self.RA2[64 * jj:64 * jj + 64, 0:NN], start=True, stop=True),
                            rd=[R_xz, self.R_const], xb=[Rpb[bk]])
                src = pb[bk][:, 0:16 * NN].rearrange("p (a b) -> p a b", b=NN)
                dst = yv[:, g16 * 16:(g16 + 1) * 16, :]
                if g16 % 2 == 0:
                    P.a(lambda e: e.activation(dst, src, AF.Copy), wrp=[R_yb], xb=[Rpb[bk]])
                else:
                    P.v(lambda e: e.tensor_copy(dst, src), wrp=[R_yb], xb=[Rpb[bk]])
            self.dump("yconv%d%s%d" % (l, seg, cg), ybuf[:, 0:Ln], rd=[R_yb])
            hb = self.sv[:, ohb + l * 4 + cg:ohb + l * 4 + cg + 1]
            oB = AZ[:, 0:Ln]
            PW = min(1024, Ln)
            for pc in range(Ln // PW):
                cs = slice(pc * PW, (pc + 1) * PW)
                P.v(lambda e: e.scalar_tensor_tensor(ybuf[:, cs], vv[:, cs], hb, ybuf[:, cs], ALU.mult, ALU.add), rd=[R_vv, self.R_sv], wr=[R_yb])
                P.v(lambda e: e.tensor_tensor(oB[:, cs], ybuf[:, cs], gate[:, cs], ALU.mult), rd=[R_gate, R_yb], wrp=[R_az])
            for blk in blocks:
                lc = blk * BLK - col0
                P.dma(self.MIX[4 + cg, :, blk * BLK:(blk + 1) * BLK], oB[:, lc:lc + BLK], rd=[R_az], wr=[self.R_MIX[4 + cg][blk]])
            self.barrier()

        def drain_w(gw):
            gw = [[g_, w_] for g_, w_ in gw]
            while gw:
                for it in list(gw):
                    for _ in range(it[1]):
                        try:
                            next(it[0])
                        except StopIteration:
                            gw.remove(it)
                            break

        for cg in groups:
            if "c" in segs:
                A.off = mark2
                drain_w([(stage1_gen("c", cg, VVC), 1)])
                self.barrier()
                if "x" not in segs:
                    A.off = mark2
                    drain_w([(convc_gen(cg, VVC), 1)])
                    self.barrier()
            if "x" in segs:
                A.off = mark2
                gw = [(stage1_gen("x", cg, VVX), 1)]
                if "c" in segs:
                    gw.insert(0, (convc_gen(cg, VVC), 1))
                drain_w(gw)
                self.barrier()
                stage2_x(cg)
        A.off = mark

    def bcast_rows(self, dst, cols_fn, R_dst, tmpd, R_tmpd):
        P, pb, Rpb = self.P, self.pb, self.R_pb
        for k in range(8):
            bk = 0 if k < 4 else 3
            P.v(lambda e: e.tensor_scalar(tmpd, self.identF, cols_fn(k), None, ALU.mult), rd=[self.R_const, self.R_mod, self.R_sv], wr=[R_tmpd])
            P.t(lambda e: e.matmul(pb[bk][:, (k % 4) * 128:(k % 4 + 1) * 128], self.onesF, tmpd, start=True, stop=True),
                rd=[R_tmpd, self.R_const], xb=[Rpb[bk]])
        P.a(lambda e: e.activation(dst[:, 0:512], pb[0][:, :], AF.Copy), wrp=[R_dst], xb=[Rpb[0]])
        P.a(lambda e: e.activation(dst[:, 512:1024], pb[3][:, :], AF.Copy), wrp=[R_dst], xb=[Rpb[3]])

    def outproj(self, l, last):
        P, A, pb, Rpb = self.P, self.ar, self.pb, self.R_pb
        mark = A.off
        wout = A.alloc([8, D], BF16); R_wo = Reg()
        wstg = [A.alloc([8, 256]) for _ in range(2)]; R_ws = [Reg(), Reg()]
        wv = self.w_out[l].rearrange("(k p) n -> p k n", p=128)
        for pc in range(4):
            P.dma(wstg[pc % 2], wv[:, :, pc * 256:(pc + 1) * 256], wr=[R_ws[pc % 2]])
            P.g(lambda e: e.tensor_copy(wout[:, :, pc * 256:(pc + 1) * 256], wstg[pc % 2]), rd=[R_ws[pc % 2]], wrp=[R_wo])
        tmpd = A.alloc([128]); R_tmpd = Reg()
        gtbc = [A.alloc([D]) for _ in range(2)]; R_gt = [Reg(), Reg()]
        for seg in ((0,) if last else (0, 1)):
            self.bcast_rows(gtbc[seg], lambda k: self.mod[:, l, 16 + k, seg:seg + 1], R_gt[seg], tmpd, R_tmpd)
        if last:
            fnwbc = A.alloc([D]); R_fn = Reg()
            of = _SV["fnw"][0]
            self.bcast_rows(fnwbc, lambda k: self.sv[:, of + k:of + k + 1], R_fn, tmpd, R_tmpd)
        xts = [A.alloc([D]) for _ in range(2)]; R_x = [Reg(), Reg()]
        xns = [A.alloc([D]) for _ in range(2)]; R_xn = [Reg(), Reg()]
        tmps = [A.alloc([D]) for _ in range(2)]; R_tp = [Reg(), Reg()]
        mts = [A.alloc([8, 128], BF16) for _ in range(2)]; R_mt = [Reg(), Reg()]
        bufs = [(A.alloc([D], BF16), A.alloc([1]), A.alloc([1]), A.alloc([D]), Reg()) for _ in range(2)]
        tiles = list(range(2, NT)) if last else list(range(NT))
        def stage_mm(i, tt):
            s2 = i % 2
            xt, mt = xts[s2], mts[s2]
            blk = tt // 2
            P.dma(mt, self.MIX[:, :, tt * 128:(tt + 1) * 128].rearrange("f p t -> p f t"),
                  rd=[self.R_MIX[f][blk] for f in range(8)], wr=[R_mt[s2]])
            if l == 0:
                src = self.ctx_in[tt * 128:(tt + 1) * 128, :] if tt < 2 else self.x_in[(tt - 2) * 128:(tt - 1) * 128, :]
                P.dma(xt, src, wr=[R_x[s2]])
            else:
                P.dma(xt, self.XRES[tt * 128:(tt + 1) * 128, :], rd=[self.R_XRES[tt]], wr=[R_x[s2]])
            for half in range(2):
                bk = 4 + half + 2 * s2
                hs = slice(half * 512, (half + 1) * 512)
                for f in range(8):
                    P.t(lambda e: e.matmul(pb[bk][:, :], mt[:, f, :], wout[:, f, hs], start=(f == 0), stop=(f == 7)),
                        rd=[R_mt[s2], R_wo], xb=[Rpb[bk]])

        def stage_fin(i, tt):
            s2 = i % 2
            seg = 1 if tt < 2 else 0
            xt, xn, tp = xts[s2], xns[s2], tmps[s2]
            for half in range(2):
                bk = 4 + half + 2 * s2
                hs = slice(half * 512, (half + 1) * 512)
                P.v(lambda e: e.tensor_tensor(tp[:, hs], pb[bk][:, :], gtbc[seg][:, hs], ALU.mult), rd=[R_gt[seg]], wrp=[R_tp[s2]], xb=[Rpb[bk]])
                P.g(lambda e: e.tensor_tensor(xn[:, hs], tp[:, hs], xt[:, hs], ALU.add), rd=[R_tp[s2], R_x[s2]], wrp=[R_xn[s2]])
            if not last:
                P.dma(self.XRES[tt * 128:(tt + 1) * 128, :], xn, rd=[R_xn[s2]], wr=[self.R_XRES[tt]])
                self.tile_to_hx(l + 1, tt, xn, R_xn[s2], bufs[s2])
            else:
                junk, ssq, rstd, on, R_t = bufs[s2]
                P.a(lambda e: e.activation(junk, xn, AF.Square, accum_out=ssq), rd=[R_xn[s2]], wr=[R_t])
                P.a(lambda e: e.activation(ssq, ssq, AF.Sqrt, scale=1.0 / D, bias=EPS), wr=[R_t])
                P.v(lambda e: e.reciprocal(rstd, ssq), wr=[R_t])
                P.v(lambda e: e.scalar_tensor_tensor(on, xn, rstd, fnwbc, ALU.mult, ALU.mult), rd=[R_xn[s2], R_fn], wr=[R_t])
                P.dma(self.out[(tt - 2) * 128:(tt - 1) * 128, :], on, rd=[R_t], q="act")

        stage_mm(0, tiles[0])
        for i, tt in enumerate(tiles):
            if i + 1 < len(tiles):
                stage_mm(i + 1, tiles[i + 1])
            stage_fin(i, tt)
        self.barrier()
        A.off = mark

    def build_all(self):
        self.adaln()
        for l in range(self.nlayers):
            last = l == self.nlayers - 1
            segs = ("x",) if (last and self.nlayers > 1) else ("c", "x")
            for s in segs:
                self.hy_filter(l, s, alias_hx=True)
        self.phase_b_from_dram(0)
        for l in range(self.nlayers):
            last = l == self.nlayers - 1
            ctx_out = not (last and self.nlayers > 1)
            self.hgrn2(l, ctx_out)
            self.hyena(l, ["c", "x"] if ctx_out else ["x"])
            self.outproj(l, last)
        self.P.finish()


_CONSTS = None


def kernel(**inputs):
    global _CONSTS
    inp = {k: np.asarray(v) for k, v in inputs.items()}
    if _CONSTS is None:
        _CONSTS = _host_consts()
    B = Builder(nlayers=2)
    B.build_all()
    in_maps = []
    for b in range(8):
        m = {"x": np.ascontiguousarray(inp["x"][b], dtype=np.float32), "ctx": np.ascontiguousarray(inp["ctx"][b], dtype=np.float32),
             "smallv": _pack_small(inp, b)}
        for k in ("w_ada", "w_in", "w_out", "hy_w1", "hy_w2", "hy_w3", "hy_w4"):
            m[k] = np.ascontiguousarray(inp[k], dtype=np.float32)
        m.update(_CONSTS)
        in_maps.append(m)
    res = run_bass_kernel_spmd(B.nc, in_maps, core_ids=list(range(8)))
    return np.stack([np.asarray(r["out"], dtype=np.float32) for r in res.results], axis=0)
```

```python
import numpy as np
import ml_dtypes
import concourse.bass as bass
import concourse.mybir as mybir
from concourse.bass_utils import run_bass_kernel_spmd

F32 = mybir.dt.float32
BF16 = mybir.dt.bfloat16
AF = mybir.ActivationFunctionType
ALU = mybir.AluOpType

NDS = 24
HG_W = 4


class Tok:
    __slots__ = ("eng", "idx")

    def __init__(self, eng, idx):
        self.eng = eng
        self.idx = idx


class Reg:
    __slots__ = ("name", "w", "r", "pw", "pr", "full")

    def __init__(self, name=""):
        self.name = name
        self.w = {}
        self.r = {}
        self.pw = {}
        self.pr = {}
        self.full = {}


def _merge(d, tok):
    o = d.get(tok.eng)
    if o is None or o.idx < tok.idx:
        d[tok.eng] = tok


class _Rec:
    __slots__ = ("fn", "waits", "flag", "dsem", "dval")

    def __init__(self, fn, waits):
        self.fn = fn
        self.waits = waits
        self.flag = False
        self.dsem = None
        self.dval = 0


class _Capture:
    def __init__(self):
        self.call = None

    def __getattr__(self, name):
        def f(*args, **kw):
            self.call = (name, args, kw)
        return f


class Prog:
    ENGS = ("pe", "dve", "act", "pool", "sp")

    def __init__(self, nc):
        self.nc = nc
        self.q = {e: [] for e in self.ENGS}
        self.waited = {e: {} for e in self.ENGS}
        self.dma_n = 0
        self.dma_last = [None] * NDS
        self.dma_val = [0] * NDS
        self.n_sb = 0

    def sb(self, name, shape, dtype):
        return self.nc.alloc_sbuf_tensor(name, list(shape), dtype)

    def ps(self, name, shape, dtype=F32):
        return self.nc.alloc_psum_tensor(name, list(shape), dtype)

    def dram(self, name, shape, dtype, kind="Internal"):
        return self.nc.dram_tensor(name, list(shape), dtype, kind=kind).ap()

    def _deps(self, eng, rd, wr, wrp, extra, xb=()):
        deps = []
        for r in xb:
            deps.extend(t for e2, t in r.w.items() if e2 != eng)
        for r in rd:
            deps.extend(r.w.values())
        for r in wr:
            deps.extend(r.w.values())
            deps.extend(r.r.values())
        for r in wrp:
            if r.r:
                r.pw, r.pr = r.w, r.r
                r.w, r.r = {}, {}
                r.full = {}
            deps.extend(r.pw.values())
            deps.extend(r.pr.values())
            deps.extend(r.full.values())
        deps.extend(extra)
        return deps

    def _post(self, tok, rd, wr, wrp, xb=()):
        for r in xb:
            _merge(r.w, tok)
        for r in rd:
            _merge(r.r, tok)
        for r in wr:
            r.pw, r.pr = {}, {}
            r.w, r.r = {tok.eng: tok}, {}
            r.full = {tok.eng: tok}
        for r in wrp:
            _merge(r.w, tok)

    def _waits(self, eng, deps):
        waits = []
        wd = self.waited[eng]
        for d in deps:
            if d is None:
                continue
            if d.eng == eng and eng == "pe":
                continue
            if wd.get(d.eng, -1) >= d.idx:
                continue
            wd[d.eng] = d.idx
            if not d.eng.startswith("dma"):
                self.q[d.eng][d.idx].flag = True
            waits.append(d)
        return waits

    def op(self, eng, fn, rd=(), wr=(), wrp=(), deps=(), xb=()):
        cap = _Capture()
        fn(cap)
        name, args, kw = cap.call
        fn = lambda h, name=name, args=args, kw=kw: getattr(h, name)(*args, **kw)
        dl = self._deps(eng, rd, wr, wrp, deps, xb)
        waits = self._waits(eng, dl)
        q = self.q[eng]
        q.append(_Rec(fn, waits))
        tok = Tok(eng, len(q) - 1)
        self._post(tok, rd, wr, wrp, xb)
        return tok

    def t(self, fn, **kw):
        return self.op("pe", fn, **kw)

    def v(self, fn, **kw):
        return self.op("dve", fn, **kw)

    def a(self, fn, **kw):
        return self.op("act", fn, **kw)

    def g(self, fn, **kw):
        return self.op("pool", fn, **kw)

    def dma(self, out, in_, rd=(), wr=(), wrp=(), deps=(), q="sp"):
        k = self.dma_n % NDS
        self.dma_n += 1
        dl = self._deps(q, rd, wr, wrp, deps)
        if self.dma_last[k] is not None:
            dl.append(self.dma_last[k])
        waits = self._waits(q, dl)
        rec = _Rec(lambda e: e.dma_start(out=out, in_=in_), waits)
        self.dma_val[k] += 16
        rec.dsem = k
        rec.dval = self.dma_val[k]
        self.q[q].append(rec)
        tok = Tok("dma%d" % k, self.dma_val[k])
        self.dma_last[k] = tok
        self._post(tok, rd, wr, wrp)
        return tok

    def last_real(self, e):
        q = self.q[e]
        for i in range(len(q) - 1, -1, -1):
            if q[i].fn is not None and q[i].dsem is None:
                return Tok(e, i)
        return None

    def finish(self):
        nc = self.nc
        fin = [t for t in self.dma_last if t is not None]
        for e in ("pe", "dve", "act", "pool"):
            t = self.last_real(e)
            if t is not None:
                fin.append(t)
        waits = self._waits("sp", fin)
        self.q["sp"].append(_Rec(None, waits))
        cnt = self.cnt = {}
        for e in self.ENGS:
            c = 0
            arr = []
            for rec in self.q[e]:
                if rec.flag:
                    c += 1
                arr.append(c)
            cnt[e] = arr
        import contextlib
        with contextlib.ExitStack() as st:
            sems = {e: st.enter_context(nc.semaphore("s_" + e)) for e in self.ENGS}
            dsems = [st.enter_context(nc.semaphore("d%d" % i)) for i in range(NDS)]
            block = st.enter_context(nc.Block())

            def run(e, h):
                for rec in self.q[e]:
                    for w in rec.waits:
                        if w.eng.startswith("dma"):
                            h.wait_ge(dsems[int(w.eng[3:])], w.idx)
                        else:
                            h.wait_ge(sems[w.eng], cnt[w.eng][w.idx])
                    if rec.fn is None:
                        continue
                    ins = rec.fn(h)
                    if rec.dsem is not None:
                        ins.then_inc(dsems[rec.dsem], 16)
                    elif rec.flag:
                        ins.then_inc(sems[e], 1)

            @block.tensor
            def _(h):
                run("pe", h)

            @block.vector
            def _(h):
                run("dve", h)

            @block.scalar
            def _(h):
                run("act", h)

            @block.gpsimd
            def _(h):
                run("pool", h)

            @block.sync
            def _(h):
                run("sp", h)


D = 1024
L = 4096
LC = 256
T = LC + L
NT = T // 128
BLK = 256
NB = T // BLK
EPS = 1e-6
NFFT = 8192
HY_MIN = float(np.log(1e-2) / 1.5)
HY_MAX = float(np.log(1e-2) / 0.3)
PI = float(np.pi)
bf16 = ml_dtypes.bfloat16

_SV = {}
_o = 0
for _n, _w in [("cvT", 16), ("normw", 16), ("fnw", 8), ("bada", 48), ("lbl", 16), ("gnw", 8),
               ("convw", 72), ("convb", 24), ("hyb", 8), ("hb1", 2), ("hb2", 2), ("hb3", 2), ("hfr", 2),
               ("delt", 4)]:
    _SV[_n] = (_o, _w)
    _o += _w
NSV = _o


def _pos_feat(Ln):
    t = np.linspace(0.0, 1.0, Ln, dtype=np.float32)[:, None]
    w = (2.0 * np.pi * np.arange(Ln, dtype=np.float32)[:, None] / Ln).astype(np.float32)
    f = np.linspace(1e-4, 15.0, 16, dtype=np.float32)[None, :]
    return np.concatenate([t, np.cos(f * w), -np.sin(f * w)], axis=-1).astype(np.float32)


def _host_consts():
    c = {}
    c["identF"] = np.eye(128, dtype=np.float32)
    c["identB"] = np.eye(128).astype(bf16)
    j = np.arange(64)[:, None]
    i = np.arange(64)[None, :]
    mk = np.stack([(j <= i), (j >= i)], axis=0).astype(np.float32)
    c["masks"] = np.concatenate([mk, mk], axis=1).transpose(1, 0, 2).copy()
    for Ln, nm in ((L, "x"), (LC, "c")):
        z = _pos_feat(Ln)
        zr = np.zeros_like(z)
        zr[1:] = z[Ln - np.arange(1, Ln)]
        c["ZZ" + nm] = np.concatenate([z.T, zr.T], axis=0).copy()
        tt = np.zeros((2, Ln), np.float32)
        tt[0] = z[:, 0]
        tt[1, 1:] = z[Ln - np.arange(1, Ln), 0]
        c["TT" + nm] = tt
    n1 = np.arange(64, dtype=np.float64)[:, None]
    FA = np.zeros((64, 64))
    k1r = np.arange(33, dtype=np.float64)[None, :]
    FA[:, 0:33] = np.cos(2 * np.pi * n1 * k1r / 64)
    k1i = np.arange(1, 32, dtype=np.float64)[None, :]
    FA[:, 33:64] = -np.sin(2 * np.pi * n1 * k1i / 64)
    c["FA2"] = np.concatenate([FA, FA], axis=0).astype(bf16)
    nn1 = np.arange(32, dtype=np.float64)[None, :]
    RA = np.zeros((64, 32))
    RA[0, :] = 1.0
    RA[32, :] = (-1.0) ** np.arange(32)
    kk = np.arange(1, 32, dtype=np.float64)[:, None]
    RA[1:32, :] = 2 * np.cos(2 * np.pi * nn1 * kk / 64)
    RA[33:64, :] = -2 * np.sin(2 * np.pi * nn1 * kk / 64)
    RA /= NFFT
    c["RA2"] = np.concatenate([RA, RA], axis=0).astype(bf16)
    n2 = np.arange(128, dtype=np.float64)[:, None]
    k2 = np.arange(128, dtype=np.float64)[None, :]
    TM = np.zeros((128, 33, 6, 128), np.float64)
    for k1 in range(33):
        th = 2 * np.pi * n2 * (k1 + 64 * k2) / NFFT
        Tc, Ts = np.cos(th), -np.sin(th)
        TM[:, k1, 0], TM[:, k1, 1], TM[:, k1, 2] = Tc, Ts, -Ts
        TM[:, k1, 3], TM[:, k1, 4], TM[:, k1, 5] = Tc.T, Ts.T, -Ts.T
    c["TM"] = TM.astype(bf16)
    return c


def _pack_small(inp, b):
    sv = np.zeros((128, NSV), np.float32)

    def put(name, arr):
        o, w = _SV[name]
        a = np.asarray(arr, np.float32).reshape(128, -1)
        assert a.shape[1] == w, (name, a.shape, w)
        sv[:, o:o + w] = a

    cv = np.stack([inp["c"][b], inp["c_ctx"]], axis=0)
    put("cvT", cv.reshape(2, 8, 128).transpose(2, 1, 0))
    put("normw", inp["norm_w"].reshape(2, 8, 128).transpose(2, 0, 1))
    put("fnw", inp["final_norm_w"].reshape(8, 128).T)
    put("bada", inp["b_ada"].reshape(2, 24, 128).transpose(2, 0, 1))
    put("lbl", inp["lb_logits"].reshape(2, 2, 4, 128).transpose(3, 0, 1, 2))
    put("gnw", inp["g_norm_w"].reshape(2, 4, 128).transpose(2, 0, 1))
    put("convw", inp["conv_w"].reshape(2, 3, 12, 128).transpose(3, 0, 1, 2))
    put("convb", inp["conv_b"].reshape(2, 12, 128).transpose(2, 0, 1))
    put("hyb", inp["hy_bias"].reshape(2, 4, 128).transpose(2, 0, 1))
    for nm, key in (("hb1", "hy_b1"), ("hb2", "hy_b2"), ("hb3", "hy_b3"), ("hfr", "hy_freq")):
        a = inp[key]
        put(nm, np.concatenate([a.T, a.T], axis=0))
    delt = -np.abs(np.linspace(HY_MIN, HY_MAX, 512, dtype=np.float32))
    put("delt", delt.reshape(4, 128).T)
    return sv


class Arena:
    def __init__(self, P, words):
        self.t = P.sb("arena", [128, words], F32)
        self.words = words
        self.off = 0

    def alloc(self, shape, dtype=F32):
        n = 1
        for s in shape:
            n *= s
        w = n if dtype == F32 else (n + 1) // 2
        w = (w + 7) // 8 * 8
        assert self.off + w <= self.words, ("arena overflow", self.off, w, self.words)
        ap = self.t[:, self.off:self.off + w]
        self.off += w
        if dtype != F32:
            ap = ap.bitcast(dtype)
        ap = ap[:, 0:n]
        if len(shape) == 2:
            ap = ap.rearrange("p (a b) -> p a b", a=shape[0])
        elif len(shape) == 3:
            ap = ap.rearrange("p (a b c) -> p a b c", a=shape[0], b=shape[1])
        return ap


class Builder:
    def __init__(self, nlayers=2, dbg=()):
        self.nlayers = nlayers
        self.dbg = set(dbg)
        nc = self.nc = bass.Bass("TRN2", target_bir_lowering=False)
        P = self.P = Prog(nc)
        self.dbg_outs = {}
        inp = lambda name, shape, dt=F32: nc.dram_tensor(name, list(shape), dt, kind="ExternalInput").ap()
        self.x_in = inp("x", [L, D])
        self.ctx_in = inp("ctx", [LC, D])
        self.smallv_in = inp("smallv", [128, NSV])
        self.w_ada = inp("w_ada", [2, D, 3 * D])
        self.w_in = inp("w_in", [2, D, 4608])
        self.w_out = inp("w_out", [2, D, D])
        self.hy_w1 = inp("hy_w1", [2, 33, 64])
        self.hy_w2 = inp("hy_w2", [2, 64, 64])
        self.hy_w3 = inp("hy_w3", [2, 64, 64])
        self.hy_w4 = inp("hy_w4", [2, 64, 1024])
        self.c_identF = inp("identF", [128, 128])
        self.c_identB = inp("identB", [128, 128], BF16)
        self.c_masks = inp("masks", [128, 2, 64])
        self.c_ZZ = {"x": inp("ZZx", [66, L]), "c": inp("ZZc", [66, LC])}
        self.c_TT = {"x": inp("TTx", [2, L]), "c": inp("TTc", [2, LC])}
        self.c_FA2 = inp("FA2", [128, 64], BF16)
        self.c_RA2 = inp("RA2", [128, 32], BF16)
        self.c_TM = inp("TM", [128, 33, 6, 128], BF16)
        self.out = nc.dram_tensor("out", [L, D], F32, kind="ExternalOutput").ap()
        self.XRES = P.dram("xres", [T, D], F32)
        self.MIX = P.dram("mix", [8, 128, T], BF16)
        self.VD = P.dram("vd", [512, L], BF16)
        self.VDC = P.dram("vdc", [512, LC], BF16)
        self.KD = P.dram("kd", [512, NFFT], BF16)
        self.KF = {}
        for l in range(nlayers):
            segs = ("c", "x") if l < nlayers - 1 or nlayers == 1 else ("x",)
            for s in segs:
                self.KF[(l, s)] = P.dram("kf%d%s" % (l, s), [33, 2, 128, 512], F32)
        self.R_XRES = [Reg() for _ in range(NT)]
        self.R_MIX = [[Reg() for _ in range(NB)] for _ in range(8)]
        self.R_VD = [Reg() for _ in range(4)]
        self.R_KD = Reg()
        self.R_KD4 = [Reg() for _ in range(4)]
        self.R_KF = {k: [Reg() for _ in range(4)] for k in self.KF}
        self.pb = [P.ps("pb%d" % i, [128, 512], F32) for i in range(8)]
        self.R_pb = [Reg() for _ in range(8)]
        self._tt_cnt = 0
        self.ar = Arena(P, 212000 // 4)
        A = self.ar
        self.sv = A.alloc([NSV]); self.R_sv = Reg()
        self.identF = A.alloc([128]); self.identB = A.alloc([128], BF16)
        self.masks = A.alloc([2, 64]); self.onesF = A.alloc([128]); self.onesB = A.alloc([128], BF16)
        self.FA2 = A.alloc([64], BF16); self.RA2 = A.alloc([32], BF16)
        self.R_const = Reg()
        self.mod = A.alloc([2, 24, 2]); self.R_mod = Reg()
        self.weff = A.alloc([2, 8, 2]); self.lbv = A.alloc([2, 8]); self.omlb = A.alloc([2, 8])
        self.kwin = A.alloc([4, 2 * LC - 1]); self.R_kwin = [Reg() for _ in range(4)]
        self.hx_off = A.off
        self.hxT = A.alloc([8, T], BF16)
        self.R_hx = [Reg() for _ in range(NT)]
        self.persist_mark = A.off
        P.dma(self.sv, self.smallv_in, wr=[self.R_sv])
        P.dma(self.identF, self.c_identF, wrp=[self.R_const])
        P.dma(self.identB, self.c_identB, wrp=[self.R_const])
        P.dma(self.masks, self.c_masks, wrp=[self.R_const])
        P.dma(self.FA2, self.c_FA2, wrp=[self.R_const])
        P.dma(self.RA2, self.c_RA2, wrp=[self.R_const])
        P.g(lambda e: e.memset(self.onesF, 1.0), wrp=[self.R_const])
        P.g(lambda e: e.memset(self.onesB, 1.0), wrp=[self.R_const])

    def svv(self, name, *idx_shape):
        o, w = _SV[name]
        return self.sv[:, o:o + w]

    def barrier(self, name=None):
        P = self.P
        if not hasattr(self, "marks"):
            self.marks = []
        self.marks.append((name or "b%d" % len(self.marks), {e: len(P.q[e]) - 1 for e in ("pe", "dve", "act", "pool")}))
        toks = [t for t in (P.last_real(e) for e in ("pe", "dve", "act", "pool")) if t is not None]
        toks += [t for t in P.dma_last if t is not None]
        for e in ("pe", "dve", "act", "pool", "sp"):
            w = P._waits(e, [t for t in toks if t.eng != e])
            if w:
                P.q[e].append(_Rec(None, w))

    def dump(self, name, ap, rd=(), shape=None, dt=F32):
        if name not in self.dbg:
            return
        if len(ap.shape) == 3:
            ap = ap.rearrange("p a b -> p (a b)")
        elif len(ap.shape) == 4:
            ap = ap.rearrange("p a b c -> p (a b c)")
        shp = list(ap.shape)
        o = self.nc.dram_tensor("dbg_" + name, shp, dt, kind="ExternalOutput").ap()
        self.dbg_outs[name] = "dbg_" + name
        self.P.dma(o, ap, rd=list(rd))

    def adaln(self, stage=9):
        P, A = self.P, self.ar
        mark = A.off
        scv = A.alloc([8, 2]); R_scv = Reg()
        o, _ = _SV["cvT"]
        cv = self.sv[:, o:o + 16].rearrange("p (k r) -> p k r", k=8)
        P.a(lambda e: e.activation(scv, cv, AF.Silu), rd=[self.R_sv], wr=[R_scv])
        wst = [A.alloc([8, 512]) for _ in range(2)]
        R_w = [Reg(), Reg()]
        ob, _ = _SV["bada"]
        bada = self.sv[:, ob:ob + 48].rearrange("p (l j) -> p l j", l=2)
        R_ps = Reg()
        n = 0
        for l in range(self.nlayers):
            wv = self.w_ada[l].rearrange("(k p) n -> p k n", p=128)
            for pc in range(6):
                wb, rw = wst[n % 2], R_w[n % 2]
                n += 1
                P.dma(wb, wv[:, :, pc * 512:(pc + 1) * 512], wr=[rw])
                for jj in range(4):
                    j = pc * 4 + jj
                    for k in range(8):
                        P.t(lambda e, wb=wb, jj=jj, k=k, j=j: e.matmul(self.pb[0][:, 2 * j:2 * j + 2], wb[:, k, jj * 128:(jj + 1) * 128],
                                                                      scv[:, k, :], start=(k == 0), stop=(k == 7)),
                            rd=[rw, R_scv], xb=[self.R_pb[0]])
            if stage < 3:
                continue
            P.v(lambda e, l=l: e.tensor_tensor(self.mod[:, l], self.pb[0][:, 0:48].rearrange("p (j r) -> p j r", j=24),
                                               bada[:, l, :].unsqueeze(2).to_broadcast([128, 24, 2]), ALU.add),
                rd=[self.R_sv], wrp=[self.R_mod], xb=[self.R_pb[0]])
            on, _ = _SV["normw"]
            nw = self.sv[:, on:on + 16].rearrange("p (l k) -> p l k", l=2)
            P.v(lambda e, l=l: e.scalar_tensor_tensor(self.weff[:, l], self.mod[:, l, 8:16, :], 1.0,
                                                      nw[:, l, :].unsqueeze(2).to_broadcast([128, 8, 2]), ALU.add, ALU.mult),
                rd=[self.R_mod, self.R_sv], wrp=[self.R_mod])
        if stage < 4:
            self.barrier(); A.off = mark; return
        ol, _ = _SV["lbl"]
        lbl = self.sv[:, ol:ol + 16].rearrange("p (l x) -> p l x", l=2)
        P.g(lambda e: e.memset(self.lbv[:, 0, :], 0.0), wrp=[self.R_mod])
        P.g(lambda e: e.memset(self.omlb[:, 0, :], 1.0), wrp=[self.R_mod])
        if self.nlayers > 1:
            dl = A.alloc([8]); R_dl = Reg()
            P.v(lambda e: e.tensor_tensor(dl, lbl[:, 1, :], lbl[:, 0, :], ALU.subtract), rd=[self.R_sv], wr=[R_dl])
            P.a(lambda e: e.activation(self.lbv[:, 1, :], dl, AF.Sigmoid), rd=[R_dl], wrp=[self.R_mod])
            P.a(lambda e: e.activation(self.omlb[:, 1, :], dl, AF.Sigmoid, scale=-1.0), rd=[R_dl], wrp=[self.R_mod])
        self.dump("mod", self.mod, rd=[self.R_mod])
        self.barrier()
        A.off = mark

    def tile_to_hx(self, l, tt, xt, R_xt, bufs):
        P = self.P
        seg = 1 if tt < 2 else 0
        junk, ssq, rstd, xn, R_t = bufs
        P.a(lambda e: e.activation(junk, xt, AF.Square, accum_out=ssq), rd=[R_xt], wr=[R_t])
        P.a(lambda e: e.activation(ssq, ssq, AF.Sqrt, scale=1.0 / D, bias=EPS), wr=[R_t])
        P.v(lambda e: e.reciprocal(rstd, ssq), wr=[R_t])
        xnb = junk
        P.v(lambda e: e.tensor_scalar(xnb, xt, rstd, None, ALU.mult), rd=[R_xt], wr=[R_t])
        for half in range(2):
            bi_ = 1 + half
            bank, R_b = self.pb[bi_], self.R_pb[bi_]
            for kk in range(4):
                k = half * 4 + kk
                P.t(lambda e, k=k, kk=kk, bank=bank: e.transpose(bank[:, 0:256].bitcast(BF16)[:, kk * 128:(kk + 1) * 128], xnb[:, k * 128:(k + 1) * 128], self.identB),
                    rd=[R_t, self.R_const], xb=[R_b])
            for kk in range(4):
                k = half * 4 + kk
                dst = self.hxT[:, k, tt * 128:(tt + 1) * 128]
                src = bank[:, 0:256].bitcast(BF16)[:, kk * 128:(kk + 1) * 128]
                sc = self.weff[:, l, k, seg:seg + 1]
                bi = self.mod[:, l, k, seg:seg + 1]
                if half == 0:
                    P.a(lambda e, dst=dst, src=src, sc=sc, bi=bi: e.activation(dst, src, AF.Identity, bias=bi, scale=sc),
                        rd=[self.R_mod], wrp=[self.R_hx[tt]], xb=[R_b])
                else:
                    P.v(lambda e, dst=dst, src=src, sc=sc, bi=bi: e.tensor_scalar(dst, src, sc, bi, ALU.mult, ALU.add),
                        rd=[self.R_mod], wrp=[self.R_hx[tt]], xb=[R_b])

    def phase_b_from_dram(self, l, ntiles=NT, stage=9):
        P, A = self.P, self.ar
        mark = A.off
        xts = [A.alloc([D]) for _ in range(2)]
        R_x = [Reg(), Reg()]
        bufs = []
        for i in range(2):
            bufs.append((A.alloc([D], BF16), A.alloc([1]), A.alloc([1]), A.alloc([D]), Reg()))
        for tt in range(ntiles):
            src = self.ctx_in[tt * 128:(tt + 1) * 128, :] if tt < 2 else self.x_in[(tt - 2) * 128:(tt - 1) * 128, :]
            P.dma(xts[tt % 2], src, wr=[R_x[tt % 2]])
            self.tile_to_hx(l, tt, xts[tt % 2], R_x[tt % 2], bufs[tt % 2])
        self.barrier()
        A.off = mark

    def load_w_slice(self, l, col0, dst_bf, R_dst, stg, R_stg, eng="dve"):
        P = self.P
        wv = self.w_in[l].rearrange("(k p) n -> p k n", p=128)
        P.dma(stg, wv[:, :, col0:col0 + 128], wr=[R_stg])
        fn = lambda e: e.tensor_copy(dst_bf, stg)
        P.op(eng, fn, rd=[R_stg], wr=[R_dst])

    def proj(self, bank_ap, R_bank, w_bf, R_w, c0, n):
        P = self.P
        t0, t1 = c0 // 128, (c0 + n - 1) // 128
        rds = [R_w] + [self.R_hx[t] for t in range(t0, t1 + 1)]
        for k in range(8):
            P.t(lambda e, k=k: e.matmul(bank_ap, w_bf[:, k, :], self.hxT[:, k, c0:c0 + n], start=(k == 0), stop=(k == 7)),
                rd=rds, xb=[R_bank])

    def hgrn2(self, l, need_ctx_out, heads=(0, 1, 2, 3)):
        P, A = self.P, self.ar
        mark = A.off
        pb, Rpb = self.pb, self.R_pb
        stg = [A.alloc([8, 128]) for _ in range(2)]; R_stg = [Reg(), Reg()]
        wbf = [A.alloc([8, 128], BF16) for _ in range(5)]; R_w = [Reg() for _ in range(5)]
        vtok = A.alloc([NT, 128], BF16); R_v = Reg()
        o_f = A.alloc([T]); R_of = [Reg() for _ in range(NB)]
        qsb = A.alloc([T], BF16); R_qsb = [Reg() for _ in range(NB)]
        S = [A.alloc([128]) for _ in range(3)]; R_S = [Reg(), Reg(), Reg()]
        U = [A.alloc([128], BF16) for _ in range(4)]; R_U = [Reg() for _ in range(4)]
        sct = [A.alloc([64], BF16) for _ in range(4)]; R_sct = [Reg() for _ in range(4)]
        scm = [A.alloc([64], BF16) for _ in range(4)]; R_scm = [Reg() for _ in range(4)]
        sets = []
        for _ in range(4):
            d_ = {}
            for nm in ("kk", "g", "p", "pe", "qs", "E", "ex", "exn", "exh", "sgA"):
                d_[nm] = A.alloc([BLK])
            for nm in ("Qt", "Kh", "res", "KtP0", "KtP1", "KtZ0", "KtZ1"):
                d_[nm] = A.alloc([BLK], BF16)
            d_["Khz"] = [A.alloc([2, 128], BF16) for _ in range(2)]
            for nm in ("L1", "sgq"):
                d_[nm] = A.alloc([BLK])
            d_["ad"] = A.alloc([4, 2]); d_["t4"] = A.alloc([4])
            d_["R"] = {nm: Reg() for nm in ("kk", "g", "p", "pe", "qs", "e", "qt", "kt", "ktz", "kh", "khtok", "ad", "sga", "o")}
            sets.append(d_)
        og, _ = _SV["gnw"]
        ones64 = self.onesF[:, 0:64]
        nstg = 0
        rmask = A.alloc([BLK]); R_rmask = Reg()
        P.g(lambda e: e.memset(rmask, 1.0), wr=[R_rmask])
        for c_ in range(4):
            P.g(lambda e: e.memset(rmask[:, c_ * 64:c_ * 64 + 1], 0.0), wr=[R_rmask])
        for st_ in sets:
            for nm in ("KtP0", "KtP1"):
                P.g(lambda e: e.memset(st_[nm], 0.0), wr=[st_["R"]["kt"]])
            for kz in st_["Khz"]:
                P.g(lambda e: e.memset(kz, 0.0), wr=[st_["R"]["khtok"]])
        for ci in range(4):
            P.g(lambda e: e.memset(scm[ci], 0.0), wr=[R_scm[ci]])
        for h in heads:
            for si in range(5):
                self.load_w_slice(l, si * 512 + h * 128, wbf[si], R_w[si], stg[nstg % 2], R_stg[nstg % 2])
                nstg += 1
            def vtok_gen():
                for g4 in range((NT + 3) // 4):
                    bk = 6 + g4 % 2
                    tiles = list(range(g4 * 4, min(NT, g4 * 4 + 4)))
                    for j, tt in enumerate(tiles):
                        for k in range(8):
                            P.t(lambda e: e.matmul(pb[bk][:, j * 128:(j + 1) * 128], self.hxT[:, k, tt * 128:(tt + 1) * 128],
                                                   wbf[3][:, k, :], start=(k == 0), stop=(k == 7)),
                                rd=[R_w[3], self.R_hx[tt]], xb=[Rpb[bk]])
                        yield
                    n = len(tiles) * 128
                    dst = vtok[:, tiles[0]:tiles[0] + len(tiles), :]
                    src = pb[bk][:, 0:n].rearrange("p (a b) -> p a b", b=128)
                    if g4 % 2 == 0:
                        P.a(lambda e: e.activation(dst, src, AF.Copy), wrp=[R_v], xb=[Rpb[bk]])
                    else:
                        P.v(lambda e: e.tensor_copy(dst, src), wrp=[R_v], xb=[Rpb[bk]])
                    yield
            vtg = vtok_gen()
            for d in (0, 1):
                oml = self.omlb[:, l, d * 4 + h:d * 4 + h + 1]
                P.g(lambda e: e.memset(S[0], 0.0), wr=[R_S[0]])
                for st_ in sets:
                    for nm in ("KtZ0", "KtZ1"):
                        P.g(lambda e: e.memset(st_[nm], 0.0), wr=[st_["R"]["ktz"]])
                order = list(range(NB)) if d == 0 else [0] + list(range(NB - 1, 0, -1))
                gcs = [0]

                def prep0(bi, blk):
                    bp = bi % 4
                    need_out = blk > 0 or need_ctx_out
                    c0 = blk * BLK
                    self.proj(pb[bp][:, 0:BLK], Rpb[bp], wbf[1 + d], R_w[1 + d], c0, BLK)
                    if need_out and d == 0:
                        self.proj(pb[bp][:, BLK:2 * BLK], Rpb[bp], wbf[0], R_w[0], c0, BLK)

                def prep1(bi, blk):
                    st = sets[bi % 4]; R = st["R"]
                    bp = bi % 4
                    need_out = blk > 0 or need_ctx_out
                    c0 = blk * BLK
                    kk, g, p, pe, qs, E, ex, exn, exh = (st[n_] for n_ in ("kk", "g", "p", "pe", "qs", "E", "ex", "exn", "exh"))
                    Qt, Kh, Khz, ad, t4, L1, sgq = st["Qt"], st["Kh"], st["Khz"], st["ad"], st["t4"], st["L1"], st["sgq"]
                    p3 = p.rearrange("p (c t) -> p c t", c=4); g3 = g.rearrange("p (c t) -> p c t", c=4)
                    pe3 = pe.rearrange("p (c t) -> p c t", c=4)
                    P.a(lambda e: e.activation(kk, pb[bp][:, 0:BLK], AF.Exp), wr=[R["kk"]], xb=[Rpb[bp]])
                    yield
                    P.a(lambda e: e.activation(L1, kk, AF.Ln, bias=1.0), rd=[R["kk"]], wr=[R["g"]])
                    yield
                    P.a(lambda e: e.activation(kk, L1, AF.Exp, scale=-1.0), rd=[R["g"]], wr=[R["kk"]])
                    yield
                    if need_out and d == 0:
                        P.a(lambda e: e.activation(sgq, pb[bp][:, BLK:2 * BLK], AF.Exp, scale=-1.0), wr=[R["qs"]], xb=[Rpb[bp]])
                        yield
                        P.a(lambda e: e.activation(sgq, sgq, AF.Ln, bias=1.0), wr=[R["qs"]])
                        yield
                        P.a(lambda e: e.activation(sgq, sgq, AF.Exp, scale=-1.0), wr=[R["qs"]])
                        yield
                        P.v(lambda e: e.tensor_tensor(qsb[:, c0:c0 + BLK], pb[bp][:, BLK:2 * BLK], sgq, ALU.mult), rd=[R["qs"]], wr=[R_qsb[blk]], xb=[Rpb[bp]])
                        yield
                    P.v(lambda e: e.tensor_scalar(kk, kk, oml, None, ALU.mult), rd=[self.R_mod], wr=[R["kk"]])
                    yield
                    P.a(lambda e: e.activation(g, kk, AF.Ln, scale=-1.0, bias=1.0), rd=[R["kk"]], wr=[R["g"]])
                    yield
                    P.v(lambda e: e.tensor_tensor_scan(p, rmask, g, 0.0, ALU.mult, ALU.add), rd=[R_rmask, R["g"]], wr=[R["p"]])
                    yield
                    if d == 0:
                        base3 = p3
                        mid = 31
                        P.v(lambda e: e.tensor_tensor(exh.rearrange("p (c t) -> p c t", c=4), p3[:, :, 63:64].to_broadcast([128, 4, 64]), p3, ALU.subtract),
                            rd=[R["p"]], wr=[R["kh"]])
                        yield
                        P.a(lambda e: e.activation(exh, exh, AF.Exp), wr=[R["kh"]])
                        yield
                        P.a(lambda e: e.activation(ad, p3[:, :, 31::32], AF.Exp), rd=[R["p"]], wr=[R["ad"]])
                        yield
                    else:
                        base3 = pe3
                        mid = 32
                        P.g(lambda e: e.tensor_tensor(pe, p, g, ALU.subtract), rd=[R["g"], R["p"]], wr=[R["pe"]])
                        yield
                        P.a(lambda e: e.activation(exh, pe, AF.Exp), rd=[R["pe"]], wr=[R["kh"]])
                        yield
                        P.v(lambda e: e.tensor_tensor(t4, p3[:, :, 63], pe3[:, :, 32], ALU.subtract), rd=[R["p"], R["pe"]], wr=[R["ad"]])
                        yield
                        P.a(lambda e: e.activation(ad[:, :, 0], t4, AF.Exp), wr=[R["ad"]])
                        yield
                        P.a(lambda e: e.activation(ad[:, :, 1], p3[:, :, 63], AF.Exp), rd=[R["p"]], wr=[R["ad"]])
                        yield
                    P.g(lambda e: e.tensor_tensor(Kh, kk, exh, ALU.mult), rd=[R["kk"]], wr=[R["kh"]])
                    yield
                    if need_out:
                        base = p if d == 0 else pe
                        rbase = R["p"] if d == 0 else R["pe"]
                        P.v(lambda e: e.tensor_tensor(E.rearrange("p (c t) -> p c t", c=4), base3, base3[:, :, mid:mid + 1].to_broadcast([128, 4, 64]), ALU.subtract),
                            rd=[rbase], wr=[R["e"]])
                        yield
                        P.a(lambda e: e.activation(ex, E, AF.Exp), wr=[R["e"]])
                        yield
                        P.a(lambda e: e.activation(exn, E, AF.Exp, scale=-1.0), wr=[R["e"]])
                        yield
                        eq, ek = (ex, exn) if d == 0 else (exn, ex)
                        P.g(lambda e: e.tensor_tensor(Qt, qsb[:, c0:c0 + BLK], eq, ALU.mult), rd=[R_qsb[blk], R["e"]], wr=[R["qt"]])
                        yield
                        hs = slice(0, 32) if d == 0 else slice(32, 64)
                        v5 = lambda a_: a_.rearrange("p (a b t) -> p a b t", a=2, b=2)
                        for hh_ in range(2):
                            P.g(lambda e: e.tensor_tensor(v5(st["KtP%d" % hh_])[:, :, hh_, :], v5(kk)[:, :, hh_, :], v5(ek)[:, :, hh_, :], ALU.mult),
                                rd=[R["kk"], R["e"]], wrp=[R["kt"]])
                            yield
                            P.g(lambda e: e.tensor_tensor(v5(st["KtZ%d" % hh_])[:, :, hh_, hs], v5(kk)[:, :, hh_, hs], v5(ek)[:, :, hh_, hs], ALU.mult),
                                rd=[R["kk"], R["e"]], wrp=[R["ktz"]])
                            yield

                def prep2(bi, blk):
                    st = sets[bi % 4]; R = st["R"]
                    bp = bi % 4
                    need_out = blk > 0 or need_ctx_out
                    c0 = blk * BLK
                    Kh, Khz, sgq = st["Kh"], st["Khz"], st["sgq"]
                    khps = pb[bp][:, 0:128].bitcast(BF16).rearrange("p (a b) -> p a b", a=2)
                    for ti in range(2):
                        P.t(lambda e, ti=ti: e.transpose(khps[:, ti, :], Kh[:, ti * 128:(ti + 1) * 128], self.identB),
                            rd=[R["kh"], self.R_const], xb=[Rpb[bp]])
                        yield
                    P.v(lambda e: e.tensor_copy(Khz[0][0:64], khps[0:64]), wrp=[R["khtok"]], xb=[Rpb[bp]])
                    yield
                    P.v(lambda e: e.tensor_copy(Khz[1][64:128], khps[64:128]), wrp=[R["khtok"]], xb=[Rpb[bp]])
                    yield
                    if d == 1 and need_out:
                        self.proj(pb[bp][:, BLK:2 * BLK], Rpb[bp], wbf[4], R_w[4], c0, BLK)
                        yield
                        P.a(lambda e: e.activation(sgq, pb[bp][:, BLK:2 * BLK], AF.Exp, scale=-1.0), rd=[R["qs"]], wr=[R["sga"]], xb=[Rpb[bp]])
                        yield
                        P.a(lambda e: e.activation(sgq, sgq, AF.Ln, bias=1.0), wr=[R["sga"]])
                        yield
                        P.a(lambda e: e.activation(sgq, sgq, AF.Exp, scale=-1.0), wr=[R["sga"]])
                        yield
                        P.v(lambda e: e.tensor_tensor(st["sgA"], pb[bp][:, BLK:2 * BLK], sgq, ALU.mult), wr=[R["sga"]], xb=[Rpb[bp]])
                        yield

                def chain(bi, blk):
                    st = sets[bi % 4]; R = st["R"]
                    bp = bi % 2
                    need_out = blk > 0 or need_ctx_out
                    c0 = blk * BLK
                    Qt, Khz, ad = st["Qt"], st["Khz"], st["ad"]
                    ob = 6 + bp
                    cis = (0, 1, 2, 3) if d == 0 else (3, 2, 1, 0)
                    if need_out:
                        for ci in cis:
                            hh = ci % 2
                            r0, r1 = 64 * hh, 64 * hh + 64
                            cs = slice(ci * 64, ci * 64 + 64)
                            sb_ = 5
                            cA = slice(ci * 64, ci * 64 + 32)
                            cB = slice(ci * 64 + 32, ci * 64 + 64)
                            c1, o1, c2, o2 = (cB, slice(32, 64), cA, slice(0, 32)) if d == 0 else (cA, slice(0, 32), cB, slice(32, 64))
                            tcs = slice((ci // 2) * 128, (ci // 2) * 128 + 128)
                            P.t(lambda e: e.matmul(pb[sb_][:, o1], st["KtP%d" % hh][:, tcs], Qt[:, c1], start=True, stop=True),
                                rd=[R["kt"], R["qt"]], xb=[Rpb[sb_]])
                            yield
                            P.t(lambda e: e.matmul(pb[sb_][:, o2], st["KtZ%d" % hh][:, tcs], Qt[:, c2], start=True, stop=True),
                                rd=[R["ktz"], R["qt"]], xb=[Rpb[sb_]])
                            yield
                            P.v(lambda e: e.tensor_copy(sct[ci][r0:r1, :], pb[sb_][r0:r1, 0:64]), wr=[R_sct[ci]], xb=[Rpb[sb_]])
                            yield
                            P.g(lambda e: e.tensor_tensor(scm[ci][r0:r1, :], sct[ci][r0:r1, :], self.masks[r0:r1, d, :], ALU.mult),
                                rd=[R_sct[ci], self.R_const], wrp=[R_scm[ci]])
                            yield
                    for ci in cis:
                        gc = gcs[0]
                        gcs[0] += 1
                        cp = gc % 2
                        Sc, Sn = S[gc % 3], S[(gc + 1) % 3]
                        RSc, RSn = R_S[gc % 3], R_S[(gc + 1) % 3]
                        ti, hh = ci // 2, ci % 2
                        tile = blk * 2 + ti
                        r0, r1 = 64 * hh, 64 * hh + 64
                        dsb, Rdsb = (pb[4][:, 0:128], Rpb[4]) if cp == 0 else (pb[4][:, 128:256], Rpb[4])
                        P.t(lambda e: e.matmul(dsb, Khz[hh][:, ti, :], vtok[:, tile, :], start=True, stop=True),
                            rd=[R["khtok"], R_v], xb=[Rdsb])
                        yield
                        if need_out:
                            P.a(lambda e: e.activation(U[ci], Sc, AF.Copy, scale=ad[:, ci, 0:1]), rd=[RSc, R["ad"]], wr=[R_U[ci]])
                            yield
                        P.v(lambda e: e.scalar_tensor_tensor(Sn, Sc, ad[:, ci, 1:2], dsb, ALU.mult, ALU.add),
                            rd=[RSc, R["ad"]], wr=[RSn], xb=[Rdsb])
                        yield
                    if need_out:
                        for ci in cis:
                            ti, hh = ci // 2, ci % 2
                            tile = blk * 2 + ti
                            r0, r1 = 64 * hh, 64 * hh + 64
                            cs = slice(ci * 64, ci * 64 + 64)
                            P.t(lambda e: e.matmul(pb[ob][:, cs], vtok[:, tile, :], scm[ci][:, :], start=True, stop=False),
                                rd=[R_v, R_scm[ci]], xb=[Rpb[ob]])
                            yield
                            P.t(lambda e: e.matmul(pb[ob][:, cs], U[ci], Qt[:, cs], start=False, stop=True),
                                rd=[R_U[ci], R["qt"]], xb=[Rpb[ob]])
                            yield
                    if need_out:
                        cols = slice(c0, c0 + BLK)
                        if d == 0:
                            P.a(lambda e, cols=cols, ob=ob: e.activation(o_f[:, cols], pb[ob][:, 0:BLK], AF.Copy), wr=[R_of[blk]], xb=[Rpb[ob]])
                            yield
                        else:
                            osum, sq, rs, res = st["E"], st["ex"], st["exn"], st["res"]
                            P.v(lambda e, cols=cols, ob=ob: e.tensor_tensor(osum, o_f[:, cols], pb[ob][:, 0:BLK], ALU.add), rd=[R_of[blk]], wr=[R["e"]], xb=[Rpb[ob]])
                            yield
                            P.g(lambda e: e.tensor_tensor(res, osum, osum, ALU.mult), rd=[R["e"]], wr=[R["o"]])
                            yield
                            P.t(lambda e, ob=ob: e.matmul(pb[ob][:, BLK:2 * BLK], self.onesB, res, start=True, stop=True), rd=[R["o"], self.R_const], xb=[Rpb[ob]])
                            yield
                            P.a(lambda e, ob=ob: e.activation(rs, pb[ob][:, BLK:2 * BLK], AF.Ln, scale=1.0 / 128, bias=EPS), wr=[R["e"]], xb=[Rpb[ob]])
                            yield
                            P.a(lambda e: e.activation(rs, rs, AF.Exp, scale=-0.5), wr=[R["e"]])
                            yield
                            P.v(lambda e: e.tensor_tensor(osum, osum, rs, ALU.mult), wr=[R["e"]])
                            yield
                            gn = self.sv[:, og + l * 4 + h:og + l * 4 + h + 1]
                            P.v(lambda e, gn=gn: e.scalar_tensor_tensor(res, osum, gn, st["sgA"], ALU.mult, ALU.mult), rd=[R["e"], R["sga"], self.R_sv], wr=[R["o"]])
                            yield
                            P.dma(self.MIX[h, :, cols], res, rd=[R["o"]], wr=[self.R_MIX[h][blk]])
                            yield

                order_ = order

                def zipgen(gs):
                    gs = list(gs)
                    while gs:
                        for g_ in list(gs):
                            try:
                                next(g_)
                                yield
                            except StopIteration:
                                gs.remove(g_)

                npairs = (len(order_) + 1) // 2

                def pair_idx(j):
                    return [b_ for b_ in (2 * j, 2 * j + 1) if b_ < len(order_)]

                def stageA(j):
                    for b_ in pair_idx(j):
                        prep0(b_, order_[b_])
                    yield
                    yield from zipgen([prep1(b_, order_[b_]) for b_ in pair_idx(j)])

                def stageBC(j):
                    for b_ in pair_idx(j):
                        yield from prep2(b_, order_[b_])
                    for b_ in pair_idx(j):
                        yield from chain(b_, order_[b_])

                ga = stageA(0)
                if d == 0:
                    gl = [ga, vtg]
                    while gl:
                        for g_ in list(gl):
                            try:
                                next(g_)
                            except StopIteration:
                                gl.remove(g_)
                else:
                    for _ in ga:
                        pass
                def drain_weighted(gw):
                    gw = [[g_, w_] for g_, w_ in gw]
                    while gw:
                        for it in list(gw):
                            for _ in range(it[1]):
                                try:
                                    next(it[0])
                                except StopIteration:
                                    gw.remove(it)
                                    break

                for j in range(npairs):
                    gw = [(stageBC(j), HG_W)]
                    if j + 1 < npairs:
                        gw.insert(0, (stageA(j + 1), 1))
                    drain_weighted(gw)
        self.barrier()
        A.off = mark

    def load_tm(self, k0, nb, m0, buf, R_buf, q="sp"):
        self.P.dma(buf[:, 0:nb], self.c_TM[:, k0:k0 + nb, m0:m0 + 3, :], wr=[R_buf], q=q)

    def fft_stepA(self, Xs, R_xs, K, A_sb, R_a):
        P, pb, Rpb = self.P, self.pb, self.R_pb
        for c8 in range(16):
            bk = c8 % 2
            for j in range(8):
                cl = c8 * 8 + j
                q, cc = cl // 64, cl % 64
                P.t(lambda e: e.matmul(pb[bk][:, j * 64:(j + 1) * 64], Xs[64 * q:64 * q + K, cc, :], self.FA2[64 * q:64 * q + K, :], start=True, stop=True),
                    rd=[R_xs, self.R_const], xb=[Rpb[bk]])
            dst = A_sb[:, :, c8 * 8:(c8 + 1) * 8].rearrange("p k c -> p c k")
            src = pb[bk][:, 0:512].rearrange("p (c k) -> p c k", c=8)
            if bk == 0:
                P.a(lambda e: e.activation(dst, src, AF.Copy), wrp=[R_a], xb=[Rpb[bk]])
            else:
                P.v(lambda e: e.tensor_copy(dst, src), wrp=[R_a], xb=[Rpb[bk]])

    def fft_stepC(self, src_sb, R_src, inverse, tmb, R_tm, consume, tmq="sp"):
        P, pb, Rpb = self.P, self.pb, self.R_pb
        im_off = 33 if inverse else 32
        for bi, k0 in enumerate(range(0, 33, 4)):
            nb = min(4, 33 - k0)
            tm, rtm = tmb[bi % 2], R_tm[bi % 2]
            self.load_tm(k0, nb, 3 if inverse else 0, tm, rtm, q=tmq)
            br, bim = 2 + 2 * (bi % 2), 3 + 2 * (bi % 2)
            for j in range(nb):
                k1 = k0 + j
                real_only = k1 in (0, 32)
                re = src_sb[:, k1, :]
                im = None if (real_only and not inverse) else src_sb[:, im_off + k1, :]
                cs = slice(j * 128, (j + 1) * 128)
                Tc, Ts, mTs = tm[:, j, 0, :], tm[:, j, 1, :], tm[:, j, 2, :]
                if not inverse:
                    P.t(lambda e: e.matmul(pb[br][:, cs], Tc, re, start=True, stop=real_only), rd=[R_src, rtm], xb=[Rpb[br]])
                    if not real_only:
                        P.t(lambda e: e.matmul(pb[br][:, cs], mTs, im, start=False, stop=True), rd=[R_src, rtm], xb=[Rpb[br]])
                    P.t(lambda e: e.matmul(pb[bim][:, cs], Ts, re, start=True, stop=real_only), rd=[R_src, rtm], xb=[Rpb[bim]])
                    if not real_only:
                        P.t(lambda e: e.matmul(pb[bim][:, cs], Tc, im, start=False, stop=True), rd=[R_src, rtm], xb=[Rpb[bim]])
                else:
                    P.t(lambda e: e.matmul(pb[br][:, cs], Tc, re, start=True, stop=False), rd=[R_src, rtm], xb=[Rpb[br]])
                    P.t(lambda e: e.matmul(pb[br][:, cs], Ts, im, start=False, stop=True), rd=[R_src, rtm], xb=[Rpb[br]])
                    if not real_only:
                        P.t(lambda e: e.matmul(pb[bim][:, cs], mTs, re, start=True, stop=False), rd=[R_src, rtm], xb=[Rpb[bim]])
                        P.t(lambda e: e.matmul(pb[bim][:, cs], Tc, im, start=False, stop=True), rd=[R_src, rtm], xb=[Rpb[bim]])
            consume(bi, k0, nb, pb[br], pb[bim], Rpb[br], Rpb[bim])

    def load_xs(self, Xs, R_xs, src_rows, n1cnt, zero_first=False, rd=()):
        P = self.P
        if zero_first:
            P.g(lambda e: e.memset(Xs, 0.0), wr=[R_xs])
        for q in range(2):
            src = src_rows[64 * q:64 * q + 64, :].rearrange("c (n1 n2) -> n1 c n2", n2=128)
            if zero_first:
                P.dma(Xs[64 * q:64 * q + n1cnt, :, :], src, rd=list(rd), wrp=[R_xs])
            else:
                P.dma(Xs[64 * q:64 * q + n1cnt, :, :], src, rd=list(rd), wrp=[R_xs])

    def hy_filter(self, l, seg, alias_hx=False):
        P, A, pb, Rpb = self.P, self.ar, self.pb, self.R_pb
        mark = A.off
        if alias_hx:
            A.off = self.hx_off
        nbuf = 2 if alias_hx else 1
        Ln = L if seg == "x" else LC
        CH = min(512, Ln)
        nch = Ln // CH
        Ha = A.alloc([Ln]); R_Ha = Reg()
        mark2 = A.off
        zz = A.alloc([Ln]); R_zz = Reg()
        Hb = A.alloc([Ln]); R_Hb = Reg()
        W1 = A.alloc([128]); W2 = A.alloc([128]); W3 = A.alloc([128]); W4f = A.alloc([512]); W4b = A.alloc([512])
        R_W = Reg()
        frb = A.alloc([4]); R_frb = Reg()
        argb = [A.alloc([CH]) for _ in range(4)]; mb = [A.alloc([CH]) for _ in range(4)]; R_arg = [Reg() for _ in range(4)]
        for w_ in (W1, W2, W3, W4f, W4b):
            P.g(lambda e: e.memset(w_, 0.0), wr=[R_W])
        P.dma(W1[0:33, 0:64], self.hy_w1[l], wr=[R_W]); P.dma(W1[33:66, 64:128], self.hy_w1[l], wr=[R_W])
        P.dma(W2[0:64, 0:64], self.hy_w2[l], wr=[R_W]); P.dma(W2[64:128, 64:128], self.hy_w2[l], wr=[R_W])
        P.dma(W3[0:64, 0:64], self.hy_w3[l], wr=[R_W]); P.dma(W3[64:128, 64:128], self.hy_w3[l], wr=[R_W])
        P.dma(W4f[0:64, :], self.hy_w4[l][:, 0:512], wr=[R_W]); P.dma(W4b[64:128, :], self.hy_w4[l][:, 512:1024], wr=[R_W])
        P.dma(zz[0:66, :], self.c_ZZ[seg], wr=[R_zz])
        ofr = _SV["hfr"][0] + l
        fr = self.sv[:, ofr:ofr + 1]
        for i, nm in enumerate(("hb1", "hb2", "hb3")):
            ob = _SV[nm][0] + l
            P.v(lambda e: e.tensor_tensor(frb[:, i:i + 1], self.sv[:, ob:ob + 1], fr, ALU.mult), rd=[self.R_sv], wrp=[R_frb])
        layers = [(W1, zz, R_zz, 66, Ha, R_Ha), (W2, Ha, R_Ha, 128, Hb, R_Hb), (W3, Hb, R_Hb, 128, Ha, R_Ha)]
        n = 0
        for li, (W, src, R_src, Kc, dst, R_dst) in enumerate(layers):
            for ch in range(nch):
                cs = slice(ch * CH, (ch + 1) * CH)
                bk = n % 4
                arg, m_, ra = argb[n % 4], mb[n % 4], R_arg[n % 4]
                n += 1
                P.t(lambda e: e.matmul(pb[bk][:, 0:CH], W[0:Kc, :], src[0:Kc, cs], start=True, stop=True), rd=[R_W, R_src], xb=[Rpb[bk]])
                P.v(lambda e: e.tensor_scalar(arg, pb[bk][:, 0:CH], fr, frb[:, li:li + 1], ALU.mult, ALU.add), rd=[self.R_sv, R_frb], wr=[ra], xb=[Rpb[bk]])
                P.v(lambda e: e.tensor_scalar(m_, arg, PI, None, ALU.is_gt), wr=[ra])
                P.v(lambda e: e.scalar_tensor_tensor(arg, m_, -2 * PI, arg, ALU.mult, ALU.add), wr=[ra])
                P.v(lambda e: e.tensor_scalar(m_, arg, -PI, None, ALU.is_lt), wr=[ra])
                P.v(lambda e: e.scalar_tensor_tensor(arg, m_, 2 * PI, arg, ALU.mult, ALU.add), wr=[ra])
                P.a(lambda e: e.activation(dst[:, cs], arg, AF.Sin), rd=[ra], wrp=[R_dst])
        self.barrier()
        A.off = mark2
        H3, R_H3 = Ha, R_Ha
        W4f2 = A.alloc([512]); W4b2 = A.alloc([512]); R_W2 = Reg()
        P.g(lambda e: e.memset(W4f2, 0.0), wr=[R_W2]); P.g(lambda e: e.memset(W4b2, 0.0), wr=[R_W2])
        P.dma(W4f2[0:64, :], self.hy_w4[l][:, 0:512], wr=[R_W2]); P.dma(W4b2[64:128, :], self.hy_w4[l][:, 512:1024], wr=[R_W2])
        gb = []
        for _ in range(nbuf):
            gb.append(dict(HF=A.alloc([Ln]), HB=A.alloc([Ln]), R_HF=Reg(), R_HB=Reg(), kfb=A.alloc([NFFT], BF16), R_kfb=Reg(),
                           Xs=A.alloc([64, 128], BF16), R_xs=Reg(), nrm=A.alloc([4]), R_nrm=Reg()))
        ttb = [A.alloc([2, CH]) for _ in range(2)]; R_tt = [Reg(), Reg()]
        dcb = [A.alloc([2, CH]) for _ in range(2)]; R_dc = [Reg(), Reg()]
        tmb = [A.alloc([4, 3, 128], BF16) for _ in range(2)]; R_tm = [Reg(), Reg()]
        evb = [A.alloc([2, 512]) for _ in range(2)]; R_ev = [Reg(), Reg()]
        od = _SV["delt"][0]
        cnt = [0]

        def gen(cg):
            b_ = gb[cg % nbuf]
            HF, HB, R_HF, R_HB, kfb, R_kfb, nrm, R_nrm = (b_[k_] for k_ in ("HF", "HB", "R_HF", "R_HB", "kfb", "R_kfb", "nrm", "R_nrm"))
            dl = self.sv[:, od + cg:od + cg + 1]
            for ch in range(nch):
                cs = slice(ch * CH, (ch + 1) * CH)
                n = cnt[0]; cnt[0] += 1
                tt, rtt, dc, rdc = ttb[n % 2], R_tt[n % 2], dcb[n % 2], R_dc[n % 2]
                for r_ in range(2):
                    P.dma(tt[:, r_, :], self.c_TT[seg][r_:r_ + 1, cs].partition_broadcast(128).squeeze(1), wrp=[rtt])
                P.a(lambda e: e.activation(dc, tt, AF.Exp, scale=dl), rd=[rtt, self.R_sv], wr=[rdc])
                P.t(lambda e: e.matmul(pb[2][:, 0:CH], W4f2[:, cg * 128:(cg + 1) * 128], H3[:, cs], start=True, stop=True), rd=[R_W2, R_H3], xb=[Rpb[2]])
                P.t(lambda e: e.matmul(pb[3][:, 0:CH], W4b2[:, cg * 128:(cg + 1) * 128], H3[:, cs], start=True, stop=True), rd=[R_W2, R_H3], xb=[Rpb[3]])
                P.v(lambda e: e.tensor_tensor(HF[:, cs], pb[2][:, 0:CH], dc[:, 0, :], ALU.mult), rd=[rdc], wrp=[R_HF], xb=[Rpb[2]])
                P.v(lambda e: e.tensor_tensor(HB[:, cs], pb[3][:, 0:CH], dc[:, 1, :], ALU.mult), rd=[rdc], wrp=[R_HB], xb=[Rpb[3]])
            P.g(lambda e: e.memset(HB[:, 0:1], 0.0), rd=[R_HB], wrp=[R_HB])
            P.v(lambda e: e.tensor_reduce(nrm[:, 0:1], HF, mybir.AxisListType.X, ALU.add, apply_absolute_value=True), rd=[R_HF], wr=[R_nrm])
            P.v(lambda e: e.tensor_reduce(nrm[:, 1:2], HB, mybir.AxisListType.X, ALU.add, apply_absolute_value=True), rd=[R_HB], wr=[R_nrm])
            P.v(lambda e: e.tensor_tensor(nrm[:, 2:3], nrm[:, 0:1], nrm[:, 1:2], ALU.add), wr=[R_nrm])
            P.v(lambda e: e.reciprocal(nrm[:, 3:4], nrm[:, 2:3]), wr=[R_nrm])
            rn = nrm[:, 3:4]
            if seg == "c":
                P.a(lambda e: e.activation(self.kwin[:, cg, 0:LC - 1], HB[:, 1:LC], AF.Copy, scale=rn), rd=[R_HB, R_nrm], wrp=[self.R_kwin[cg]])
                P.a(lambda e: e.activation(self.kwin[:, cg, LC - 1:2 * LC - 1], HF, AF.Copy, scale=rn), rd=[R_HF, R_nrm], wrp=[self.R_kwin[cg]])
                return
            P.a(lambda e: e.activation(kfb[:, 0:Ln], HF, AF.Copy, scale=rn), rd=[R_HF, R_nrm], wr=[R_kfb])
            P.a(lambda e: e.activation(kfb[:, NFFT - Ln:NFFT], HB, AF.Copy, scale=rn), rd=[R_HB, R_nrm], wr=[R_kfb])
            self.dump("kfilt%d%s%d" % (l, seg, cg), kfb, rd=[R_kfb], dt=BF16)
            P.dma(self.KD[cg * 128:(cg + 1) * 128, :], kfb, rd=[R_kfb], wr=[self.R_KD4[cg]])
            self.load_xs(b_["Xs"], b_["R_xs"], self.KD[cg * 128:(cg + 1) * 128, :], 64, rd=[self.R_KD4[cg]])

        def fft(cg):
            b_ = gb[cg % nbuf]
            kfb, R_kfb = b_["kfb"], b_["R_kfb"]
            A_sb = kfb.rearrange("p (k c) -> p k c", c=128); R_a = R_kfb
            self.fft_stepA(b_["Xs"], b_["R_xs"], 64, A_sb, R_a)
            KFd = self.KF[(l, seg)]

            def consume(bi, k0, nb, XR, XI, RXR, RXI):
                ev, rev = evb[bi % 2], R_ev[bi % 2]
                P.a(lambda e: e.activation(ev[:, 0, 0:nb * 128], XR[:, 0:nb * 128], AF.Copy), wr=[rev], xb=[RXR])
                P.v(lambda e: e.tensor_copy(ev[:, 1, 0:nb * 128], XI[:, 0:nb * 128]), wr=[rev], xb=[RXI])
                for ri in range(2):
                    P.dma(KFd[k0:k0 + nb, ri, :, cg * 128:(cg + 1) * 128].rearrange("k p c -> p k c"),
                          ev[:, ri, 0:nb * 128].rearrange("p (k c) -> p k c", c=128), rd=[rev], wrp=[self.R_KF[(l, seg)][cg]])
            self.fft_stepC(A_sb, R_a, False, tmb, R_tm, consume, tmq="act")

        if nbuf == 1:
            for cg in range(4):
                gen(cg)
                if seg != "c":
                    fft(cg)
        else:
            gen(0)
            for cg in range(4):
                if cg + 1 < 4:
                    gen(cg + 1)
                if seg != "c":
                    fft(cg)
        self.barrier()
        A.off = mark

    def hyena(self, l, segs, groups=(0, 1, 2, 3)):
        P, A, pb, Rpb = self.P, self.ar, self.pb, self.R_pb
        mark = A.off
        vv = A.alloc([L]); R_vv = Reg()
        gate = A.alloc([L], BF16); R_gate = Reg()
        vvb = A.alloc([L], BF16); R_vvb = Reg()
        XZ = A.alloc([64, 128], BF16); R_xz = Reg()
        AZ = A.alloc([64 * 128], BF16); R_az = Reg()
        tmb = [A.alloc([4, 3, 128], BF16) for _ in range(2)]; R_tm = [Reg(), Reg()]
        mark2 = A.off
        ocw, ocb, ohb = _SV["convw"][0], _SV["convb"][0], _SV["hyb"][0]
        vvc = A.alloc([LC]); gatec = A.alloc([LC], BF16); vvbc = A.alloc([LC], BF16)
        VVC = (vvc, Reg(), gatec, Reg(), vvbc, Reg())
        VVX = (vv, R_vv, gate, R_gate, vvb, R_vvb)
        mark2 = A.off

        def seg_info(seg):
            Ln = L if seg == "x" else LC
            col0 = LC if seg == "x" else 0
            blocks = list(range(1, NB)) if seg == "x" else [0]
            return Ln, col0, blocks, Ln // 128

        def stage1_gen(seg, cg, VV):
            vv, R_vv, gate, R_gate, vvb, R_vvb = VV
            Ln, col0, blocks, n1cnt = seg_info(seg)
            stg = A.alloc([8, 128]); R_stg = Reg()
            wbf = [A.alloc([8, 128], BF16) for _ in range(4)]; R_w = [Reg() for _ in range(4)]
            Us = [[A.alloc([BLK + 2]) for _ in range(3)] for _ in range(2)]; R_U = [[Reg() for _ in range(3)] for _ in range(2)]
            cvs = [[A.alloc([BLK]) for _ in range(4)] for _ in range(2)]; R_cv = [[Reg() for _ in range(4)] for _ in range(2)]
            for si in range(4):
                self.load_w_slice(l, 2560 + si * 512 + cg * 128, wbf[si], R_w[si], stg, R_stg)
                yield
            for bi, blk in enumerate(blocks):
                sset = bi % 2
                c0 = blk * BLK
                first, last = bi == 0, bi == len(blocks) - 1
                lo = c0 if first else c0 - 1
                hi = c0 + BLK if last else c0 + BLK + 1
                n = hi - lo
                uo = 1 if first else 0
                lc = c0 - col0
                for si in range(4):
                    self.proj(pb[si][:, 0:n], Rpb[si], wbf[si], R_w[si], lo, n)
                    yield
                for si in range(3):
                    U, RU = Us[sset][si], R_U[sset][si]
                    if first:
                        P.g(lambda e: e.memset(U[:, 0:1], 0.0), wr=[RU])
                        yield
                    if last:
                        P.g(lambda e: e.memset(U[:, BLK + 1:BLK + 2], 0.0), wr=[RU])
                        yield
                    P.a(lambda e: e.activation(U[:, uo:uo + n], pb[si][:, 0:n], AF.Copy), wr=[RU], xb=[Rpb[si]])
                    yield
                sg, Rsg = cvs[sset][3], R_cv[sset][3]
                go = 0 if first else 1
                P.a(lambda e: e.activation(sg, pb[3][:, go:go + BLK], AF.Silu), wr=[Rsg], xb=[Rpb[3]])
                yield
                for si in range(3):
                    U, RU = Us[sset][si], R_U[sset][si]
                    cv, Rcv = cvs[sset][si], R_cv[sset][si]
                    wj = lambda j: self.sv[:, ocw + l * 36 + j * 12 + si * 4 + cg:ocw + l * 36 + j * 12 + si * 4 + cg + 1]
                    bb = self.sv[:, ocb + l * 12 + si * 4 + cg:ocb + l * 12 + si * 4 + cg + 1]
                    P.g(lambda e: e.tensor_scalar(cv, U[:, 1:BLK + 1], wj(1), bb, ALU.mult, ALU.add), rd=[RU, self.R_sv], wr=[Rcv])
                    yield
                    P.v(lambda e: e.scalar_tensor_tensor(cv, U[:, 0:BLK], wj(0), cv, ALU.mult, ALU.add), rd=[RU, self.R_sv], wr=[Rcv])
                    yield
                    P.v(lambda e: e.scalar_tensor_tensor(cv, U[:, 2:BLK + 2], wj(2), cv, ALU.mult, ALU.add), rd=[RU, self.R_sv], wr=[Rcv])
                    yield
                x0c, x1c, vc = cvs[sset][0], cvs[sset][1], cvs[sset][2]
                lcs = slice(lc, lc + BLK)
                P.v(lambda e: e.tensor_tensor(vv[:, lcs], vc, x1c, ALU.mult), rd=[R_cv[sset][1], R_cv[sset][2]], wrp=[R_vv])
                yield
                P.g(lambda e: e.tensor_copy(vvb[:, lcs], vv[:, lcs]), rd=[R_vv], wrp=[R_vvb])
                yield
                P.v(lambda e: e.tensor_tensor(gate[:, lcs], x0c, sg, ALU.mult), rd=[R_cv[sset][0], Rsg], wrp=[R_gate])
                yield
            VDd = self.VD if seg == "x" else self.VDC
            P.dma(VDd[cg * 128:(cg + 1) * 128, :], vvb[:, 0:Ln], rd=[R_vvb, R_vv], wr=[self.R_VD[cg]])
            yield
            self.dump("vv%d%s%d" % (l, seg, cg), vv[:, 0:Ln], rd=[R_vv])

        def convc_gen(cg, VV):
            vv, R_vv, gate, R_gate, vvb, R_vvb = VV
            seg = "c"
            Ln = LC
            ybuf = A.alloc([LC]); R_yb = Reg()
            accD = [A.alloc([LC]) for _ in range(8)]; R_aD = [Reg() for _ in range(8)]
            accP = [A.alloc([LC]) for _ in range(4)]; R_aP = [Reg() for _ in range(4)]
            tmpA = [A.alloc([LC]) for _ in range(8)]; R_tA = [Reg() for _ in range(8)]
            for a_, r_ in zip(accD + accP, R_aD + R_aP):
                P.g(lambda e: e.memset(a_, 0.0), wr=[r_])
                yield
            nd = na = 0
            for s_ in range(LC):
                win = self.kwin[:, cg, LC - 1 - s_:2 * LC - 1 - s_]
                vs = vv[:, s_:s_ + 1]
                if s_ % 8 < 7:
                    k_ = nd % 8; nd += 1
                    P.v(lambda e: e.scalar_tensor_tensor(accD[k_], win, vs, accD[k_], ALU.mult, ALU.add), rd=[self.R_kwin[cg], R_vv], wr=[R_aD[k_]])
                    yield
                else:
                    k_ = na % 8; j_ = na % 4; na += 1
                    P.a(lambda e: e.activation(tmpA[k_], win, AF.Copy, scale=vs), rd=[self.R_kwin[cg], R_vv], wr=[R_tA[k_]])
                    yield
                    P.g(lambda e: e.tensor_tensor(accP[j_], accP[j_], tmpA[k_], ALU.add), rd=[R_tA[k_]], wr=[R_aP[j_]])
                    yield
            for a_, b_ in ((0, 1), (2, 3), (4, 5), (6, 7), (0, 2), (4, 6), (0, 4)):
                P.v(lambda e: e.tensor_tensor(accD[a_], accD[a_], accD[b_], ALU.add), rd=[R_aD[b_]], wr=[R_aD[a_]])
                yield
            for a_, b_ in ((0, 1), (2, 3), (0, 2)):
                P.g(lambda e: e.tensor_tensor(accP[a_], accP[a_], accP[b_], ALU.add), rd=[R_aP[b_]], wr=[R_aP[a_]])
                yield
            P.v(lambda e: e.tensor_tensor(ybuf[:, 0:LC], accD[0], accP[0], ALU.add), rd=[R_aD[0], R_aP[0]], wr=[R_yb])
            yield
            self.dump("yconv%d%s%d" % (l, seg, cg), ybuf[:, 0:Ln], rd=[R_yb])
            hb = self.sv[:, ohb + l * 4 + cg:ohb + l * 4 + cg + 1]
            oBc = AZ[:, 0:Ln]
            P.v(lambda e: e.scalar_tensor_tensor(ybuf[:, 0:LC], vv[:, 0:LC], hb, ybuf[:, 0:LC], ALU.mult, ALU.add), rd=[R_vv, self.R_sv], wr=[R_yb])
            yield
            P.v(lambda e: e.tensor_tensor(oBc, ybuf[:, 0:LC], gate[:, 0:LC], ALU.mult), rd=[R_gate, R_yb], wrp=[R_az])
            yield
            P.dma(self.MIX[4 + cg, :, 0:BLK], oBc, rd=[R_az], wr=[self.R_MIX[4 + cg][0]])
            yield

        def stage2_x(cg):
            seg = "x"
            Ln, col0, blocks, n1cnt = seg_info(seg)
            VDd = self.VD
            A.off = mark2
            Y = A.alloc([66 * 128], BF16); R_y = Reg()
            Y_sb = Y.rearrange("p (k c) -> p k c", c=128)
            tq = [A.alloc([512]) for _ in range(4)]; R_tq = Reg()
            kfb = [A.alloc([2, 512]) for _ in range(2)]; R_kf = [Reg(), Reg()]
            ybuf = A.alloc([L]); R_yb = Reg()
            Xs = XZ
            self.load_xs(Xs, R_xz, VDd[cg * 128:(cg + 1) * 128, :], n1cnt, zero_first=(n1cnt < 32), rd=[self.R_VD[cg]])
            A_sb = AZ.rearrange("p (k c) -> p k c", c=128)
            self.fft_stepA(Xs, R_xz, 32, A_sb, R_az)
            KFd = self.KF[(l, seg)]

            def consume_f(bi, k0, nb, XR, XI, RXR, RXI):
                kf, rkf = kfb[bi % 2], R_kf[bi % 2]
                w = nb * 128
                for ri in range(2):
                    P.dma(kf[:, ri, 0:w].rearrange("p (k c) -> p k c", c=128),
                          KFd[k0:k0 + nb, ri, :, cg * 128:(cg + 1) * 128].rearrange("k p c -> p k c"),
                          rd=[self.R_KF[(l, seg)][cg]], wrp=[rkf])
                Kr, Ki = kf[:, 0, 0:w], kf[:, 1, 0:w]
                P.v(lambda e: e.tensor_tensor(tq[0][:, 0:w], XR[:, 0:w], Kr, ALU.mult), rd=[rkf], wr=[R_tq], xb=[RXR])
                P.v(lambda e: e.tensor_tensor(tq[1][:, 0:w], XI[:, 0:w], Ki, ALU.mult), rd=[rkf], wrp=[R_tq], xb=[RXI])
                P.v(lambda e: e.tensor_tensor(tq[2][:, 0:w], XR[:, 0:w], Ki, ALU.mult), rd=[rkf], wrp=[R_tq], xb=[RXR])
                P.v(lambda e: e.tensor_tensor(tq[3][:, 0:w], XI[:, 0:w], Kr, ALU.mult), rd=[rkf], wrp=[R_tq], xb=[RXI])
                yre = Y_sb[:, k0:k0 + nb, :].rearrange("p k c -> p (k c)")
                yim = Y_sb[:, 33 + k0:33 + k0 + nb, :].rearrange("p k c -> p (k c)")
                P.g(lambda e: e.tensor_tensor(yre, tq[0][:, 0:w], tq[1][:, 0:w], ALU.subtract), rd=[R_tq], wrp=[R_y])
                P.g(lambda e: e.tensor_tensor(yim, tq[2][:, 0:w], tq[3][:, 0:w], ALU.add), rd=[R_tq], wrp=[R_y])
            self.fft_stepC(A_sb, R_az, False, tmb, R_tm, consume_f)
            Zd = AZ.rearrange("p (m j k) -> p k j m", j=2, k=64)

            def consume_i(bi, k0, nb, ZR, ZI, RZR, RZI):
                dst = Zd[:, k0:k0 + nb, :, :]
                src = ZR[:, 0:nb * 128].rearrange("p (k j m) -> p k j m", j=2, m=64)
                P.a(lambda e: e.activation(dst, src, AF.Copy), wrp=[R_az], xb=[RZR])
                ks = [k for k in range(k0, k0 + nb) if k not in (0, 32)]
                if ks:
                    j0 = ks[0] - k0
                    dsti = Zd[:, 32 + ks[0]:32 + ks[-1] + 1, :, :]
                    srci = ZI[:, j0 * 128:(j0 + len(ks)) * 128].rearrange("p (k j m) -> p k j m", j=2, m=64)
                    P.v(lambda e: e.tensor_copy(dsti, srci), wrp=[R_az], xb=[RZI])
            self.fft_stepC(Y_sb, R_y, True, tmb, R_tm, consume_i)
            ZT = XZ
            Zv = AZ.rearrange("p (m x) -> p m x", x=128)
            for m8 in range(8):
                bk = m8 % 2
                pv = pb[bk][:, 0:512].bitcast(BF16).rearrange("p (a b) -> p a b", a=8)
                for j in range(8):
                    m = m8 * 8 + j
                    P.t(lambda e: e.transpose(pv[:, j, :], Zv[:, m, :], self.identB), rd=[R_az, self.R_const], xb=[Rpb[bk]])
                if bk == 0:
                    P.a(lambda e: e.activation(ZT[:, m8 * 8:(m8 + 1) * 8, :], pv, AF.Copy), wrp=[R_xz], xb=[Rpb[bk]])
                else:
                    P.v(lambda e: e.tensor_copy(ZT[:, m8 * 8:(m8 + 1) * 8, :], pv), wrp=[R_xz], xb=[Rpb[bk]])
            NN = n1cnt
            yv = ybuf[:, 0:Ln].rearrange("p (n1 n2) -> p n2 n1", n2=128)
            for g16 in range(8):
                bk = 2 + g16 % 2
                for j in range(16):
                    n2 = g16 * 16 + j
                    for jj in range(2):
                        P.t(lambda e: e.matmul(pb[bk][64 * jj:64 * jj + 64, j * NN:(j + 1) * NN], ZT[64 * jj:64 * jj + 64, :, n2],
                                               self.RA2[64 * jj:64 * jj + 64, 0:NN], start=True, stop=True),
                            rd=[R_xz, self.R_const], xb=[Rpb[bk]])
                src = pb[bk][:, 0:16 * NN].rearrange("p (a b) -> p a b", b=NN)
                dst = yv[:, g16 * 16:(g16 + 1) * 16, :]
                if g16 % 2 == 0:
                    P.a(lambda e: e.activation(dst, src, AF.Copy), wrp=[R_yb], xb=[Rpb[bk]])
                else:
                    P.v(lambda e: e.tensor_copy(dst, src), wrp=[R_yb], xb=[Rpb[bk]])
            self.dump("yconv%d%s%d" % (l, seg, cg), ybuf[:, 0:Ln], rd=[R_yb])
            hb = self.sv[:, ohb + l * 4 + cg:ohb + l * 4 + cg + 1]
            oB = AZ[:, 0:Ln]
            PW = min(1024, Ln)
            for pc in range(Ln // PW):
                cs = slice(pc * PW, (pc + 1) * PW)
                P.v(lambda e: e.scalar_tensor_tensor(ybuf[:, cs], vv[:, cs], hb, ybuf[:, cs], ALU.mult, ALU.add), rd=[R_vv, self.R_sv], wr=[R_yb])
                P.v(lambda e: e.tensor_tensor(oB[:, cs], ybuf[:, cs], gate[:, cs], ALU.mult), rd=[R_gate, R_yb], wrp=[R_az])
            for blk in blocks:
                lc = blk * BLK - col0
                P.dma(self.MIX[4 + cg, :, blk * BLK:(blk + 1) * BLK], oB[:, lc:lc + BLK], rd=[R_az], wr=[self.R_MIX[4 + cg][blk]])
            self.barrier()

        def drain_w(gw):
            gw = [[g_, w_] for g_, w_ in gw]
            while gw:
                for it in list(gw):
                    for _ in range(it[1]):
                        try:
                            next(it[0])
                        except StopIteration:
                            gw.remove(it)
                            break

        for cg in groups:
            if "c" in segs:
                A.off = mark2
                drain_w([(stage1_gen("c", cg, VVC), 1)])
                self.barrier()
                if "x" not in segs:
                    A.off = mark2
                    drain_w([(convc_gen(cg, VVC), 1)])
                    self.barrier()
            if "x" in segs:
                A.off = mark2
                gw = [(stage1_gen("x", cg, VVX), 1)]
                if "c" in segs:
                    gw.insert(0, (convc_gen(cg, VVC), 2))
                drain_w(gw)
                self.barrier()
                stage2_x(cg)
        A.off = mark

    def bcast_rows(self, dst, cols_fn, R_dst, tmpd, R_tmpd):
        P, pb, Rpb = self.P, self.pb, self.R_pb
        for k in range(8):
            bk = 0 if k < 4 else 3
            P.v(lambda e: e.tensor_scalar(tmpd, self.identF, cols_fn(k), None, ALU.mult), rd=[self.R_const, self.R_mod, self.R_sv], wr=[R_tmpd])
            P.t(lambda e: e.matmul(pb[bk][:, (k % 4) * 128:(k % 4 + 1) * 128], self.onesF, tmpd, start=True, stop=True),
                rd=[R_tmpd, self.R_const], xb=[Rpb[bk]])
        P.a(lambda e: e.activation(dst[:, 0:512], pb[0][:, :], AF.Copy), wrp=[R_dst], xb=[Rpb[0]])
        P.a(lambda e: e.activation(dst[:, 512:1024], pb[3][:, :], AF.Copy), wrp=[R_dst], xb=[Rpb[3]])

    def outproj(self, l, last):
        P, A, pb, Rpb = self.P, self.ar, self.pb, self.R_pb
        mark = A.off
        wout = A.alloc([8, D], BF16); R_wo = Reg()
        wstg = [A.alloc([8, 256]) for _ in range(2)]; R_ws = [Reg(), Reg()]
        wv = self.w_out[l].rearrange("(k p) n -> p k n", p=128)
        for pc in range(4):
            P.dma(wstg[pc % 2], wv[:, :, pc * 256:(pc + 1) * 256], wr=[R_ws[pc % 2]])
            P.g(lambda e: e.tensor_copy(wout[:, :, pc * 256:(pc + 1) * 256], wstg[pc % 2]), rd=[R_ws[pc % 2]], wrp=[R_wo])
        tmpd = A.alloc([128]); R_tmpd = Reg()
        gtbc = [A.alloc([D]) for _ in range(2)]; R_gt = [Reg(), Reg()]
        for seg in ((0,) if last else (0, 1)):
            self.bcast_rows(gtbc[seg], lambda k: self.mod[:, l, 16 + k, seg:seg + 1], R_gt[seg], tmpd, R_tmpd)
        if last:
            fnwbc = A.alloc([D]); R_fn = Reg()
            of = _SV["fnw"][0]
            self.bcast_rows(fnwbc, lambda k: self.sv[:, of + k:of + k + 1], R_fn, tmpd, R_tmpd)
        xts = [A.alloc([D]) for _ in range(2)]; R_x = [Reg(), Reg()]
        xns = [A.alloc([D]) for _ in range(2)]; R_xn = [Reg(), Reg()]
        tmps = [A.alloc([D]) for _ in range(2)]; R_tp = [Reg(), Reg()]
        mts = [A.alloc([8, 128], BF16) for _ in range(2)]; R_mt = [Reg(), Reg()]
        bufs = [(A.alloc([D], BF16), A.alloc([1]), A.alloc([1]), A.alloc([D]), Reg()) for _ in range(2)]
        tiles = list(range(2, NT)) if last else list(range(NT))
        def stage_mm(i, tt):
            s2 = i % 2
            xt, mt = xts[s2], mts[s2]
            blk = tt // 2
            P.dma(mt, self.MIX[:, :, tt * 128:(tt + 1) * 128].rearrange("f p t -> p f t"),
                  rd=[self.R_MIX[f][blk] for f in range(8)], wr=[R_mt[s2]])
            if l == 0:
                src = self.ctx_in[tt * 128:(tt + 1) * 128, :] if tt < 2 else self.x_in[(tt - 2) * 128:(tt - 1) * 128, :]
                P.dma(xt, src, wr=[R_x[s2]])
            else:
                P.dma(xt, self.XRES[tt * 128:(tt + 1) * 128, :], rd=[self.R_XRES[tt]], wr=[R_x[s2]])
            for half in range(2):
                bk = 4 + half + 2 * s2
                hs = slice(half * 512, (half + 1) * 512)
                for f in range(8):
                    P.t(lambda e: e.matmul(pb[bk][:, :], mt[:, f, :], wout[:, f, hs], start=(f == 0), stop=(f == 7)),
                        rd=[R_mt[s2], R_wo], xb=[Rpb[bk]])

        def stage_fin(i, tt):
            s2 = i % 2
            seg = 1 if tt < 2 else 0
            xt, xn, tp = xts[s2], xns[s2], tmps[s2]
            for half in range(2):
                bk = 4 + half + 2 * s2
                hs = slice(half * 512, (half + 1) * 512)
                P.v(lambda e: e.tensor_tensor(tp[:, hs], pb[bk][:, :], gtbc[seg][:, hs], ALU.mult), rd=[R_gt[seg]], wrp=[R_tp[s2]], xb=[Rpb[bk]])
                P.g(lambda e: e.tensor_tensor(xn[:, hs], tp[:, hs], xt[:, hs], ALU.add), rd=[R_tp[s2], R_x[s2]], wrp=[R_xn[s2]])
            if not last:
                P.dma(self.XRES[tt * 128:(tt + 1) * 128, :], xn, rd=[R_xn[s2]], wr=[self.R_XRES[tt]])
                self.tile_to_hx(l + 1, tt, xn, R_xn[s2], bufs[s2])
            else:
                junk, ssq, rstd, on, R_t = bufs[s2]
                P.a(lambda e: e.activation(junk, xn, AF.Square, accum_out=ssq), rd=[R_xn[s2]], wr=[R_t])
                P.a(lambda e: e.activation(ssq, ssq, AF.Sqrt, scale=1.0 / D, bias=EPS), wr=[R_t])
                P.v(lambda e: e.reciprocal(rstd, ssq), wr=[R_t])
                P.v(lambda e: e.scalar_tensor_tensor(on, xn, rstd, fnwbc, ALU.mult, ALU.mult), rd=[R_xn[s2], R_fn], wr=[R_t])
                P.dma(self.out[(tt - 2) * 128:(tt - 1) * 128, :], on, rd=[R_t], q="act")

        stage_mm(0, tiles[0])
        for i, tt in enumerate(tiles):
            if i + 1 < len(tiles):
                stage_mm(i + 1, tiles[i + 1])
            stage_fin(i, tt)
        self.barrier()
        A.off = mark

    def build_all(self):
        self.adaln()
        for l in range(self.nlayers):
            last = l == self.nlayers - 1
            segs = ("x",) if (last and self.nlayers > 1) else ("c", "x")
            for s in segs:
                self.hy_filter(l, s, alias_hx=True)
        self.phase_b_from_dram(0)
        for l in range(self.nlayers):
            last = l == self.nlayers - 1
            ctx_out = not (last and self.nlayers > 1)
            self.hgrn2(l, ctx_out)
            self.hyena(l, ["c", "x"] if ctx_out else ["x"])
            self.outproj(l, last)
        self.P.finish()


_CONSTS = None


def kernel(**inputs):
    global _CONSTS
    inp = {k: np.asarray(v) for k, v in inputs.items()}
    if _CONSTS is None:
        _CONSTS = _host_consts()
    B = Builder(nlayers=2)
    B.build_all()
    in_maps = []
    for b in range(8):
        m = {"x": np.ascontiguousarray(inp["x"][b], dtype=np.float32), "ctx": np.ascontiguousarray(inp["ctx"][b], dtype=np.float32),
             "smallv": _pack_small(inp, b)}
        for k in ("w_ada", "w_in", "w_out", "hy_w1", "hy_w2", "hy_w3", "hy_w4"):
            m[k] = np.ascontiguousarray(inp[k], dtype=np.float32)
        m.update(_CONSTS)
        in_maps.append(m)
    res = run_bass_kernel_spmd(B.nc, in_maps, core_ids=list(range(8)))
    return np.stack([np.asarray(r["out"], dtype=np.float32) for r in res.results], axis=0)
```

```python
import numpy as np
import ml_dtypes
import concourse.bass as bass
import concourse.mybir as mybir
from concourse.bass_utils import run_bass_kernel_spmd

F32 = mybir.dt.float32
BF16 = mybir.dt.bfloat16
AF = mybir.ActivationFunctionType
ALU = mybir.AluOpType

NDS = 24
HG_W = 4


class Tok:
    __slots__ = ("eng", "idx")

    def __init__(self, eng, idx):
        self.eng = eng
        self.idx = idx


class Reg:
    __slots__ = ("name", "w", "r", "pw", "pr", "full")

    def __init__(self, name=""):
        self.name = name
        self.w = {}
        self.r = {}
        self.pw = {}
        self.pr = {}
        self.full = {}


def _merge(d, tok):
    o = d.get(tok.eng)
    if o is None or o.idx < tok.idx:
        d[tok.eng] = tok


class _Rec:
    __slots__ = ("fn", "waits", "flag", "dsem", "dval")

    def __init__(self, fn, waits):
        self.fn = fn
        self.waits = waits
        self.flag = False
        self.dsem = None
        self.dval = 0


class _Capture:
    def __init__(self):
        self.call = None

    def __getattr__(self, name):
        def f(*args, **kw):
            self.call = (name, args, kw)
        return f


class Prog:
    ENGS = ("pe", "dve", "act", "pool", "sp")

    def __init__(self, nc):
        self.nc = nc
        self.q = {e: [] for e in self.ENGS}
        self.waited = {e: {} for e in self.ENGS}
        self.dma_n = 0
        self.dma_last = [None] * NDS
        self.dma_val = [0] * NDS
        self.n_sb = 0

    def sb(self, name, shape, dtype):
        return self.nc.alloc_sbuf_tensor(name, list(shape), dtype)

    def ps(self, name, shape, dtype=F32):
        return self.nc.alloc_psum_tensor(name, list(shape), dtype)

    def dram(self, name, shape, dtype, kind="Internal"):
        return self.nc.dram_tensor(name, list(shape), dtype, kind=kind).ap()

    def _deps(self, eng, rd, wr, wrp, extra, xb=()):
        deps = []
        for r in xb:
            deps.extend(t for e2, t in r.w.items() if e2 != eng)
        for r in rd:
            deps.extend(r.w.values())
        for r in wr:
            deps.extend(r.w.values())
            deps.extend(r.r.values())
        for r in wrp:
            if r.r:
                r.pw, r.pr = r.w, r.r
                r.w, r.r = {}, {}
                r.full = {}
            deps.extend(r.pw.values())
            deps.extend(r.pr.values())
            deps.extend(r.full.values())
        deps.extend(extra)
        return deps

    def _post(self, tok, rd, wr, wrp, xb=()):
        for r in xb:
            _merge(r.w, tok)
        for r in rd:
            _merge(r.r, tok)
        for r in wr:
            r.pw, r.pr = {}, {}
            r.w, r.r = {tok.eng: tok}, {}
            r.full = {tok.eng: tok}
        for r in wrp:
            _merge(r.w, tok)

    def _waits(self, eng, deps):
        waits = []
        wd = self.waited[eng]
        for d in deps:
            if d is None:
                continue
            if d.eng == eng and eng == "pe":
                continue
            if wd.get(d.eng, -1) >= d.idx:
                continue
            wd[d.eng] = d.idx
            if not d.eng.startswith("dma"):
                self.q[d.eng][d.idx].flag = True
            waits.append(d)
        return waits

    def op(self, eng, fn, rd=(), wr=(), wrp=(), deps=(), xb=()):
        cap = _Capture()
        fn(cap)
        name, args, kw = cap.call
        fn = lambda h, name=name, args=args, kw=kw: getattr(h, name)(*args, **kw)
        dl = self._deps(eng, rd, wr, wrp, deps, xb)
        waits = self._waits(eng, dl)
        q = self.q[eng]
        q.append(_Rec(fn, waits))
        tok = Tok(eng, len(q) - 1)
        self._post(tok, rd, wr, wrp, xb)
        return tok

    def t(self, fn, **kw):
        return self.op("pe", fn, **kw)

    def v(self, fn, **kw):
        return self.op("dve", fn, **kw)

    def a(self, fn, **kw):
        return self.op("act", fn, **kw)

    def g(self, fn, **kw):
        return self.op("pool", fn, **kw)

    def dma(self, out, in_, rd=(), wr=(), wrp=(), deps=(), q="sp"):
        k = self.dma_n % NDS
        self.dma_n += 1
        dl = self._deps(q, rd, wr, wrp, deps)
        if self.dma_last[k] is not None:
            dl.append(self.dma_last[k])
        waits = self._waits(q, dl)
        rec = _Rec(lambda e: e.dma_start(out=out, in_=in_), waits)
        self.dma_val[k] += 16
        rec.dsem = k
        rec.dval = self.dma_val[k]
        self.q[q].append(rec)
        tok = Tok("dma%d" % k, self.dma_val[k])
        self.dma_last[k] = tok
        self._post(tok, rd, wr, wrp)
        return tok

    def last_real(self, e):
        q = self.q[e]
        for i in range(len(q) - 1, -1, -1):
            if q[i].fn is not None and q[i].dsem is None:
                return Tok(e, i)
        return None

    def finish(self):
        nc = self.nc
        fin = [t for t in self.dma_last if t is not None]
        for e in ("pe", "dve", "act", "pool"):
            t = self.last_real(e)
            if t is not None:
                fin.append(t)
        waits = self._waits("sp", fin)
        self.q["sp"].append(_Rec(None, waits))
        cnt = self.cnt = {}
        for e in self.ENGS:
            c = 0
            arr = []
            for rec in self.q[e]:
                if rec.flag:
                    c += 1
                arr.append(c)
            cnt[e] = arr
        import contextlib
        with contextlib.ExitStack() as st:
            sems = {e: st.enter_context(nc.semaphore("s_" + e)) for e in self.ENGS}
            dsems = [st.enter_context(nc.semaphore("d%d" % i)) for i in range(NDS)]
            block = st.enter_context(nc.Block())

            def run(e, h):
                for rec in self.q[e]:
                    for w in rec.waits:
                        if w.eng.startswith("dma"):
                            h.wait_ge(dsems[int(w.eng[3:])], w.idx)
                        else:
                            h.wait_ge(sems[w.eng], cnt[w.eng][w.idx])
                    if rec.fn is None:
                        continue
                    ins = rec.fn(h)
                    if rec.dsem is not None:
                        ins.then_inc(dsems[rec.dsem], 16)
                    elif rec.flag:
                        ins.then_inc(sems[e], 1)

            @block.tensor
            def _(h):
                run("pe", h)

            @block.vector
            def _(h):
                run("dve", h)

            @block.scalar
            def _(h):
                run("act", h)

            @block.gpsimd
            def _(h):
                run("pool", h)

            @block.sync
            def _(h):
                run("sp", h)


D = 1024
L = 4096
LC = 256
T = LC + L
NT = T // 128
BLK = 256
NB = T // BLK
EPS = 1e-6
NFFT = 8192
HY_MIN = float(np.log(1e-2) / 1.5)
HY_MAX = float(np.log(1e-2) / 0.3)
PI = float(np.pi)
bf16 = ml_dtypes.bfloat16

_SV = {}
_o = 0
for _n, _w in [("cvT", 16), ("normw", 16), ("fnw", 8), ("bada", 48), ("lbl", 16), ("gnw", 8),
               ("convw", 72), ("convb", 24), ("hyb", 8), ("hb1", 2), ("hb2", 2), ("hb3", 2), ("hfr", 2),
               ("delt", 4)]:
    _SV[_n] = (_o, _w)
    _o += _w
NSV = _o


def _pos_feat(Ln):
    t = np.linspace(0.0, 1.0, Ln, dtype=np.float32)[:, None]
    w = (2.0 * np.pi * np.arange(Ln, dtype=np.float32)[:, None] / Ln).astype(np.float32)
    f = np.linspace(1e-4, 15.0, 16, dtype=np.float32)[None, :]
    return np.concatenate([t, np.cos(f * w), -np.sin(f * w)], axis=-1).astype(np.float32)


def _host_consts():
    c = {}
    c["identF"] = np.eye(128, dtype=np.float32)
    c["identB"] = np.eye(128).astype(bf16)
    j = np.arange(64)[:, None]
    i = np.arange(64)[None, :]
    mk = np.stack([(j <= i), (j >= i)], axis=0).astype(np.float32)
    c["masks"] = np.concatenate([mk, mk], axis=1).transpose(1, 0, 2).copy()
    for Ln, nm in ((L, "x"), (LC, "c")):
        z = _pos_feat(Ln)
        zr = np.zeros_like(z)
        zr[1:] = z[Ln - np.arange(1, Ln)]
        c["ZZ" + nm] = np.concatenate([z.T, zr.T], axis=0).copy()
        tt = np.zeros((2, Ln), np.float32)
        tt[0] = z[:, 0]
        tt[1, 1:] = z[Ln - np.arange(1, Ln), 0]
        c["TT" + nm] = tt
    n1 = np.arange(64, dtype=np.float64)[:, None]
    FA = np.zeros((64, 64))
    k1r = np.arange(33, dtype=np.float64)[None, :]
    FA[:, 0:33] = np.cos(2 * np.pi * n1 * k1r / 64)
    k1i = np.arange(1, 32, dtype=np.float64)[None, :]
    FA[:, 33:64] = -np.sin(2 * np.pi * n1 * k1i / 64)
    c["FA2"] = np.concatenate([FA, FA], axis=0).astype(bf16)
    nn1 = np.arange(32, dtype=np.float64)[None, :]
    RA = np.zeros((64, 32))
    RA[0, :] = 1.0
    RA[32, :] = (-1.0) ** np.arange(32)
    kk = np.arange(1, 32, dtype=np.float64)[:, None]
    RA[1:32, :] = 2 * np.cos(2 * np.pi * nn1 * kk / 64)
    RA[33:64, :] = -2 * np.sin(2 * np.pi * nn1 * kk / 64)
    RA /= NFFT
    c["RA2"] = np.concatenate([RA, RA], axis=0).astype(bf16)
    n2 = np.arange(128, dtype=np.float64)[:, None]
    k2 = np.arange(128, dtype=np.float64)[None, :]
    TM = np.zeros((128, 33, 6, 128), np.float64)
    for k1 in range(33):
        th = 2 * np.pi * n2 * (k1 + 64 * k2) / NFFT
        Tc, Ts = np.cos(th), -np.sin(th)
        TM[:, k1, 0], TM[:, k1, 1], TM[:, k1, 2] = Tc, Ts, -Ts
        TM[:, k1, 3], TM[:, k1, 4], TM[:, k1, 5] = Tc.T, Ts.T, -Ts.T
    c["TM"] = TM.astype(bf16)
    return c


def _pack_small(inp, b):
    sv = np.zeros((128, NSV), np.float32)

    def put(name, arr):
        o, w = _SV[name]
        a = np.asarray(arr, np.float32).reshape(128, -1)
        assert a.shape[1] == w, (name, a.shape, w)
        sv[:, o:o + w] = a

    cv = np.stack([inp["c"][b], inp["c_ctx"]], axis=0)
    put("cvT", cv.reshape(2, 8, 128).transpose(2, 1, 0))
    put("normw", inp["norm_w"].reshape(2, 8, 128).transpose(2, 0, 1))
    put("fnw", inp["final_norm_w"].reshape(8, 128).T)
    put("bada", inp["b_ada"].reshape(2, 24, 128).transpose(2, 0, 1))
    put("lbl", inp["lb_logits"].reshape(2, 2, 4, 128).transpose(3, 0, 1, 2))
    put("gnw", inp["g_norm_w"].reshape(2, 4, 128).transpose(2, 0, 1))
    put("convw", inp["conv_w"].reshape(2, 3, 12, 128).transpose(3, 0, 1, 2))
    put("convb", inp["conv_b"].reshape(2, 12, 128).transpose(2, 0, 1))
    put("hyb", inp["hy_bias"].reshape(2, 4, 128).transpose(2, 0, 1))
    for nm, key in (("hb1", "hy_b1"), ("hb2", "hy_b2"), ("hb3", "hy_b3"), ("hfr", "hy_freq")):
        a = inp[key]
        put(nm, np.concatenate([a.T, a.T], axis=0))
    delt = -np.abs(np.linspace(HY_MIN, HY_MAX, 512, dtype=np.float32))
    put("delt", delt.reshape(4, 128).T)
    return sv


class Arena:
    def __init__(self, P, words):
        self.t = P.sb("arena", [128, words], F32)
        self.words = words
        self.off = 0

    def alloc(self, shape, dtype=F32):
        n = 1
        for s in shape:
            n *= s
        w = n if dtype == F32 else (n + 1) // 2
        w = (w + 7) // 8 * 8
        assert self.off + w <= self.words, ("arena overflow", self.off, w, self.words)
        ap = self.t[:, self.off:self.off + w]
        self.off += w
        if dtype != F32:
            ap = ap.bitcast(dtype)
        ap = ap[:, 0:n]
        if len(shape) == 2:
            ap = ap.rearrange("p (a b) -> p a b", a=shape[0])
        elif len(shape) == 3:
            ap = ap.rearrange("p (a b c) -> p a b c", a=shape[0], b=shape[1])
        return ap


class Builder:
    def __init__(self, nlayers=2, dbg=()):
        self.nlayers = nlayers
        self.dbg = set(dbg)
        nc = self.nc = bass.Bass("TRN2", target_bir_lowering=False)
        P = self.P = Prog(nc)
        self.dbg_outs = {}
        inp = lambda name, shape, dt=F32: nc.dram_tensor(name, list(shape), dt, kind="ExternalInput").ap()
        self.x_in = inp("x", [L, D])
        self.ctx_in = inp("ctx", [LC, D])
        self.smallv_in = inp("smallv", [128, NSV])
        self.w_ada = inp("w_ada", [2, D, 3 * D])
        self.w_in = inp("w_in", [2, D, 4608])
        self.w_out = inp("w_out", [2, D, D])
        self.hy_w1 = inp("hy_w1", [2, 33, 64])
        self.hy_w2 = inp("hy_w2", [2, 64, 64])
        self.hy_w3 = inp("hy_w3", [2, 64, 64])
        self.hy_w4 = inp("hy_w4", [2, 64, 1024])
        self.c_identF = inp("identF", [128, 128])
        self.c_identB = inp("identB", [128, 128], BF16)
        self.c_masks = inp("masks", [128, 2, 64])
        self.c_ZZ = {"x": inp("ZZx", [66, L]), "c": inp("ZZc", [66, LC])}
        self.c_TT = {"x": inp("TTx", [2, L]), "c": inp("TTc", [2, LC])}
        self.c_FA2 = inp("FA2", [128, 64], BF16)
        self.c_RA2 = inp("RA2", [128, 32], BF16)
        self.c_TM = inp("TM", [128, 33, 6, 128], BF16)
        self.out = nc.dram_tensor("out", [L, D], F32, kind="ExternalOutput").ap()
        self.XRES = P.dram("xres", [T, D], F32)
        self.MIX = P.dram("mix", [8, 128, T], BF16)
        self.VD = P.dram("vd", [512, L], BF16)
        self.VDC = P.dram("vdc", [512, LC], BF16)
        self.KD = P.dram("kd", [512, NFFT], BF16)
        self.KF = {}
        for l in range(nlayers):
            segs = ("c", "x") if l < nlayers - 1 or nlayers == 1 else ("x",)
            for s in segs:
                self.KF[(l, s)] = P.dram("kf%d%s" % (l, s), [33, 2, 128, 512], F32)
        self.R_XRES = [Reg() for _ in range(NT)]
        self.R_MIX = [[Reg() for _ in range(NB)] for _ in range(8)]
        self.R_VD = [Reg() for _ in range(4)]
        self.R_KD = Reg()
        self.R_KD4 = [Reg() for _ in range(4)]
        self.R_KF = {k: [Reg() for _ in range(4)] for k in self.KF}
        self.pb = [P.ps("pb%d" % i, [128, 512], F32) for i in range(8)]
        self.R_pb = [Reg() for _ in range(8)]
        self._tt_cnt = 0
        self.ar = Arena(P, 212000 // 4)
        A = self.ar
        self.sv = A.alloc([NSV]); self.R_sv = Reg()
        self.identF = A.alloc([128]); self.identB = A.alloc([128], BF16)
        self.masks = A.alloc([2, 64]); self.onesF = A.alloc([128]); self.onesB = A.alloc([128], BF16)
        self.FA2 = A.alloc([64], BF16); self.RA2 = A.alloc([32], BF16)
        self.R_const = Reg()
        self.mod = A.alloc([2, 24, 2]); self.R_mod = Reg()
        self.weff = A.alloc([2, 8, 2]); self.lbv = A.alloc([2, 8]); self.omlb = A.alloc([2, 8])
        self.kwin = A.alloc([4, 2 * LC - 1]); self.R_kwin = [Reg() for _ in range(4)]
        self.hx_off = A.off
        self.hxT = A.alloc([8, T], BF16)
        self.R_hx = [Reg() for _ in range(NT)]
        self.persist_mark = A.off
        P.dma(self.sv, self.smallv_in, wr=[self.R_sv])
        P.dma(self.identF, self.c_identF, wrp=[self.R_const])
        P.dma(self.identB, self.c_identB, wrp=[self.R_const])
        P.dma(self.masks, self.c_masks, wrp=[self.R_const])
        P.dma(self.FA2, self.c_FA2, wrp=[self.R_const])
        P.dma(self.RA2, self.c_RA2, wrp=[self.R_const])
        P.g(lambda e: e.memset(self.onesF, 1.0), wrp=[self.R_const])
        P.g(lambda e: e.memset(self.onesB, 1.0), wrp=[self.R_const])

    def svv(self, name, *idx_shape):
        o, w = _SV[name]
        return self.sv[:, o:o + w]

    def barrier(self, name=None):
        P = self.P
        if not hasattr(self, "marks"):
            self.marks = []
        self.marks.append((name or "b%d" % len(self.marks), {e: len(P.q[e]) - 1 for e in ("pe", "dve", "act", "pool")}))
        toks = [t for t in (P.last_real(e) for e in ("pe", "dve", "act", "pool")) if t is not None]
        toks += [t for t in P.dma_last if t is not None]
        for e in ("pe", "dve", "act", "pool", "sp"):
            w = P._waits(e, [t for t in toks if t.eng != e])
            if w:
                P.q[e].append(_Rec(None, w))

    def dump(self, name, ap, rd=(), shape=None, dt=F32):
        if name not in self.dbg:
            return
        if len(ap.shape) == 3:
            ap = ap.rearrange("p a b -> p (a b)")
        elif len(ap.shape) == 4:
            ap = ap.rearrange("p a b c -> p (a b c)")
        shp = list(ap.shape)
        o = self.nc.dram_tensor("dbg_" + name, shp, dt, kind="ExternalOutput").ap()
        self.dbg_outs[name] = "dbg_" + name
        self.P.dma(o, ap, rd=list(rd))

    def adaln(self, stage=9):
        P, A = self.P, self.ar
        mark = A.off
        scv = A.alloc([8, 2]); R_scv = Reg()
        o, _ = _SV["cvT"]
        cv = self.sv[:, o:o + 16].rearrange("p (k r) -> p k r", k=8)
        P.a(lambda e: e.activation(scv, cv, AF.Silu), rd=[self.R_sv], wr=[R_scv])
        wst = [A.alloc([8, 512]) for _ in range(2)]
        R_w = [Reg(), Reg()]
        ob, _ = _SV["bada"]
        bada = self.sv[:, ob:ob + 48].rearrange("p (l j) -> p l j", l=2)
        R_ps = Reg()
        n = 0
        for l in range(self.nlayers):
            wv = self.w_ada[l].rearrange("(k p) n -> p k n", p=128)
            for pc in range(6):
                wb, rw = wst[n % 2], R_w[n % 2]
                n += 1
                P.dma(wb, wv[:, :, pc * 512:(pc + 1) * 512], wr=[rw])
                for jj in range(4):
                    j = pc * 4 + jj
                    for k in range(8):
                        P.t(lambda e, wb=wb, jj=jj, k=k, j=j: e.matmul(self.pb[0][:, 2 * j:2 * j + 2], wb[:, k, jj * 128:(jj + 1) * 128],
                                                                      scv[:, k, :], start=(k == 0), stop=(k == 7)),
                            rd=[rw, R_scv], xb=[self.R_pb[0]])
            if stage < 3:
                continue
            P.v(lambda e, l=l: e.tensor_tensor(self.mod[:, l], self.pb[0][:, 0:48].rearrange("p (j r) -> p j r", j=24),
                                               bada[:, l, :].unsqueeze(2).to_broadcast([128, 24, 2]), ALU.add),
                rd=[self.R_sv], wrp=[self.R_mod], xb=[self.R_pb[0]])
            on, _ = _SV["normw"]
            nw = self.sv[:, on:on + 16].rearrange("p (l k) -> p l k", l=2)
            P.v(lambda e, l=l: e.scalar_tensor_tensor(self.weff[:, l], self.mod[:, l, 8:16, :], 1.0,
                                                      nw[:, l, :].unsqueeze(2).to_broadcast([128, 8, 2]), ALU.add, ALU.mult),
                rd=[self.R_mod, self.R_sv], wrp=[self.R_mod])
        if stage < 4:
            self.barrier(); A.off = mark; return
        ol, _ = _SV["lbl"]
        lbl = self.sv[:, ol:ol + 16].rearrange("p (l x) -> p l x", l=2)
        P.g(lambda e: e.memset(self.lbv[:, 0, :], 0.0), wrp=[self.R_mod])
        P.g(lambda e: e.memset(self.omlb[:, 0, :], 1.0), wrp=[self.R_mod])
        if self.nlayers > 1:
            dl = A.alloc([8]); R_dl = Reg()
            P.v(lambda e: e.tensor_tensor(dl, lbl[:, 1, :], lbl[:, 0, :], ALU.subtract), rd=[self.R_sv], wr=[R_dl])
            P.a(lambda e: e.activation(self.lbv[:, 1, :], dl, AF.Sigmoid), rd=[R_dl], wrp=[self.R_mod])
            P.a(lambda e: e.activation(self.omlb[:, 1, :], dl, AF.Sigmoid, scale=-1.0), rd=[R_dl], wrp=[self.R_mod])
        self.dump("mod", self.mod, rd=[self.R_mod])
        self.barrier()
        A.off = mark

    def tile_to_hx(self, l, tt, xt, R_xt, bufs):
        P = self.P
        seg = 1 if tt < 2 else 0
        junk, ssq, rstd, xn, R_t = bufs
        P.a(lambda e: e.activation(junk, xt, AF.Square, accum_out=ssq), rd=[R_xt], wr=[R_t])
        P.a(lambda e: e.activation(ssq, ssq, AF.Sqrt, scale=1.0 / D, bias=EPS), wr=[R_t])
        P.v(lambda e: e.reciprocal(rstd, ssq), wr=[R_t])
        xnb = junk
        P.v(lambda e: e.tensor_scalar(xnb, xt, rstd, None, ALU.mult), rd=[R_xt], wr=[R_t])
        for half in range(2):
            bi_ = 1 + half
            bank, R_b = self.pb[bi_], self.R_pb[bi_]
            for kk in range(4):
                k = half * 4 + kk
                P.t(lambda e, k=k, kk=kk, bank=bank: e.transpose(bank[:, 0:256].bitcast(BF16)[:, kk * 128:(kk + 1) * 128], xnb[:, k * 128:(k + 1) * 128], self.identB),
                    rd=[R_t, self.R_const], xb=[R_b])
            for kk in range(4):
                k = half * 4 + kk
                dst = self.hxT[:, k, tt * 128:(tt + 1) * 128]
                src = bank[:, 0:256].bitcast(BF16)[:, kk * 128:(kk + 1) * 128]
                sc = self.weff[:, l, k, seg:seg + 1]
                bi = self.mod[:, l, k, seg:seg + 1]
                if half == 0:
                    P.a(lambda e, dst=dst, src=src, sc=sc, bi=bi: e.activation(dst, src, AF.Identity, bias=bi, scale=sc),
                        rd=[self.R_mod], wrp=[self.R_hx[tt]], xb=[R_b])
                else:
                    P.v(lambda e, dst=dst, src=src, sc=sc, bi=bi: e.tensor_scalar(dst, src, sc, bi, ALU.mult, ALU.add),
                        rd=[self.R_mod], wrp=[self.R_hx[tt]], xb=[R_b])

    def phase_b_from_dram(self, l, ntiles=NT, stage=9):
        P, A = self.P, self.ar
        mark = A.off
        xts = [A.alloc([D]) for _ in range(2)]
        R_x = [Reg(), Reg()]
        bufs = []
        for i in range(2):
            bufs.append((A.alloc([D], BF16), A.alloc([1]), A.alloc([1]), A.alloc([D]), Reg()))
        for tt in range(ntiles):
            src = self.ctx_in[tt * 128:(tt + 1) * 128, :] if tt < 2 else self.x_in[(tt - 2) * 128:(tt - 1) * 128, :]
            P.dma(xts[tt % 2], src, wr=[R_x[tt % 2]])
            self.tile_to_hx(l, tt, xts[tt % 2], R_x[tt % 2], bufs[tt % 2])
        self.barrier()
        A.off = mark

    def load_w_slice(self, l, col0, dst_bf, R_dst, stg, R_stg, eng="dve"):
        P = self.P
        wv = self.w_in[l].rearrange("(k p) n -> p k n", p=128)
        P.dma(stg, wv[:, :, col0:col0 + 128], wr=[R_stg])
        fn = lambda e: e.tensor_copy(dst_bf, stg)
        P.op(eng, fn, rd=[R_stg], wr=[R_dst])

    def proj(self, bank_ap, R_bank, w_bf, R_w, c0, n):
        P = self.P
        t0, t1 = c0 // 128, (c0 + n - 1) // 128
        rds = [R_w] + [self.R_hx[t] for t in range(t0, t1 + 1)]
        for k in range(8):
            P.t(lambda e, k=k: e.matmul(bank_ap, w_bf[:, k, :], self.hxT[:, k, c0:c0 + n], start=(k == 0), stop=(k == 7)),
                rd=rds, xb=[R_bank])

    def hgrn2(self, l, need_ctx_out, heads=(0, 1, 2, 3)):
        P, A = self.P, self.ar
        mark = A.off
        pb, Rpb = self.pb, self.R_pb
        stg = [A.alloc([8, 128]) for _ in range(2)]; R_stg = [Reg(), Reg()]
        wbf = [A.alloc([8, 128], BF16) for _ in range(5)]; R_w = [Reg() for _ in range(5)]
        vtok = A.alloc([NT, 128], BF16); R_v = Reg()
        o_f = A.alloc([T]); R_of = [Reg() for _ in range(NB)]
        qsb = A.alloc([T], BF16); R_qsb = [Reg() for _ in range(NB)]
        S = [A.alloc([128]) for _ in range(3)]; R_S = [Reg(), Reg(), Reg()]
        U = [A.alloc([128], BF16) for _ in range(4)]; R_U = [Reg() for _ in range(4)]
        sct = [A.alloc([64], BF16) for _ in range(4)]; R_sct = [Reg() for _ in range(4)]
        scm = [A.alloc([64], BF16) for _ in range(4)]; R_scm = [Reg() for _ in range(4)]
        sets = []
        for _ in range(4):
            d_ = {}
            for nm in ("kk", "g", "p", "pe", "qs", "E", "ex", "exn", "exh", "sgA"):
                d_[nm] = A.alloc([BLK])
            for nm in ("Qt", "Kh", "res", "KtP0", "KtP1", "KtZ0", "KtZ1"):
                d_[nm] = A.alloc([BLK], BF16)
            d_["Khz"] = [A.alloc([2, 128], BF16) for _ in range(2)]
            for nm in ("L1", "sgq"):
                d_[nm] = A.alloc([BLK])
            d_["ad"] = A.alloc([4, 2]); d_["t4"] = A.alloc([4])
            d_["R"] = {nm: Reg() for nm in ("kk", "g", "p", "pe", "qs", "e", "qt", "kt", "ktz", "kh", "khtok", "ad", "sga", "o")}
            sets.append(d_)
        og, _ = _SV["gnw"]
        ones64 = self.onesF[:, 0:64]
        nstg = 0
        rmask = A.alloc([BLK]); R_rmask = Reg()
        P.g(lambda e: e.memset(rmask, 1.0), wr=[R_rmask])
        for c_ in range(4):
            P.g(lambda e: e.memset(rmask[:, c_ * 64:c_ * 64 + 1], 0.0), wr=[R_rmask])
        for st_ in sets:
            for nm in ("KtP0", "KtP1"):
                P.g(lambda e: e.memset(st_[nm], 0.0), wr=[st_["R"]["kt"]])
            for kz in st_["Khz"]:
                P.g(lambda e: e.memset(kz, 0.0), wr=[st_["R"]["khtok"]])
        for ci in range(4):
            P.g(lambda e: e.memset(scm[ci], 0.0), wr=[R_scm[ci]])
        for h in heads:
            for si in range(5):
                self.load_w_slice(l, si * 512 + h * 128, wbf[si], R_w[si], stg[nstg % 2], R_stg[nstg % 2])
                nstg += 1
            def vtok_gen():
                for g4 in range((NT + 3) // 4):
                    bk = 6 + g4 % 2
                    tiles = list(range(g4 * 4, min(NT, g4 * 4 + 4)))
                    for j, tt in enumerate(tiles):
                        for k in range(8):
                            P.t(lambda e: e.matmul(pb[bk][:, j * 128:(j + 1) * 128], self.hxT[:, k, tt * 128:(tt + 1) * 128],
                                                   wbf[3][:, k, :], start=(k == 0), stop=(k == 7)),
                                rd=[R_w[3], self.R_hx[tt]], xb=[Rpb[bk]])
                        yield
                    n = len(tiles) * 128
                    dst = vtok[:, tiles[0]:tiles[0] + len(tiles), :]
                    src = pb[bk][:, 0:n].rearrange("p (a b) -> p a b", b=128)
                    if g4 % 2 == 0:
                        P.a(lambda e: e.activation(dst, src, AF.Copy), wrp=[R_v], xb=[Rpb[bk]])
                    else:
                        P.v(lambda e: e.tensor_copy(dst, src), wrp=[R_v], xb=[Rpb[bk]])
                    yield
            vtg = vtok_gen()
            for d in (0, 1):
                oml = self.omlb[:, l, d * 4 + h:d * 4 + h + 1]
                P.g(lambda e: e.memset(S[0], 0.0), wr=[R_S[0]])
                for st_ in sets:
                    for nm in ("KtZ0", "KtZ1"):
                        P.g(lambda e: e.memset(st_[nm], 0.0), wr=[st_["R"]["ktz"]])
                order = list(range(NB)) if d == 0 else [0] + list(range(NB - 1, 0, -1))
                gcs = [0]

                def prep0(bi, blk):
                    bp = bi % 4
                    need_out = blk > 0 or need_ctx_out
                    c0 = blk * BLK
                    self.proj(pb[bp][:, 0:BLK], Rpb[bp], wbf[1 + d], R_w[1 + d], c0, BLK)
                    if need_out and d == 0:
                        self.proj(pb[bp][:, BLK:2 * BLK], Rpb[bp], wbf[0], R_w[0], c0, BLK)

                def prep1(bi, blk):
                    st = sets[bi % 4]; R = st["R"]
                    bp = bi % 4
                    need_out = blk > 0 or need_ctx_out
                    c0 = blk * BLK
                    kk, g, p, pe, qs, E, ex, exn, exh = (st[n_] for n_ in ("kk", "g", "p", "pe", "qs", "E", "ex", "exn", "exh"))
                    Qt, Kh, Khz, ad, t4, L1, sgq = st["Qt"], st["Kh"], st["Khz"], st["ad"], st["t4"], st["L1"], st["sgq"]
                    p3 = p.rearrange("p (c t) -> p c t", c=4); g3 = g.rearrange("p (c t) -> p c t", c=4)
                    pe3 = pe.rearrange("p (c t) -> p c t", c=4)
                    P.a(lambda e: e.activation(kk, pb[bp][:, 0:BLK], AF.Exp), wr=[R["kk"]], xb=[Rpb[bp]])
                    yield
                    P.a(lambda e: e.activation(L1, kk, AF.Ln, bias=1.0), rd=[R["kk"]], wr=[R["g"]])
                    yield
                    P.a(lambda e: e.activation(kk, L1, AF.Exp, scale=-1.0), rd=[R["g"]], wr=[R["kk"]])
                    yield
                    if need_out and d == 0:
                        P.a(lambda e: e.activation(sgq, pb[bp][:, BLK:2 * BLK], AF.Exp, scale=-1.0), wr=[R["qs"]], xb=[Rpb[bp]])
                        yield
                        P.a(lambda e: e.activation(sgq, sgq, AF.Ln, bias=1.0), wr=[R["qs"]])
                        yield
                        P.a(lambda e: e.activation(sgq, sgq, AF.Exp, scale=-1.0), wr=[R["qs"]])
                        yield
                        P.v(lambda e: e.tensor_tensor(qsb[:, c0:c0 + BLK], pb[bp][:, BLK:2 * BLK], sgq, ALU.mult), rd=[R["qs"]], wr=[R_qsb[blk]], xb=[Rpb[bp]])
                        yield
                    P.v(lambda e: e.tensor_scalar(kk, kk, oml, None, ALU.mult), rd=[self.R_mod], wr=[R["kk"]])
                    yield
                    P.a(lambda e: e.activation(g, kk, AF.Ln, scale=-1.0, bias=1.0), rd=[R["kk"]], wr=[R["g"]])
                    yield
                    P.v(lambda e: e.tensor_tensor_scan(p, rmask, g, 0.0, ALU.mult, ALU.add), rd=[R_rmask, R["g"]], wr=[R["p"]])
                    yield
                    if d == 0:
                        base3 = p3
                        mid = 31
                        P.v(lambda e: e.tensor_tensor(exh.rearrange("p (c t) -> p c t", c=4), p3[:, :, 63:64].to_broadcast([128, 4, 64]), p3, ALU.subtract),
                            rd=[R["p"]], wr=[R["kh"]])
                        yield
                        P.a(lambda e: e.activation(exh, exh, AF.Exp), wr=[R["kh"]])
                        yield
                        P.a(lambda e: e.activation(ad, p3[:, :, 31::32], AF.Exp), rd=[R["p"]], wr=[R["ad"]])
                        yield
                    else:
                        base3 = pe3
                        mid = 32
                        P.g(lambda e: e.tensor_tensor(pe, p, g, ALU.subtract), rd=[R["g"], R["p"]], wr=[R["pe"]])
                        yield
                        P.a(lambda e: e.activation(exh, pe, AF.Exp), rd=[R["pe"]], wr=[R["kh"]])
                        yield
                        P.v(lambda e: e.tensor_tensor(t4, p3[:, :, 63], pe3[:, :, 32], ALU.subtract), rd=[R["p"], R["pe"]], wr=[R["ad"]])
                        yield
                        P.a(lambda e: e.activation(ad[:, :, 0], t4, AF.Exp), wr=[R["ad"]])
                        yield
                        P.a(lambda e: e.activation(ad[:, :, 1], p3[:, :, 63], AF.Exp), rd=[R["p"]], wr=[R["ad"]])
                        yield
                    P.g(lambda e: e.tensor_tensor(Kh, kk, exh, ALU.mult), rd=[R["kk"]], wr=[R["kh"]])
                    yield
                    if need_out:
                        base = p if d == 0 else pe
                        rbase = R["p"] if d == 0 else R["pe"]
                        P.v(lambda e: e.tensor_tensor(E.rearrange("p (c t) -> p c t", c=4), base3, base3[:, :, mid:mid + 1].to_broadcast([128, 4, 64]), ALU.subtract),
                            rd=[rbase], wr=[R["e"]])
                        yield
                        P.a(lambda e: e.activation(ex, E, AF.Exp), wr=[R["e"]])
                        yield
                        P.a(lambda e: e.activation(exn, E, AF.Exp, scale=-1.0), wr=[R["e"]])
                        yield
                        eq, ek = (ex, exn) if d == 0 else (exn, ex)
                        P.g(lambda e: e.tensor_tensor(Qt, qsb[:, c0:c0 + BLK], eq, ALU.mult), rd=[R_qsb[blk], R["e"]], wr=[R["qt"]])
                        yield
                        hs = slice(0, 32) if d == 0 else slice(32, 64)
                        v5 = lambda a_: a_.rearrange("p (a b t) -> p a b t", a=2, b=2)
                        for hh_ in range(2):
                            P.g(lambda e: e.tensor_tensor(v5(st["KtP%d" % hh_])[:, :, hh_, :], v5(kk)[:, :, hh_, :], v5(ek)[:, :, hh_, :], ALU.mult),
                                rd=[R["kk"], R["e"]], wrp=[R["kt"]])
                            yield
                            P.g(lambda e: e.tensor_tensor(v5(st["KtZ%d" % hh_])[:, :, hh_, hs], v5(kk)[:, :, hh_, hs], v5(ek)[:, :, hh_, hs], ALU.mult),
                                rd=[R["kk"], R["e"]], wrp=[R["ktz"]])
                            yield

                def prep2(bi, blk):
                    st = sets[bi % 4]; R = st["R"]
                    bp = bi % 4
                    need_out = blk > 0 or need_ctx_out
                    c0 = blk * BLK
                    Kh, Khz, sgq = st["Kh"], st["Khz"], st["sgq"]
                    khps = pb[bp][:, 0:128].bitcast(BF16).rearrange("p (a b) -> p a b", a=2)
                    for ti in range(2):
                        P.t(lambda e, ti=ti: e.transpose(khps[:, ti, :], Kh[:, ti * 128:(ti + 1) * 128], self.identB),
                            rd=[R["kh"], self.R_const], xb=[Rpb[bp]])
                        yield
                    P.v(lambda e: e.tensor_copy(Khz[0][0:64], khps[0:64]), wrp=[R["khtok"]], xb=[Rpb[bp]])
                    yield
                    P.v(lambda e: e.tensor_copy(Khz[1][64:128], khps[64:128]), wrp=[R["khtok"]], xb=[Rpb[bp]])
                    yield
                    if d == 1 and need_out:
                        self.proj(pb[bp][:, BLK:2 * BLK], Rpb[bp], wbf[4], R_w[4], c0, BLK)
                        yield
                        P.a(lambda e: e.activation(sgq, pb[bp][:, BLK:2 * BLK], AF.Exp, scale=-1.0), rd=[R["qs"]], wr=[R["sga"]], xb=[Rpb[bp]])
                        yield
                        P.a(lambda e: e.activation(sgq, sgq, AF.Ln, bias=1.0), wr=[R["sga"]])
                        yield
                        P.a(lambda e: e.activation(sgq, sgq, AF.Exp, scale=-1.0), wr=[R["sga"]])
                        yield
                        P.v(lambda e: e.tensor_tensor(st["sgA"], pb[bp][:, BLK:2 * BLK], sgq, ALU.mult), wr=[R["sga"]], xb=[Rpb[bp]])
                        yield

                def chain(bi, blk):
                    st = sets[bi % 4]; R = st["R"]
                    bp = bi % 2
                    need_out = blk > 0 or need_ctx_out
                    c0 = blk * BLK
                    Qt, Khz, ad = st["Qt"], st["Khz"], st["ad"]
                    ob = 6 + bp
                    cis = (0, 1, 2, 3) if d == 0 else (3, 2, 1, 0)
                    if need_out:
                        for ci in cis:
                            hh = ci % 2
                            r0, r1 = 64 * hh, 64 * hh + 64
                            cs = slice(ci * 64, ci * 64 + 64)
                            sb_ = 5
                            cA = slice(ci * 64, ci * 64 + 32)
                            cB = slice(ci * 64 + 32, ci * 64 + 64)
                            c1, o1, c2, o2 = (cB, slice(32, 64), cA, slice(0, 32)) if d == 0 else (cA, slice(0, 32), cB, slice(32, 64))
                            tcs = slice((ci // 2) * 128, (ci // 2) * 128 + 128)
                            P.t(lambda e: e.matmul(pb[sb_][:, o1], st["KtP%d" % hh][:, tcs], Qt[:, c1], start=True, stop=True),
                                rd=[R["kt"], R["qt"]], xb=[Rpb[sb_]])
                            yield
                            P.t(lambda e: e.matmul(pb[sb_][:, o2], st["KtZ%d" % hh][:, tcs], Qt[:, c2], start=True, stop=True),
                                rd=[R["ktz"], R["qt"]], xb=[Rpb[sb_]])
                            yield
                            P.v(lambda e: e.tensor_copy(sct[ci][r0:r1, :], pb[sb_][r0:r1, 0:64]), wr=[R_sct[ci]], xb=[Rpb[sb_]])
                            yield
                            P.g(lambda e: e.tensor_tensor(scm[ci][r0:r1, :], sct[ci][r0:r1, :], self.masks[r0:r1, d, :], ALU.mult),
                                rd=[R_sct[ci], self.R_const], wrp=[R_scm[ci]])
                            yield
                    for ci in cis:
                        gc = gcs[0]
                        gcs[0] += 1
                        cp = gc % 2
                        Sc, Sn = S[gc % 3], S[(gc + 1) % 3]
                        RSc, RSn = R_S[gc % 3], R_S[(gc + 1) % 3]
                        ti, hh = ci // 2, ci % 2
                        tile = blk * 2 + ti
                        r0, r1 = 64 * hh, 64 * hh + 64
                        dsb, Rdsb = (pb[4][:, 0:128], Rpb[4]) if cp == 0 else (pb[4][:, 128:256], Rpb[4])
                        P.t(lambda e: e.matmul(dsb, Khz[hh][:, ti, :], vtok[:, tile, :], start=True, stop=True),
                            rd=[R["khtok"], R_v], xb=[Rdsb])
                        yield
                        if need_out:
                            P.a(lambda e: e.activation(U[ci], Sc, AF.Copy, scale=ad[:, ci, 0:1]), rd=[RSc, R["ad"]], wr=[R_U[ci]])
                            yield
                        P.v(lambda e: e.scalar_tensor_tensor(Sn, Sc, ad[:, ci, 1:2], dsb, ALU.mult, ALU.add),
                            rd=[RSc, R["ad"]], wr=[RSn], xb=[Rdsb])
                        yield
                    if need_out:
                        for ci in cis:
                            ti, hh = ci // 2, ci % 2
                            tile = blk * 2 + ti
                            r0, r1 = 64 * hh, 64 * hh + 64
                            cs = slice(ci * 64, ci * 64 + 64)
                            P.t(lambda e: e.matmul(pb[ob][:, cs], vtok[:, tile, :], scm[ci][:, :], start=True, stop=False),
                                rd=[R_v, R_scm[ci]], xb=[Rpb[ob]])
                            yield
                            P.t(lambda e: e.matmul(pb[ob][:, cs], U[ci], Qt[:, cs], start=False, stop=True),
                                rd=[R_U[ci], R["qt"]], xb=[Rpb[ob]])
                            yield
                    if need_out:
                        cols = slice(c0, c0 + BLK)
                        if d == 0:
                            P.a(lambda e, cols=cols, ob=ob: e.activation(o_f[:, cols], pb[ob][:, 0:BLK], AF.Copy), wr=[R_of[blk]], xb=[Rpb[ob]])
                            yield
                        else:
                            osum, sq, rs, res = st["E"], st["ex"], st["exn"], st["res"]
                            P.v(lambda e, cols=cols, ob=ob: e.tensor_tensor(osum, o_f[:, cols], pb[ob][:, 0:BLK], ALU.add), rd=[R_of[blk]], wr=[R["e"]], xb=[Rpb[ob]])
                            yield
                            P.v(lambda e: e.tensor_tensor(res, osum, osum, ALU.mult), rd=[R["e"]], wr=[R["o"]])
                            yield
                            P.t(lambda e, ob=ob: e.matmul(pb[ob][:, BLK:2 * BLK], self.onesB, res, start=True, stop=True), rd=[R["o"], self.R_const], xb=[Rpb[ob]])
                            yield
                            P.a(lambda e, ob=ob: e.activation(rs, pb[ob][:, BLK:2 * BLK], AF.Ln, scale=1.0 / 128, bias=EPS), wr=[R["e"]], xb=[Rpb[ob]])
                            yield
                            P.a(lambda e: e.activation(rs, rs, AF.Exp, scale=-0.5), wr=[R["e"]])
                            yield
                            P.v(lambda e: e.tensor_tensor(osum, osum, rs, ALU.mult), wr=[R["e"]])
                            yield
                            gn = self.sv[:, og + l * 4 + h:og + l * 4 + h + 1]
                            P.v(lambda e, gn=gn: e.scalar_tensor_tensor(res, osum, gn, st["sgA"], ALU.mult, ALU.mult), rd=[R["e"], R["sga"], self.R_sv], wr=[R["o"]])
                            yield
                            P.dma(self.MIX[h, :, cols], res, rd=[R["o"]], wr=[self.R_MIX[h][blk]])
                            yield

                order_ = order

                def zipgen(gs):
                    gs = list(gs)
                    while gs:
                        for g_ in list(gs):
                            try:
                                next(g_)
                                yield
                            except StopIteration:
                                gs.remove(g_)

                npairs = (len(order_) + 1) // 2

                def pair_idx(j):
                    return [b_ for b_ in (2 * j, 2 * j + 1) if b_ < len(order_)]

                def stageA(j):
                    for b_ in pair_idx(j):
                        prep0(b_, order_[b_])
                    yield
                    yield from zipgen([prep1(b_, order_[b_]) for b_ in pair_idx(j)])

                def stageBC(j):
                    for b_ in pair_idx(j):
                        yield from prep2(b_, order_[b_])
                    for b_ in pair_idx(j):
                        yield from chain(b_, order_[b_])

                ga = stageA(0)
                if d == 0:
                    gl = [ga, vtg]
                    while gl:
                        for g_ in list(gl):
                            try:
                                next(g_)
                            except StopIteration:
                                gl.remove(g_)
                else:
                    for _ in ga:
                        pass
                def drain_weighted(gw):
                    gw = [[g_, w_] for g_, w_ in gw]
                    while gw:
                        for it in list(gw):
                            for _ in range(it[1]):
                                try:
                                    next(it[0])
                                except StopIteration:
                                    gw.remove(it)
                                    break

                for j in range(npairs):
                    gw = [(stageBC(j), HG_W)]
                    if j + 1 < npairs:
                        gw.insert(0, (stageA(j + 1), 1))
                    drain_weighted(gw)
        self.barrier()
        A.off = mark

    def load_tm(self, k0, nb, m0, buf, R_buf, q="sp"):
        self.P.dma(buf[:, 0:nb], self.c_TM[:, k0:k0 + nb, m0:m0 + 3, :], wr=[R_buf], q=q)

    def fft_stepA(self, Xs, R_xs, K, A_sb, R_a):
        P, pb, Rpb = self.P, self.pb, self.R_pb
        for c8 in range(16):
            bk = c8 % 2
            for j in range(8):
                cl = c8 * 8 + j
                q, cc = cl // 64, cl % 64
                P.t(lambda e: e.matmul(pb[bk][:, j * 64:(j + 1) * 64], Xs[64 * q:64 * q + K, cc, :], self.FA2[64 * q:64 * q + K, :], start=True, stop=True),
                    rd=[R_xs, self.R_const], xb=[Rpb[bk]])
            dst = A_sb[:, :, c8 * 8:(c8 + 1) * 8].rearrange("p k c -> p c k")
            src = pb[bk][:, 0:512].rearrange("p (c k) -> p c k", c=8)
            if bk == 0:
                P.a(lambda e: e.activation(dst, src, AF.Copy), wrp=[R_a], xb=[Rpb[bk]])
            else:
                P.v(lambda e: e.tensor_copy(dst, src), wrp=[R_a], xb=[Rpb[bk]])

    def fft_stepC(self, src_sb, R_src, inverse, tmb, R_tm, consume, tmq="sp"):
        P, pb, Rpb = self.P, self.pb, self.R_pb
        im_off = 33 if inverse else 32
        for bi, k0 in enumerate(range(0, 33, 4)):
            nb = min(4, 33 - k0)
            tm, rtm = tmb[bi % 2], R_tm[bi % 2]
            self.load_tm(k0, nb, 3 if inverse else 0, tm, rtm, q=tmq)
            br, bim = 2 + 2 * (bi % 2), 3 + 2 * (bi % 2)
            for j in range(nb):
                k1 = k0 + j
                real_only = k1 in (0, 32)
                re = src_sb[:, k1, :]
                im = None if (real_only and not inverse) else src_sb[:, im_off + k1, :]
                cs = slice(j * 128, (j + 1) * 128)
                Tc, Ts, mTs = tm[:, j, 0, :], tm[:, j, 1, :], tm[:, j, 2, :]
                if not inverse:
                    P.t(lambda e: e.matmul(pb[br][:, cs], Tc, re, start=True, stop=real_only), rd=[R_src, rtm], xb=[Rpb[br]])
                    if not real_only:
                        P.t(lambda e: e.matmul(pb[br][:, cs], mTs, im, start=False, stop=True), rd=[R_src, rtm], xb=[Rpb[br]])
                    P.t(lambda e: e.matmul(pb[bim][:, cs], Ts, re, start=True, stop=real_only), rd=[R_src, rtm], xb=[Rpb[bim]])
                    if not real_only:
                        P.t(lambda e: e.matmul(pb[bim][:, cs], Tc, im, start=False, stop=True), rd=[R_src, rtm], xb=[Rpb[bim]])
                else:
                    P.t(lambda e: e.matmul(pb[br][:, cs], Tc, re, start=True, stop=False), rd=[R_src, rtm], xb=[Rpb[br]])
                    P.t(lambda e: e.matmul(pb[br][:, cs], Ts, im, start=False, stop=True), rd=[R_src, rtm], xb=[Rpb[br]])
                    if not real_only:
                        P.t(lambda e: e.matmul(pb[bim][:, cs], mTs, re, start=True, stop=False), rd=[R_src, rtm], xb=[Rpb[bim]])
                        P.t(lambda e: e.matmul(pb[bim][:, cs], Tc, im, start=False, stop=True), rd=[R_src, rtm], xb=[Rpb[bim]])
            consume(bi, k0, nb, pb[br], pb[bim], Rpb[br], Rpb[bim])

    def load_xs(self, Xs, R_xs, src_rows, n1cnt, zero_first=False, rd=()):
        P = self.P
        if zero_first:
            P.g(lambda e: e.memset(Xs, 0.0), wr=[R_xs])
        for q in range(2):
            src = src_rows[64 * q:64 * q + 64, :].rearrange("c (n1 n2) -> n1 c n2", n2=128)
            if zero_first:
                P.dma(Xs[64 * q:64 * q + n1cnt, :, :], src, rd=list(rd), wrp=[R_xs])
            else:
                P.dma(Xs[64 * q:64 * q + n1cnt, :, :], src, rd=list(rd), wrp=[R_xs])

    def hy_filter(self, l, seg, alias_hx=False):
        P, A, pb, Rpb = self.P, self.ar, self.pb, self.R_pb
        mark = A.off
        if alias_hx:
            A.off = self.hx_off
        nbuf = 2 if alias_hx else 1
        Ln = L if seg == "x" else LC
        CH = min(512, Ln)
        nch = Ln // CH
        Ha = A.alloc([Ln]); R_Ha = Reg()
        mark2 = A.off
        zz = A.alloc([Ln]); R_zz = Reg()
        Hb = A.alloc([Ln]); R_Hb = Reg()
        W1 = A.alloc([128]); W2 = A.alloc([128]); W3 = A.alloc([128]); W4f = A.alloc([512]); W4b = A.alloc([512])
        R_W = Reg()
        frb = A.alloc([4]); R_frb = Reg()
        argb = [A.alloc([CH]) for _ in range(4)]; mb = [A.alloc([CH]) for _ in range(4)]; R_arg = [Reg() for _ in range(4)]
        for w_ in (W1, W2, W3, W4f, W4b):
            P.g(lambda e: e.memset(w_, 0.0), wr=[R_W])
        P.dma(W1[0:33, 0:64], self.hy_w1[l], wr=[R_W]); P.dma(W1[33:66, 64:128], self.hy_w1[l], wr=[R_W])
        P.dma(W2[0:64, 0:64], self.hy_w2[l], wr=[R_W]); P.dma(W2[64:128, 64:128], self.hy_w2[l], wr=[R_W])
        P.dma(W3[0:64, 0:64], self.hy_w3[l], wr=[R_W]); P.dma(W3[64:128, 64:128], self.hy_w3[l], wr=[R_W])
        P.dma(W4f[0:64, :], self.hy_w4[l][:, 0:512], wr=[R_W]); P.dma(W4b[64:128, :], self.hy_w4[l][:, 512:1024], wr=[R_W])
        P.dma(zz[0:66, :], self.c_ZZ[seg], wr=[R_zz])
        ofr = _SV["hfr"][0] + l
        fr = self.sv[:, ofr:ofr + 1]
        for i, nm in enumerate(("hb1", "hb2", "hb3")):
            ob = _SV[nm][0] + l
            P.v(lambda e: e.tensor_tensor(frb[:, i:i + 1], self.sv[:, ob:ob + 1], fr, ALU.mult), rd=[self.R_sv], wrp=[R_frb])
        layers = [(W1, zz, R_zz, 66, Ha, R_Ha), (W2, Ha, R_Ha, 128, Hb, R_Hb), (W3, Hb, R_Hb, 128, Ha, R_Ha)]
        n = 0
        for li, (W, src, R_src, Kc, dst, R_dst) in enumerate(layers):
            for ch in range(nch):
                cs = slice(ch * CH, (ch + 1) * CH)
                bk = n % 4
                arg, m_, ra = argb[n % 4], mb[n % 4], R_arg[n % 4]
                n += 1
                P.t(lambda e: e.matmul(pb[bk][:, 0:CH], W[0:Kc, :], src[0:Kc, cs], start=True, stop=True), rd=[R_W, R_src], xb=[Rpb[bk]])
                P.v(lambda e: e.tensor_scalar(arg, pb[bk][:, 0:CH], fr, frb[:, li:li + 1], ALU.mult, ALU.add), rd=[self.R_sv, R_frb], wr=[ra], xb=[Rpb[bk]])
                P.v(lambda e: e.tensor_scalar(m_, arg, PI, None, ALU.is_gt), wr=[ra])
                P.v(lambda e: e.scalar_tensor_tensor(arg, m_, -2 * PI, arg, ALU.mult, ALU.add), wr=[ra])
                P.v(lambda e: e.tensor_scalar(m_, arg, -PI, None, ALU.is_lt), wr=[ra])
                P.v(lambda e: e.scalar_tensor_tensor(arg, m_, 2 * PI, arg, ALU.mult, ALU.add), wr=[ra])
                P.a(lambda e: e.activation(dst[:, cs], arg, AF.Sin), rd=[ra], wrp=[R_dst])
        self.barrier()
        A.off = mark2
        H3, R_H3 = Ha, R_Ha
        W4f2 = A.alloc([512]); W4b2 = A.alloc([512]); R_W2 = Reg()
        P.g(lambda e: e.memset(W4f2, 0.0), wr=[R_W2]); P.g(lambda e: e.memset(W4b2, 0.0), wr=[R_W2])
        P.dma(W4f2[0:64, :], self.hy_w4[l][:, 0:512], wr=[R_W2]); P.dma(W4b2[64:128, :], self.hy_w4[l][:, 512:1024], wr=[R_W2])
        gb = []
        for _ in range(nbuf):
            gb.append(dict(HF=A.alloc([Ln]), HB=A.alloc([Ln]), R_HF=Reg(), R_HB=Reg(), kfb=A.alloc([NFFT], BF16), R_kfb=Reg(),
                           Xs=A.alloc([64, 128], BF16), R_xs=Reg(), nrm=A.alloc([4]), R_nrm=Reg()))
        ttb = [A.alloc([2, CH]) for _ in range(2)]; R_tt = [Reg(), Reg()]
        dcb = [A.alloc([2, CH]) for _ in range(2)]; R_dc = [Reg(), Reg()]
        tmb = [A.alloc([4, 3, 128], BF16) for _ in range(2)]; R_tm = [Reg(), Reg()]
        evb = [A.alloc([2, 512]) for _ in range(2)]; R_ev = [Reg(), Reg()]
        od = _SV["delt"][0]
        cnt = [0]

        def gen(cg):
            b_ = gb[cg % nbuf]
            HF, HB, R_HF, R_HB, kfb, R_kfb, nrm, R_nrm = (b_[k_] for k_ in ("HF", "HB", "R_HF", "R_HB", "kfb", "R_kfb", "nrm", "R_nrm"))
            dl = self.sv[:, od + cg:od + cg + 1]
            for ch in range(nch):
                cs = slice(ch * CH, (ch + 1) * CH)
                n = cnt[0]; cnt[0] += 1
                tt, rtt, dc, rdc = ttb[n % 2], R_tt[n % 2], dcb[n % 2], R_dc[n % 2]
                for r_ in range(2):
                    P.dma(tt[:, r_, :], self.c_TT[seg][r_:r_ + 1, cs].partition_broadcast(128).squeeze(1), wrp=[rtt])
                P.a(lambda e: e.activation(dc, tt, AF.Exp, scale=dl), rd=[rtt, self.R_sv], wr=[rdc])
                P.t(lambda e: e.matmul(pb[2][:, 0:CH], W4f2[:, cg * 128:(cg + 1) * 128], H3[:, cs], start=True, stop=True), rd=[R_W2, R_H3], xb=[Rpb[2]])
                P.t(lambda e: e.matmul(pb[3][:, 0:CH], W4b2[:, cg * 128:(cg + 1) * 128], H3[:, cs], start=True, stop=True), rd=[R_W2, R_H3], xb=[Rpb[3]])
                P.v(lambda e: e.tensor_tensor(HF[:, cs], pb[2][:, 0:CH], dc[:, 0, :], ALU.mult), rd=[rdc], wrp=[R_HF], xb=[Rpb[2]])
                P.v(lambda e: e.tensor_tensor(HB[:, cs], pb[3][:, 0:CH], dc[:, 1, :], ALU.mult), rd=[rdc], wrp=[R_HB], xb=[Rpb[3]])
            P.g(lambda e: e.memset(HB[:, 0:1], 0.0), rd=[R_HB], wrp=[R_HB])
            P.v(lambda e: e.tensor_reduce(nrm[:, 0:1], HF, mybir.AxisListType.X, ALU.add, apply_absolute_value=True), rd=[R_HF], wr=[R_nrm])
            P.v(lambda e: e.tensor_reduce(nrm[:, 1:2], HB, mybir.AxisListType.X, ALU.add, apply_absolute_value=True), rd=[R_HB], wr=[R_nrm])
            P.v(lambda e: e.tensor_tensor(nrm[:, 2:3], nrm[:, 0:1], nrm[:, 1:2], ALU.add), wr=[R_nrm])
            P.v(lambda e: e.reciprocal(nrm[:, 3:4], nrm[:, 2:3]), wr=[R_nrm])
            rn = nrm[:, 3:4]
            if seg == "c":
                P.a(lambda e: e.activation(self.kwin[:, cg, 0:LC - 1], HB[:, 1:LC], AF.Copy, scale=rn), rd=[R_HB, R_nrm], wrp=[self.R_kwin[cg]])
                P.a(lambda e: e.activation(self.kwin[:, cg, LC - 1:2 * LC - 1], HF, AF.Copy, scale=rn), rd=[R_HF, R_nrm], wrp=[self.R_kwin[cg]])
                return
            P.a(lambda e: e.activation(kfb[:, 0:Ln], HF, AF.Copy, scale=rn), rd=[R_HF, R_nrm], wr=[R_kfb])
            P.a(lambda e: e.activation(kfb[:, NFFT - Ln:NFFT], HB, AF.Copy, scale=rn), rd=[R_HB, R_nrm], wr=[R_kfb])
            self.dump("kfilt%d%s%d" % (l, seg, cg), kfb, rd=[R_kfb], dt=BF16)
            P.dma(self.KD[cg * 128:(cg + 1) * 128, :], kfb, rd=[R_kfb], wr=[self.R_KD4[cg]])
            self.load_xs(b_["Xs"], b_["R_xs"], self.KD[cg * 128:(cg + 1) * 128, :], 64, rd=[self.R_KD4[cg]])

        def fft(cg):
            b_ = gb[cg % nbuf]
            kfb, R_kfb = b_["kfb"], b_["R_kfb"]
            A_sb = kfb.rearrange("p (k c) -> p k c", c=128); R_a = R_kfb
            self.fft_stepA(b_["Xs"], b_["R_xs"], 64, A_sb, R_a)
            KFd = self.KF[(l, seg)]

            def consume(bi, k0, nb, XR, XI, RXR, RXI):
                ev, rev = evb[bi % 2], R_ev[bi % 2]
                P.a(lambda e: e.activation(ev[:, 0, 0:nb * 128], XR[:, 0:nb * 128], AF.Copy), wr=[rev], xb=[RXR])
                P.v(lambda e: e.tensor_copy(ev[:, 1, 0:nb * 128], XI[:, 0:nb * 128]), wr=[rev], xb=[RXI])
                for ri in range(2):
                    P.dma(KFd[k0:k0 + nb, ri, :, cg * 128:(cg + 1) * 128].rearrange("k p c -> p k c"),
                          ev[:, ri, 0:nb * 128].rearrange("p (k c) -> p k c", c=128), rd=[rev], wrp=[self.R_KF[(l, seg)][cg]])
            self.fft_stepC(A_sb, R_a, False, tmb, R_tm, consume, tmq="act")

        if nbuf == 1:
            for cg in range(4):
                gen(cg)
                if seg != "c":
                    fft(cg)
        else:
            gen(0)
            for cg in range(4):
                if cg + 1 < 4:
                    gen(cg + 1)
                if seg != "c":
                    fft(cg)
        self.barrier()
        A.off = mark

    def hyena(self, l, segs, groups=(0, 1, 2, 3)):
        P, A, pb, Rpb = self.P, self.ar, self.pb, self.R_pb
        mark = A.off
        vv = A.alloc([L]); R_vv = Reg()
        gate = A.alloc([L], BF16); R_gate = Reg()
        vvb = A.alloc([L], BF16); R_vvb = Reg()
        XZ = A.alloc([64, 128], BF16); R_xz = Reg()
        AZ = A.alloc([64 * 128], BF16); R_az = Reg()
        tmb = [A.alloc([4, 3, 128], BF16) for _ in range(2)]; R_tm = [Reg(), Reg()]
        mark2 = A.off
        ocw, ocb, ohb = _SV["convw"][0], _SV["convb"][0], _SV["hyb"][0]
        vvc = A.alloc([LC]); gatec = A.alloc([LC], BF16); vvbc = A.alloc([LC], BF16)
        VVC = (vvc, Reg(), gatec, Reg(), vvbc, Reg())
        VVX = (vv, R_vv, gate, R_gate, vvb, R_vvb)
        mark2 = A.off

        def seg_info(seg):
            Ln = L if seg == "x" else LC
            col0 = LC if seg == "x" else 0
            blocks = list(range(1, NB)) if seg == "x" else [0]
            return Ln, col0, blocks, Ln // 128

        def stage1_gen(seg, cg, VV):
            vv, R_vv, gate, R_gate, vvb, R_vvb = VV
            Ln, col0, blocks, n1cnt = seg_info(seg)
            stg = A.alloc([8, 128]); R_stg = Reg()
            wbf = [A.alloc([8, 128], BF16) for _ in range(4)]; R_w = [Reg() for _ in range(4)]
            Us = [[A.alloc([BLK + 2]) for _ in range(3)] for _ in range(2)]; R_U = [[Reg() for _ in range(3)] for _ in range(2)]
            cvs = [[A.alloc([BLK]) for _ in range(4)] for _ in range(2)]; R_cv = [[Reg() for _ in range(4)] for _ in range(2)]
            for si in range(4):
                self.load_w_slice(l, 2560 + si * 512 + cg * 128, wbf[si], R_w[si], stg, R_stg)
                yield
            for bi, blk in enumerate(blocks):
                sset = bi % 2
                c0 = blk * BLK
                first, last = bi == 0, bi == len(blocks) - 1
                lo = c0 if first else c0 - 1
                hi = c0 + BLK if last else c0 + BLK + 1
                n = hi - lo
                uo = 1 if first else 0
                lc = c0 - col0
                for si in range(4):
                    self.proj(pb[si][:, 0:n], Rpb[si], wbf[si], R_w[si], lo, n)
                    yield
                for si in range(3):
                    U, RU = Us[sset][si], R_U[sset][si]
                    if first:
                        P.g(lambda e: e.memset(U[:, 0:1], 0.0), wr=[RU])
                        yield
                    if last:
                        P.g(lambda e: e.memset(U[:, BLK + 1:BLK + 2], 0.0), wr=[RU])
                        yield
                    P.a(lambda e: e.activation(U[:, uo:uo + n], pb[si][:, 0:n], AF.Copy), wr=[RU], xb=[Rpb[si]])
                    yield
                sg, Rsg = cvs[sset][3], R_cv[sset][3]
                go = 0 if first else 1
                P.a(lambda e: e.activation(sg, pb[3][:, go:go + BLK], AF.Silu), wr=[Rsg], xb=[Rpb[3]])
                yield
                for si in range(3):
                    U, RU = Us[sset][si], R_U[sset][si]
                    cv, Rcv = cvs[sset][si], R_cv[sset][si]
                    wj = lambda j: self.sv[:, ocw + l * 36 + j * 12 + si * 4 + cg:ocw + l * 36 + j * 12 + si * 4 + cg + 1]
                    bb = self.sv[:, ocb + l * 12 + si * 4 + cg:ocb + l * 12 + si * 4 + cg + 1]
                    P.g(lambda e: e.tensor_scalar(cv, U[:, 1:BLK + 1], wj(1), bb, ALU.mult, ALU.add), rd=[RU, self.R_sv], wr=[Rcv])
                    yield
                    P.v(lambda e: e.scalar_tensor_tensor(cv, U[:, 0:BLK], wj(0), cv, ALU.mult, ALU.add), rd=[RU, self.R_sv], wr=[Rcv])
                    yield
                    P.v(lambda e: e.scalar_tensor_tensor(cv, U[:, 2:BLK + 2], wj(2), cv, ALU.mult, ALU.add), rd=[RU, self.R_sv], wr=[Rcv])
                    yield
                x0c, x1c, vc = cvs[sset][0], cvs[sset][1], cvs[sset][2]
                lcs = slice(lc, lc + BLK)
                P.v(lambda e: e.tensor_tensor(vv[:, lcs], vc, x1c, ALU.mult), rd=[R_cv[sset][1], R_cv[sset][2]], wrp=[R_vv])
                yield
                P.g(lambda e: e.tensor_copy(vvb[:, lcs], vv[:, lcs]), rd=[R_vv], wrp=[R_vvb])
                yield
                P.v(lambda e: e.tensor_tensor(gate[:, lcs], x0c, sg, ALU.mult), rd=[R_cv[sset][0], Rsg], wrp=[R_gate])
                yield
            VDd = self.VD if seg == "x" else self.VDC
            P.dma(VDd[cg * 128:(cg + 1) * 128, :], vvb[:, 0:Ln], rd=[R_vvb, R_vv], wr=[self.R_VD[cg]])
            yield
            self.dump("vv%d%s%d" % (l, seg, cg), vv[:, 0:Ln], rd=[R_vv])

        def convc_gen(cg, VV):
            vv, R_vv, gate, R_gate, vvb, R_vvb = VV
            seg = "c"
            Ln = LC
            ybuf = A.alloc([LC]); R_yb = Reg()
            accD = [A.alloc([LC]) for _ in range(8)]; R_aD = [Reg() for _ in range(8)]
            accP = [A.alloc([LC]) for _ in range(4)]; R_aP = [Reg() for _ in range(4)]
            tmpA = [A.alloc([LC]) for _ in range(8)]; R_tA = [Reg() for _ in range(8)]
            for a_, r_ in zip(accD + accP, R_aD + R_aP):
                P.g(lambda e: e.memset(a_, 0.0), wr=[r_])
                yield
            nd = na = 0
            for s_ in range(LC):
                win = self.kwin[:, cg, LC - 1 - s_:2 * LC - 1 - s_]
                vs = vv[:, s_:s_ + 1]
                if s_ % 8 < 7:
                    k_ = nd % 8; nd += 1
                    P.v(lambda e: e.scalar_tensor_tensor(accD[k_], win, vs, accD[k_], ALU.mult, ALU.add), rd=[self.R_kwin[cg], R_vv], wr=[R_aD[k_]])
                    yield
                else:
                    k_ = na % 8; j_ = na % 4; na += 1
                    P.a(lambda e: e.activation(tmpA[k_], win, AF.Copy, scale=vs), rd=[self.R_kwin[cg], R_vv], wr=[R_tA[k_]])
                    yield
                    P.g(lambda e: e.tensor_tensor(accP[j_], accP[j_], tmpA[k_], ALU.add), rd=[R_tA[k_]], wr=[R_aP[j_]])
                    yield
            for a_, b_ in ((0, 1), (2, 3), (4, 5), (6, 7), (0, 2), (4, 6), (0, 4)):
                P.v(lambda e: e.tensor_tensor(accD[a_], accD[a_], accD[b_], ALU.add), rd=[R_aD[b_]], wr=[R_aD[a_]])
                yield
            for a_, b_ in ((0, 1), (2, 3), (0, 2)):
                P.g(lambda e: e.tensor_tensor(accP[a_], accP[a_], accP[b_], ALU.add), rd=[R_aP[b_]], wr=[R_aP[a_]])
                yield
            P.v(lambda e: e.tensor_tensor(ybuf[:, 0:LC], accD[0], accP[0], ALU.add), rd=[R_aD[0], R_aP[0]], wr=[R_yb])
            yield
            self.dump("yconv%d%s%d" % (l, seg, cg), ybuf[:, 0:Ln], rd=[R_yb])
            hb = self.sv[:, ohb + l * 4 + cg:ohb + l * 4 + cg + 1]
            oBc = AZ[:, 0:Ln]
            P.v(lambda e: e.scalar_tensor_tensor(ybuf[:, 0:LC], vv[:, 0:LC], hb, ybuf[:, 0:LC], ALU.mult, ALU.add), rd=[R_vv, self.R_sv], wr=[R_yb])
            yield
            P.v(lambda e: e.tensor_tensor(oBc, ybuf[:, 0:LC], gate[:, 0:LC], ALU.mult), rd=[R_gate, R_yb], wrp=[R_az])
            yield
            P.dma(self.MIX[4 + cg, :, 0:BLK], oBc, rd=[R_az], wr=[self.R_MIX[4 + cg][0]])
            yield

        def stage2_x(cg):
            seg = "x"
            Ln, col0, blocks, n1cnt = seg_info(seg)
            VDd = self.VD
            A.off = mark2
            Y = A.alloc([66 * 128], BF16); R_y = Reg()
            Y_sb = Y.rearrange("p (k c) -> p k c", c=128)
            tq = [A.alloc([512]) for _ in range(4)]; R_tq = Reg()
            kfb = [A.alloc([2, 512]) for _ in range(2)]; R_kf = [Reg(), Reg()]
            ybuf = A.alloc([L]); R_yb = Reg()
            Xs = XZ
            self.load_xs(Xs, R_xz, VDd[cg * 128:(cg + 1) * 128, :], n1cnt, zero_first=(n1cnt < 32), rd=[self.R_VD[cg]])
            A_sb = AZ.rearrange("p (k c) -> p k c", c=128)
            self.fft_stepA(Xs, R_xz, 32, A_sb, R_az)
            KFd = self.KF[(l, seg)]

            def consume_f(bi, k0, nb, XR, XI, RXR, RXI):
                kf, rkf = kfb[bi % 2], R_kf[bi % 2]
                w = nb * 128
                for ri in range(2):
                    P.dma(kf[:, ri, 0:w].rearrange("p (k c) -> p k c", c=128),
                          KFd[k0:k0 + nb, ri, :, cg * 128:(cg + 1) * 128].rearrange("k p c -> p k c"),
                          rd=[self.R_KF[(l, seg)][cg]], wrp=[rkf])
                Kr, Ki = kf[:, 0, 0:w], kf[:, 1, 0:w]
                P.v(lambda e: e.tensor_tensor(tq[0][:, 0:w], XR[:, 0:w], Kr, ALU.mult), rd=[rkf], wr=[R_tq], xb=[RXR])
                P.v(lambda e: e.tensor_tensor(tq[1][:, 0:w], XI[:, 0:w], Ki, ALU.mult), rd=[rkf], wrp=[R_tq], xb=[RXI])
                P.v(lambda e: e.tensor_tensor(tq[2][:, 0:w], XR[:, 0:w], Ki, ALU.mult), rd=[rkf], wrp=[R_tq], xb=[RXR])
                P.v(lambda e: e.tensor_tensor(tq[3][:, 0:w], XI[:, 0:w], Kr, ALU.mult), rd=[rkf], wrp=[R_tq], xb=[RXI])
                yre = Y_sb[:, k0:k0 + nb, :].rearrange("p k c -> p (k c)")
                yim = Y_sb[:, 33 + k0:33 + k0 + nb, :].rearrange("p k c -> p (k c)")
                P.g(lambda e: e.tensor_tensor(yre, tq[0][:, 0:w], tq[1][:, 0:w], ALU.subtract), rd=[R_tq], wrp=[R_y])
                P.g(lambda e: e.tensor_tensor(yim, tq[2][:, 0:w], tq[3][:, 0:w], ALU.add), rd=[R_tq], wrp=[R_y])
            self.fft_stepC(A_sb, R_az, False, tmb, R_tm, consume_f)
            Zd = AZ.rearrange("p (m j k) -> p k j m", j=2, k=64)

            def consume_i(bi, k0, nb, ZR, ZI, RZR, RZI):
                dst = Zd[:, k0:k0 + nb, :, :]
                src = ZR[:, 0:nb * 128].rearrange("p (k j m) -> p k j m", j=2, m=64)
                P.a(lambda e: e.activation(dst, src, AF.Copy), wrp=[R_az], xb=[RZR])
                ks = [k for k in range(k0, k0 + nb) if k not in (0, 32)]
                if ks:
                    j0 = ks[0] - k0
                    dsti = Zd[:, 32 + ks[0]:32 + ks[-1] + 1, :, :]
                    srci = ZI[:, j0 * 128:(j0 + len(ks)) * 128].rearrange("p (k j m) -> p k j m", j=2, m=64)
                    P.v(lambda e: e.tensor_copy(dsti, srci), wrp=[R_az], xb=[RZI])
            self.fft_stepC(Y_sb, R_y, True, tmb, R_tm, consume_i)
            ZT = XZ
            Zv = AZ.rearrange("p (m x) -> p m x", x=128)
            for m8 in range(8):
                bk = m8 % 2
                pv = pb[bk][:, 0:512].bitcast(BF16).rearrange("p (a b) -> p a b", a=8)
                for j in range(8):
                    m = m8 * 8 + j
                    P.t(lambda e: e.transpose(pv[:, j, :], Zv[:, m, :], self.identB), rd=[R_az, self.R_const], xb=[Rpb[bk]])
                if bk == 0:
                    P.a(lambda e: e.activation(ZT[:, m8 * 8:(m8 + 1) * 8, :], pv, AF.Copy), wrp=[R_xz], xb=[Rpb[bk]])
                else:
                    P.v(lambda e: e.tensor_copy(ZT[:, m8 * 8:(m8 + 1) * 8, :], pv), wrp=[R_xz], xb=[Rpb[bk]])
            NN = n1cnt
            yv = ybuf[:, 0:Ln].rearrange("p (n1 n2) -> p n2 n1", n2=128)
            for g16 in range(8):
                bk = 2 + g16 % 2
                for j in range(16):
                    n2 = g16 * 16 + j
                    for jj in range(2):
                        P.t(lambda e: e.matmul(pb[bk][64 * jj:64 * jj + 64, j * NN:(j + 1) * NN], ZT[64 * jj:64 * jj + 64, :, n2],
                                               self.RA2[64 * jj:64 * jj + 64, 0:NN], start=True, stop=True),
                            rd=[R_xz, self.R_const], xb=[Rpb[bk]])
                src = pb[bk][:, 0:16 * NN].rearrange("p (a b) -> p a b", b=NN)
                dst = yv[:, g16 * 16:(g16 + 1) * 16, :]
                if g16 % 2 == 0:
                    P.a(lambda e: e.activation(dst, src, AF.Copy), wrp=[R_yb], xb=[Rpb[bk]])
                else:
                    P.v(lambda e: e.tensor_copy(dst, src), wrp=[R_yb], xb=[Rpb[bk]])
            self.dump("yconv%d%s%d" % (l, seg, cg), ybuf[:, 0:Ln], rd=[R_yb])
            hb = self.sv[:, ohb + l * 4 + cg:ohb + l * 4 + cg + 1]
            oB = AZ[:, 0:Ln]
            PW = min(1024, Ln)
            for pc in range(Ln // PW):
                cs = slice(pc * PW, (pc + 1) * PW)
                P.v(lambda e: e.scalar_tensor_tensor(ybuf[:, cs], vv[:, cs], hb, ybuf[:, cs], ALU.mult, ALU.add), rd=[R_vv, self.R_sv], wr=[R_yb])
                P.v(lambda e: e.tensor_tensor(oB[:, cs], ybuf[:, cs], gate[:, cs], ALU.mult), rd=[R_gate, R_yb], wrp=[R_az])
            for blk in blocks:
                lc = blk * BLK - col0
                P.dma(self.MIX[4 + cg, :, blk * BLK:(blk + 1) * BLK], oB[:, lc:lc + BLK], rd=[R_az], wr=[self.R_MIX[4 + cg][blk]])
            self.barrier()

        def drain_w(gw):
            gw = [[g_, w_] for g_, w_ in gw]
            while gw:
                for it in list(gw):
                    for _ in range(it[1]):
                        try:
                            next(it[0])
                        except StopIteration:
                            gw.remove(it)
                            break

        for cg in groups:
            if "c" in segs:
                A.off = mark2
                drain_w([(stage1_gen("c", cg, VVC), 1)])
                self.barrier()
                if "x" not in segs:
                    A.off = mark2
                    drain_w([(convc_gen(cg, VVC), 1)])
                    self.barrier()
            if "x" in segs:
                A.off = mark2
                gw = [(stage1_gen("x", cg, VVX), 1)]
                if "c" in segs:
                    gw.insert(0, (convc_gen(cg, VVC), 1))
                drain_w(gw)
                self.barrier()
                stage2_x(cg)
        A.off = mark

    def bcast_rows(self, dst, cols_fn, R_dst, tmpd, R_tmpd):
        P, pb, Rpb = self.P, self.pb, self.R_pb
        for k in range(8):
            bk = 0 if k < 4 else 3
            P.v(lambda e: e.tensor_scalar(tmpd, self.identF, cols_fn(k), None, ALU.mult), rd=[self.R_const, self.R_mod, self.R_sv], wr=[R_tmpd])
            P.t(lambda e: e.matmul(pb[bk][:, (k % 4) * 128:(k % 4 + 1) * 128], self.onesF, tmpd, start=True, stop=True),
                rd=[R_tmpd, self.R_const], xb=[Rpb[bk]])
        P.a(lambda e: e.activation(dst[:, 0:512], pb[0][:, :], AF.Copy), wrp=[R_dst], xb=[Rpb[0]])
        P.a(lambda e: e.activation(dst[:, 512:1024], pb[3][:, :], AF.Copy), wrp=[R_dst], xb=[Rpb[3]])

    def outproj(self, l, last):
        P, A, pb, Rpb = self.P, self.ar, self.pb, self.R_pb
        mark = A.off
        wout = A.alloc([8, D], BF16); R_wo = Reg()
        wstg = [A.alloc([8, 256]) for _ in range(2)]; R_ws = [Reg(), Reg()]
        wv = self.w_out[l].rearrange("(k p) n -> p k n", p=128)
        for pc in range(4):
            P.dma(wstg[pc % 2], wv[:, :, pc * 256:(pc + 1) * 256], wr=[R_ws[pc % 2]])
            P.g(lambda e: e.tensor_copy(wout[:, :, pc * 256:(pc + 1) * 256], wstg[pc % 2]), rd=[R_ws[pc % 2]], wrp=[R_wo])
        tmpd = A.alloc([128]); R_tmpd = Reg()
        gtbc = [A.alloc([D]) for _ in range(2)]; R_gt = [Reg(), Reg()]
        for seg in ((0,) if last else (0, 1)):
            self.bcast_rows(gtbc[seg], lambda k: self.mod[:, l, 16 + k, seg:seg + 1], R_gt[seg], tmpd, R_tmpd)
        if last:
            fnwbc = A.alloc([D]); R_fn = Reg()
            of = _SV["fnw"][0]
            self.bcast_rows(fnwbc, lambda k: self.sv[:, of + k:of + k + 1], R_fn, tmpd, R_tmpd)
        xts = [A.alloc([D]) for _ in range(2)]; R_x = [Reg(), Reg()]
        xns = [A.alloc([D]) for _ in range(2)]; R_xn = [Reg(), Reg()]
        tmps = [A.alloc([D]) for _ in range(2)]; R_tp = [Reg(), Reg()]
        mts = [A.alloc([8, 128], BF16) for _ in range(2)]; R_mt = [Reg(), Reg()]
        bufs = [(A.alloc([D], BF16), A.alloc([1]), A.alloc([1]), A.alloc([D]), Reg()) for _ in range(2)]
        tiles = list(range(2, NT)) if last else list(range(NT))
        def stage_mm(i, tt):
            s2 = i % 2
            xt, mt = xts[s2], mts[s2]
            blk = tt // 2
            P.dma(mt, self.MIX[:, :, tt * 128:(tt + 1) * 128].rearrange("f p t -> p f t"),
                  rd=[self.R_MIX[f][blk] for f in range(8)], wr=[R_mt[s2]])
            if l == 0:
                src = self.ctx_in[tt * 128:(tt + 1) * 128, :] if tt < 2 else self.x_in[(tt - 2) * 128:(tt - 1) * 128, :]
                P.dma(xt, src, wr=[R_x[s2]])
            else:
                P.dma(xt, self.XRES[tt * 128:(tt + 1) * 128, :], rd=[self.R_XRES[tt]], wr=[R_x[s2]])
            for half in range(2):
                bk = 4 + half + 2 * s2
                hs = slice(half * 512, (half + 1) * 512)
                for f in range(8):
                    P.t(lambda e: e.matmul(pb[bk][:, :], mt[:, f, :], wout[:, f, hs], start=(f == 0), stop=(f == 7)),
                        rd=[R_mt[s2], R_wo], xb=[Rpb[bk]])

        def stage_fin(i, tt):
            s2 = i % 2
            seg = 1 if tt < 2 else 0
            xt, xn, tp = xts[s2], xns[s2], tmps[s2]
            for half in range(2):
                bk = 4 + half + 2 * s2
                hs = slice(half * 512, (half + 1) * 512)
                P.v(lambda e: e.tensor_tensor(tp[:, hs], pb[bk][:, :], gtbc[seg][:, hs], ALU.mult), rd=[R_gt[seg]], wrp=[R_tp[s2]], xb=[Rpb[bk]])
                P.g(lambda e: e.tensor_tensor(xn[:, hs], tp[:, hs], xt[:, hs], ALU.add), rd=[R_tp[s2], R_x[s2]], wrp=[R_xn[s2]])
            if not last:
                P.dma(self.XRES[tt * 128:(tt + 1) * 128, :], xn, rd=[R_xn[s2]], wr=[self.R_XRES[tt]])
                self.tile_to_hx(l + 1, tt, xn, R_xn[s2], bufs[s2])
            else:
                junk, ssq, rstd, on, R_t = bufs[s2]
                P.a(lambda e: e.activation(junk, xn, AF.Square, accum_out=ssq), rd=[R_xn[s2]], wr=[R_t])
                P.a(lambda e: e.activation(ssq, ssq, AF.Sqrt, scale=1.0 / D, bias=EPS), wr=[R_t])
                P.v(lambda e: e.reciprocal(rstd, ssq), wr=[R_t])
                P.v(lambda e: e.scalar_tensor_tensor(on, xn, rstd, fnwbc, ALU.mult, ALU.mult), rd=[R_xn[s2], R_fn], wr=[R_t])
                P.dma(self.out[(tt - 2) * 128:(tt - 1) * 128, :], on, rd=[R_t], q="act")

        stage_mm(0, tiles[0])
        for i, tt in enumerate(tiles):
            if i + 1 < len(tiles):
                stage_mm(i + 1, tiles[i + 1])
            stage_fin(i, tt)
        self.barrier()
        A.off = mark

    def build_all(self):
        self.adaln()
        for l in range(self.nlayers):
            last = l == self.nlayers - 1
            segs = ("x",) if (last and self.nlayers > 1) else ("c", "x")
            for s in segs:
                self.hy_filter(l, s, alias_hx=True)
        self.phase_b_from_dram(0)
        for l in range(self.nlayers):
            last = l == self.nlayers - 1
            ctx_out = not (last and self.nlayers > 1)
            self.hgrn2(l, ctx_out)
            self.hyena(l, ["c", "x"] if ctx_out else ["x"])
            self.outproj(l, last)
        self.P.finish()


_CONSTS = None


def kernel(**inputs):
    global _CONSTS
    inp = {k: np.asarray(v) for k, v in inputs.items()}
    if _CONSTS is None:
        _CONSTS = _host_consts()
    B = Builder(nlayers=2)
    B.build_all()
    in_maps = []
    for b in range(8):
        m = {"x": np.ascontiguousarray(inp["x"][b], dtype=np.float32), "ctx": np.ascontiguousarray(inp["ctx"][b], dtype=np.float32),
             "smallv": _pack_small(inp, b)}
        for k in ("w_ada", "w_in", "w_out", "hy_w1", "hy_w2", "hy_w3", "hy_w4"):
            m[k] = np.ascontiguousarray(inp[k], dtype=np.float32)
        m.update(_CONSTS)
        in_maps.append(m)
    res = run_bass_kernel_spmd(B.nc, in_maps, core_ids=list(range(8)))
    return np.stack([np.asarray(r["out"], dtype=np.float32) for r in res.results], axis=0)
```
